# Optimizing a Trainium2 kernel written in Bass

```python
import math
import jax, jax.numpy as jnp
from jax import lax
import numpy as np

D_MODEL = 1024
BATCH = 8
SEQ = 4096
DEPTH = 1

D_MIX = D_MODEL
MLA_HEADS = 8
MLA_NOPE = 64
MLA_ROPE = 32
MLA_V = 64
MLA_Q_RANK = 256
MLA_KV_RANK = 128
MLA_WIDTH = MLA_HEADS * MLA_V
HG_HEADS = 4
HG_EXPAND = 128
HG_WIDTH = D_MIX - MLA_WIDTH
HG_HEAD_V = HG_WIDTH // HG_HEADS
HG_FDIM = HG_HEADS * HG_EXPAND
HG_CHUNK = 64
Q_BLOCK = 128
ROPE_THETA = 10000.0
EPS = 1e-6
IN_SPLITS = (MLA_Q_RANK, MLA_KV_RANK, MLA_ROPE, MLA_WIDTH,
             HG_FDIM, HG_FDIM, HG_WIDTH, HG_WIDTH)
D_IN = MLA_Q_RANK + MLA_KV_RANK + MLA_ROPE + MLA_WIDTH + 2 * HG_FDIM + 2 * HG_WIDTH

kernel_name = "hybrid_mla_hgrn2_parallel_heads"


def rms_norm(x, g):
    xf = x.astype(jnp.float32)
    y = xf * lax.rsqrt(jnp.mean(xf * xf, axis=-1, keepdims=True) + EPS)
    return (y * g.astype(jnp.float32)).astype(x.dtype)


def rope(x, positions):
    half = x.shape[-1] // 2
    inv = ROPE_THETA ** (-jnp.arange(half, dtype=jnp.float32) / half)
    ang = positions.astype(jnp.float32)[..., None] * inv
    ang = ang.reshape(ang.shape[:2] + (1,) * (x.ndim - 3) + (half,))
    cos, sin = jnp.cos(ang), jnp.sin(ang)
    xf = x.astype(jnp.float32)
    x1, x2 = xf[..., :half], xf[..., half:]
    out = jnp.concatenate([x1 * cos - x2 * sin, x2 * cos + x1 * sin], axis=-1)
    return out.astype(x.dtype)


def mla_group(q_lat, kv_lat, k_rope, positions, q_a_norm_g, w_q_b, kv_a_norm_g, w_kv_b):
    B, S, _ = q_lat.shape
    q = rms_norm(q_lat, q_a_norm_g) @ w_q_b
    q = q.reshape(B, S, MLA_HEADS, MLA_NOPE + MLA_ROPE)
    q_nope, q_pe = q[..., :MLA_NOPE], rope(q[..., MLA_NOPE:], positions)
    kv = (rms_norm(kv_lat, kv_a_norm_g) @ w_kv_b).reshape(B, S, MLA_HEADS, MLA_NOPE + MLA_V)
    k_nope, v = kv[..., :MLA_NOPE], kv[..., MLA_NOPE:]
    k_pe = rope(k_rope, positions)
    scale = 1.0 / math.sqrt(MLA_NOPE + MLA_ROPE)

    nb = S // Q_BLOCK
    qn_b = q_nope.reshape(B, nb, Q_BLOCK, MLA_HEADS, MLA_NOPE).transpose(1, 0, 2, 3, 4)
    qp_b = q_pe.reshape(B, nb, Q_BLOCK, MLA_HEADS, MLA_ROPE).transpose(1, 0, 2, 3, 4)
    kpos = jnp.arange(S)

    def block(args):
        qn, qp, bi = args
        s = (jnp.einsum('bqhd,bkhd->bhqk', qn, k_nope)
             + jnp.einsum('bqhr,bkr->bhqk', qp, k_pe)).astype(jnp.float32) * scale
        qpos = bi * Q_BLOCK + jnp.arange(Q_BLOCK)
        mask = kpos[None, :] <= qpos[:, None]
        s = jnp.where(mask, s, -jnp.inf)
        p = jax.nn.softmax(s, axis=-1).astype(v.dtype)
        return jnp.einsum('bhqk,bkhd->bqhd', p, v)

    out = lax.map(block, (qn_b, qp_b, jnp.arange(nb)))
    return out.transpose(1, 0, 2, 3, 4).reshape(B, S, MLA_WIDTH)


def hgrn2_group(q_in, f_in, i_in, lb, norm_g):
    B, S, _ = q_in.shape
    nc = S // HG_CHUNK
    lbf = lb.astype(jnp.float32)
    f = lbf + (1.0 - lbf) * jax.nn.sigmoid(f_in.astype(jnp.float32))
    log_f = jnp.log(f)
    k = 1.0 - f
    q = jax.nn.silu(q_in.astype(jnp.float32))
    v = i_in.astype(jnp.float32)

    def chunks(t, d):
        return t.reshape(B, nc, HG_CHUNK, HG_HEADS, d).transpose(1, 0, 3, 2, 4)

    qc, kc, gc = chunks(q, HG_EXPAND), chunks(k, HG_EXPAND), chunks(log_f, HG_EXPAND)
    vc = chunks(v, HG_HEAD_V)
    causal = jnp.tril(jnp.ones((HG_CHUNK, HG_CHUNK), dtype=bool))

    def step(state, inp):
        qb, kb, vb, gb = inp
        G = jnp.cumsum(gb, axis=-2)
        diff = G[..., :, None, :] - G[..., None, :, :]
        decay = jnp.exp(jnp.where(causal[:, :, None], diff, -jnp.inf))
        A = jnp.einsum('bhtk,bhtsk,bhsk->bhts', qb, decay, kb)
        o = (jnp.einsum('bhts,bhsv->bhtv', A, vb)
             + jnp.einsum('bhtk,bhkv->bhtv', qb * jnp.exp(G), state))
        G_end = G[..., -1:, :]
        new_state = (state * jnp.exp(G_end)[..., 0, :, None]
                     + jnp.einsum('bhsk,bhsv->bhkv', kb * jnp.exp(G_end - G), vb))
        return new_state, o

    s0 = jnp.zeros((B, HG_HEADS, HG_EXPAND, HG_HEAD_V), jnp.float32)
    _, o = lax.scan(step, s0, (qc, kc, vc, gc))
    o = o.transpose(1, 0, 3, 2, 4).reshape(B, S, HG_HEADS, HG_HEAD_V)
    o = o * lax.rsqrt(jnp.mean(o * o, axis=-1, keepdims=True) + EPS)
    o = o * norm_g.astype(jnp.float32).reshape(HG_HEADS, HG_HEAD_V)
    return o.reshape(B, S, HG_WIDTH).astype(q_in.dtype)


def setup_inputs(seed: int = 0) -> dict:
    key = jax.random.key(seed)
    ks = jax.random.split(key, 12)
    nrm = jax.random.normal
    x = nrm(ks[0], (BATCH, SEQ, D_MODEL), jnp.float32)
    start = jax.random.randint(ks[1], (BATCH, 1), 0, 1024, dtype=jnp.int32)
    positions = (start + jnp.arange(SEQ, dtype=jnp.int32)[None, :]).astype(jnp.int32)
    ln_g = 1.0 + 0.02 * nrm(ks[2], (DEPTH, D_MODEL), jnp.float32)
    w_in = nrm(ks[3], (DEPTH, D_MODEL, D_IN), jnp.float32) * D_MODEL ** -0.5
    q_a_norm_g = 1.0 + 0.02 * nrm(ks[4], (DEPTH, MLA_Q_RANK), jnp.float32)
    w_q_b = nrm(ks[5], (DEPTH, MLA_Q_RANK, MLA_HEADS * (MLA_NOPE + MLA_ROPE)), jnp.float32) * MLA_Q_RANK ** -0.5
    kv_a_norm_g = 1.0 + 0.02 * nrm(ks[6], (DEPTH, MLA_KV_RANK), jnp.float32)
    w_kv_b = nrm(ks[7], (DEPTH, MLA_KV_RANK, MLA_HEADS * (MLA_NOPE + MLA_V)), jnp.float32) * MLA_KV_RANK ** -0.5
    hg_lower_bounds = nrm(ks[8], (DEPTH + 1, HG_FDIM), jnp.float32)
    hg_norm_g = 1.0 + 0.02 * nrm(ks[9], (DEPTH, HG_WIDTH), jnp.float32)
    w_out = nrm(ks[10], (DEPTH, D_MIX, D_MODEL), jnp.float32) * D_MIX ** -0.5
    final_norm_g = 1.0 + 0.02 * nrm(ks[11], (D_MODEL,), jnp.float32)
    return {"x": x, "positions": positions, "ln_g": ln_g, "w_in": w_in,
            "q_a_norm_g": q_a_norm_g, "w_q_b": w_q_b, "kv_a_norm_g": kv_a_norm_g,
            "w_kv_b": w_kv_b, "hg_lower_bounds": hg_lower_bounds, "hg_norm_g": hg_norm_g,
            "w_out": w_out, "final_norm_g": final_norm_g}


def reference(x, positions, ln_g, w_in, q_a_norm_g, w_q_b, kv_a_norm_g, w_kv_b,
              hg_lower_bounds, hg_norm_g, w_out, final_norm_g):
    lb_all = jnp.cumsum(jax.nn.softmax(hg_lower_bounds.astype(jnp.float32), axis=0), axis=0)
    split_pts = [int(p) for p in np.cumsum(IN_SPLITS)[:-1]]
    for l in range(DEPTH):
        h = rms_norm(x, ln_g[l])
        proj = h @ w_in[l]
        q_lat, kv_lat, k_rope, g_mla, hq, hf, hi, g_hg = jnp.split(proj, split_pts, axis=-1)
        y_mla = mla_group(q_lat, kv_lat, k_rope, positions,
                          q_a_norm_g[l], w_q_b[l], kv_a_norm_g[l], w_kv_b[l]) * jax.nn.silu(g_mla)
        y_hg = hgrn2_group(hq, hf, hi, lb_all[l], hg_norm_g[l]) * jax.nn.silu(g_hg)
        y = jnp.concatenate([y_mla, y_hg], axis=-1) @ w_out[l]
        x = x + y.astype(x.dtype)
    return rms_norm(x, final_norm_g)
```

```python
import math
import os
from contextlib import ExitStack

import numpy as np
import ml_dtypes
import concourse.bass as bass
import concourse.mybir as mybir
from concourse.bass_utils import run_bass_kernel_spmd

F32 = mybir.dt.float32
BF16 = mybir.dt.bfloat16
I32 = mybir.dt.int32
AF = mybir.ActivationFunctionType
ALU = mybir.AluOpType

SAME_ENGINE_SYNC = os.environ.get("MK_SES", "1") == "1"
NOSYNC_ENGS = set(os.environ.get("MK_NOSYNC", "").split(","))
EPS = 1e-6
D = 1024
DIN = 2976
QL, KVL, KR, GM, HQ, HF, HI, GH = 0, 256, 384, 416, 928, 1440, 1952, 2464


class Buf:
    __slots__ = ("name", "w", "r", "excl")

    def __init__(self, name="", excl=False):
        self.name = name
        self.w = None
        self.r = []
        self.excl = excl


class Op:
    __slots__ = ("eng", "fn", "deps", "dma", "key", "tok", "has_dep", "id")


class Prog:
    def __init__(self):
        self.ops = []

    def emit(self, eng, fn, reads=(), writes=(), dma=False, key=None):
        op = Op()
        op.eng, op.fn, op.dma, op.id, op.tok, op.has_dep = eng, fn, dma, len(self.ops), None, False
        deps = set()
        writes = list(writes) + [r for r in reads if r.excl]
        reads = [r for r in reads if not r.excl]
        for r in reads:
            if r.w is not None:
                deps.add(r.w)
        for w in writes:
            if w.w is not None:
                deps.add(w.w)
            deps.update(w.r)
        keep = set()
        for d in deps:
            dop = self.ops[d]
            if not dop.dma and dop.eng == eng and (eng == "pe" or not SAME_ENGINE_SYNC or eng in NOSYNC_ENGS):
                continue
            keep.add(d)
        op.deps = keep
        for d in keep:
            self.ops[d].has_dep = True
        for r in reads:
            r.r.append(op.id)
        for w in writes:
            w.w = op.id
            w.r = []
        if dma:
            op.key = key if key is not None else id((list(writes) + list(reads))[0])
        self.ops.append(op)
        return op

    def build(self, nc, st, tag):
        engs = ["pe", "act", "dve", "pool", "sp"]
        ops = self.ops
        last = {}
        for op in ops:
            if not op.dma:
                last[op.eng] = op
        finals = [o for o in ops if o.dma] + [o for e, o in last.items() if e != "sp"]
        for o in finals:
            o.has_dep = True
        esem = {e: st.enter_context(nc.semaphore("s%s_%s" % (tag, e))) for e in engs}
        dsem, dcnt, ecnt = {}, {}, {e: 0 for e in engs}
        for op in ops:
            if op.dma:
                if op.key not in dsem:
                    dsem[op.key] = st.enter_context(nc.semaphore("d%s_%d" % (tag, len(dsem))))
                    dcnt[op.key] = 0
                dcnt[op.key] += 16
                op.tok = (dsem[op.key], dcnt[op.key], 16)
            elif op.has_dep:
                ecnt[op.eng] += 1
                op.tok = (esem[op.eng], ecnt[op.eng], 1)
        self.n_sems = len(dsem) + 5
        block = st.enter_context(nc.Block())

        def run_engine(ename):
            def body(eng):
                waited = {}

                def wait(tok):
                    sem, val, _ = tok
                    if waited.get(sem.num, 0) < val:
                        eng.wait_ge(sem, val)
                        waited[sem.num] = val

                for op in ops:
                    if op.eng != ename:
                        continue
                    for d in sorted(op.deps):
                        wait(ops[d].tok)
                    inst = op.fn(eng)
                    if op.tok is not None:
                        inst.then_inc(op.tok[0], op.tok[2])
                if ename == "sp":
                    best = {}
                    for fo in finals:
                        if fo.tok[0].num not in best or best[fo.tok[0].num][1] < fo.tok[1]:
                            best[fo.tok[0].num] = fo.tok
                    for tk in best.values():
                        wait(tk)
            return body

        block.tensor(run_engine("pe"))
        block.scalar(run_engine("act"))
        block.vector(run_engine("dve"))
        block.gpsimd(run_engine("pool"))
        block.sync(run_engine("sp"))


class Em:
    def __init__(self, P):
        self.P = P

    def mm(self, out, lhsT, rhs, start, stop, reads, writes):
        self.P.emit("pe", lambda e: e.matmul(out, lhsT, rhs, start=start, stop=stop), reads, writes)

    def tr(self, out, in_, ident, reads, writes):
        self.P.emit("pe", lambda e: e.transpose(out, in_, ident), reads, writes)

    def act(self, out, in_, func, reads, writes, bias=0.0, scale=1.0, accum=None):
        if accum is None:
            self.P.emit("act", lambda e: e.activation(out, in_, func, bias=bias, scale=scale), reads, writes)
        else:
            self.P.emit("act", lambda e: e.activation(out, in_, func, bias=bias, scale=scale, accum_out=accum), reads, writes)

    def tt(self, eng, out, a, b, op, reads, writes):
        self.P.emit(eng, lambda e: e.tensor_tensor(out, a, b, op=op), reads, writes)

    def ts(self, eng, out, a, s1, s2, op0, op1, reads, writes):
        if s2 is None:
            self.P.emit(eng, lambda e: e.tensor_scalar(out, a, s1, None, op0=op0), reads, writes)
        else:
            self.P.emit(eng, lambda e: e.tensor_scalar(out, a, s1, s2, op0=op0, op1=op1), reads, writes)

    def stt(self, eng, out, in0, scalar, in1, op0, op1, reads, writes):
        self.P.emit(eng, lambda e: e.scalar_tensor_tensor(out, in0, scalar, in1, op0=op0, op1=op1), reads, writes)

    def cp(self, eng, out, in_, reads, writes):
        if eng == "act":
            self.P.emit("act", lambda e: e.activation(out, in_, AF.Copy), reads, writes)
        else:
            self.P.emit(eng, lambda e: e.tensor_copy(out, in_), reads, writes)

    def recip(self, out, in_, reads, writes):
        self.P.emit("dve", lambda e: e.reciprocal(out, in_), reads, writes)

    def memset(self, eng, ap, val, writes):
        self.P.emit(eng, lambda e: e.memset(ap, val), (), writes)

    def dma(self, out, in_, reads, writes, key=None, store=False):
        if key is None:
            key = id(reads[0]) if store else id(writes[0])
        return self.P.emit("sp", lambda e: e.dma_start(out=out, in_=in_), reads, writes, dma=True, key=key)


class Rot:
    def __init__(self, items):
        self.items = items
        self.i = 0

    def next(self):
        it = self.items[self.i % len(self.items)]
        self.i += 1
        return it


def build(S, BT=256, dbg=False):
    NT = S // 128
    NB1 = S // BT
    TPB = BT // 128
    CPB = BT // 64
    NB2 = S // 512
    nc = bass.Bass("TRN2", target_bir_lowering=False)

    def din(name, shape, dt):
        return nc.dram_tensor(name, shape, dt, kind="ExternalInput").ap()

    def dscr(name, shape, dt):
        return nc.dram_tensor(name, shape, dt, kind="Internal").ap()

    x_d = din("x", [S, D], F32)
    pos_d = din("pos", [1, S], I32)
    win_d = din("w_in_l", [128, 8, DIN], F32)
    lng_d = din("ln_g_l", [128, 8], F32)
    wq_d = din("w_q_l", [128, 2, 768], F32)
    qg_d = din("q_g_l", [128, 2], F32)
    wkv_d = din("w_kv_l", [128, 1024], F32)
    kvg_d = din("kv_g_l", [128, 1], F32)
    wout_d = din("w_out_l", [128, 8, 1024], F32)
    hgn_d = din("hgn_l", [128, 4], F32)
    hlb_d = din("hlb", [2, 512], F32)
    hlbfm_d = din("hlb_fm", [128, 2, 4], F32)
    fng_d = din("fng", [1, 1024], F32)
    ident_d = din("c_ident", [128, 128], BF16)
    tri_d = din("c_tri", [128, 128], BF16)
    lm_d = din("c_lm", [128, 128], BF16)
    lcols_d = din("c_lcols", [128, 4], BF16)
    us_d = din("c_us", [128, 128], BF16)
    lrep_d = din("c_lrep", [128, 512], F32)
    invf_d = din("c_invf", [128, 1], F32)
    out_d = nc.dram_tensor("out", [S, D], F32, kind="ExternalOutput").ap()

    Qd = dscr("Qd", [8, 96, S], BF16)
    Kd = dscr("Kd", [8, 96, S], BF16)
    Vd = dscr("Vd", [NT, 128, 640], BF16)
    Gd = dscr("Gd", [4, 128, S], BF16)
    Yd = dscr("Yd", [4, 128, S], BF16)
    Rd = dscr("Rd", [4, 512], F32)
    Cd = dscr("Cd", [32, S], F32)
    Sd = dscr("Sd", [32, S], F32)
    bQ = [[Buf() for _ in range(NB1)] for _ in range(8)]
    bK = [[Buf() for _ in range(NB1)] for _ in range(8)]
    bV = [Buf() for _ in range(NT)]
    bG = [[Buf() for _ in range(NB1)] for _ in range(4)]
    bY = [[Buf() for _ in range(NB1)] for _ in range(4)]
    bC = [Buf() for _ in range(NB1)]
    bS = [Buf() for _ in range(NB1)]

    with ExitStack() as st:
        P = Prog()
        E = Em(P)

        def sb(name, shape, dt):
            return st.enter_context(nc.sbuf_tensor("a_" + name, shape, dt)), Buf(name)

        def ps(name, shape, dt):
            return st.enter_context(nc.psum_tensor("a_" + name, shape, dt)), Buf(name, True)

        winb, b_winb = sb("winb", [128, 8, DIN], BF16)
        b_wp = [Buf("winb_p%d" % j) for j in range((DIN + 511) // 512)]

        def wb(col0, M):
            return [b_wp[j] for j in range(col0 // 512, (col0 + M - 1) // 512 + 1)]
        wkrr, b_wkrr = sb("wkrr", [128, 8, 96], BF16)
        wqb, b_wqb = sb("wqb", [128, 2, 768], BF16)
        wqr, b_wqr = sb("wqr", [128, 2, 768], BF16)
        wkvb, b_wkvb = sb("wkvb", [128, 1024], BF16)
        ident, b_ident = sb("ident", [128, 128], BF16)
        onesf, b_onesf = sb("onesf", [128, 128], BF16)
        lm, b_lm = sb("lm", [128, 128], BF16)
        lcols, b_lcols = sb("lcols", [128, 4], BF16)
        us, b_us = sb("us", [128, 128], BF16)
        lrep, b_lrep = sb("lrep", [128, 512], F32)
        invf, b_invf = sb("invf", [128, 1], F32)
        lbrep, b_lbrep = sb("lbrep", [128, 512], F32)
        omlrep, b_omlrep = sb("omlrep", [128, 512], F32)
        omlfm, b_omlfm = sb("omlfm", [128, 4], F32)
        hgn, b_hgn = sb("hgn", [128, 4], F32)
        lng, b_lng = sb("lng", [128, 8], F32)
        qg, b_qg = sb("qg", [128, 2], F32)
        kvg, b_kvg = sb("kvg", [128, 1], F32)
        Sst = [sb("S%d" % h, [128, 128], F32) for h in range(4)]
        Sp = [[sb("Sp%d_%d" % (h, k), [128, 128], BF16) for k in range(2)] for h in range(4)]
        tmw = Rot([sb("tmw%d" % i, [128, 512], F32) for i in range(3)])
        tmp = Rot([sb("tmp%d" % i, [128, BT], F32) for i in range(6)])
        hl = Rot([sb("hl%d" % i, [128, BT], BF16) for i in range(6)])
        tmpY = Rot([sb("tmpY%d" % i, [128, BT], F32) for i in range(4)])
        tmpB = Rot([sb("tmpB%d" % i, [128, BT], F32) for i in range(2)])
        kr1, b_kr1 = sb("kr1", [96, BT], F32)
        kr2, b_kr2 = sb("kr2", [96, BT], F32)
        hlY = Rot([sb("hlY%d" % i, [128, BT], BF16) for i in range(2)])
        xs = [sb("xs%d" % i, [128, D], F32) for i in range(2)]
        hb = [sb("hb%d" % i, [128, D], BF16) for i in range(2)]
        ssq = Rot([sb("ssq%d" % i, [128, 1], F32) for i in range(2)])
        lnv = Rot([sb("lnv%d" % i, [128, 1], F32) for i in range(2)])
        rstd = Rot([sb("rstd%d" % i, [128, 1], F32) for i in range(2)])
        hT, b_hT = sb("hT", [128, 8, BT], BF16)
        cosb = [sb("cosb%d" % k, [96, BT], F32) for k in range(2)]
        sinb = [sb("sinb%d" % k, [96, BT], F32) for k in range(2)]
        def grouped(name, n, dt_, sets):
            out_t, out_v = [], []
            for k in range(sets):
                t_, b_ = sb("%s%d" % (name, k), [128, n, BT], dt_)
                out_t.append((t_, b_))
                out_v.append([(t_[:, j, :], b_) for j in range(n)])
            return out_t, out_v
        qlf_t, qlf = grouped("qlf", 2, F32, 2)
        sqq_t, sqq = grouped("sqq", 2, F32, 2)
        kvf = [sb("kvf%d" % k, [128, BT], F32) for k in range(2)]
        sqk = [sb("sqk%d" % k, [128, BT], F32) for k in range(2)]
        kpe = [sb("kpe%d" % k, [96, BT], BF16) for k in range(2)]
        hfm_t, hfm = grouped("hfm", 4, F32, 2)
        hft = [[sb("hft%d_%d" % (k, i), [128, 512], F32) for i in range(TPB)] for k in range(2)]
        hqr_t, hqr = grouped("hqr", 4, F32, 2)
        gmr_t, gmr = grouped("gmr", 4, BF16, 2)
        ghr_t, ghr = grouped("ghr", 4, BF16, 2)
        vtm = [[sb("vtm%d_%d" % (k, i), [128, 512], BF16) for i in range(TPB)] for k in range(3)]
        qn, b_qn = sb("qn", [128, 2, BT], BF16)
        kvn, b_kvn = sb("kvn", [128, BT], BF16)
        QTo = Rot([sb("QTo%d" % i, [96, BT], BF16) for i in range(2)])
        KTo = Rot([sb("KTo%d" % i, [96, BT], BF16) for i in range(2)])
        Vb = Rot([sb("Vb%d" % i, [128, 4, 160], BF16) for i in range(2)])
        go = Rot([sb("go%d" % i, [128, 4, BT], BF16) for i in range(2)])
        gtm = [sb("gtm%d" % i, [128, 512], F32) for i in range(TPB)]
        ghi = [sb("ghi%d" % i, [128, 512], BF16) for i in range(TPB)]
        glo = [sb("glo%d" % i, [128, 512], BF16) for i in range(TPB)]
        sq_t, sq_v = grouped("sq_", 4, BF16, 1)
        sq_ = sq_v[0]
        ktl = [sb("ktl%d" % h, [128, BT], BF16) for h in range(4)]
        khat = [[sb("khat%d_%d" % (k, i), [128, 512], BF16) for i in range(TPB)] for k in range(2)]
        ghs_t, ghs = grouped("ghs", 4, BF16, 2)
        qtl = [[sb("qtl%d_%d" % (k, h), [128, BT], BF16) for h in range(4)] for k in range(2)]
        Abf = [[sb("Abf%d_%d" % (k, h), [128, BT], BF16) for h in range(4)] for k in range(2)]
        eC = [[sb("eC%d_%d" % (k, h), [128, 4 * TPB], F32) for h in range(4)] for k in range(2)]
        oT = [sb("oT%d" % h, [128, BT], F32) for h in range(4)]
        yo = Rot([sb("yo%d" % i, [128, BT], BF16) for i in range(2)])
        ri = Rot([sb("ri%d" % i, [128, 512], I32) for i in range(1)])
        stg = Rot(list(tmw.items) + [hft[k][i] for k in range(2) for i in range(TPB)])
        pA = Rot([ps("pA%d" % i, [128, 512], F32) for i in range(3)])
        pB = Rot([ps("pB%d" % i, [128, 512], F32) for i in range(2)])
        pU, _ = ps("pU", [128, 512], F32)
        b_pU = [Buf("pU", True)] * 4
        pO, _ = ps("pO", [128, 512], F32)
        b_pO = [Buf("pO", True)] * 4
        pI, _ = ps("pI", [128, 512], F32)
        b_pI = [Buf("pI", True)] * 4

        for (t, b), d in [((ident, b_ident), ident_d), ((lm, b_lm), lm_d), ((lcols, b_lcols), lcols_d),
                          ((us, b_us), us_d), ((lrep, b_lrep), lrep_d), ((invf, b_invf), invf_d),
                          ((hgn, b_hgn), hgn_d), ((lng, b_lng), lng_d), ((qg, b_qg), qg_d), ((kvg, b_kvg), kvg_d)]:
            E.dma(t[:], d, [], [b])
        E.memset("pool", onesf[:], 1.0, [b_onesf])
        for h in range(4):
            E.memset("pool", Sst[h][0][:], 0.0, [Sst[h][1]])
        for vb, b_vb in Vb.items:
            E.memset("pool", vb[:], 1.0, [b_vb])
        E.memset("pool", wkrr[:], 0.0, [b_wkrr])
        E.memset("pool", wqr[:], 0.0, [b_wqr])
        a0, b_a0 = tmw.next()
        a1, b_a1 = tmw.next()
        E.dma(a0[:], hlb_d[0:1, :].partition_broadcast(128), [], [b_a0])
        E.dma(a1[:], hlb_d[1:2, :].partition_broadcast(128), [], [b_a1])
        E.tt("dve", a0[:], a0[:], a1[:], ALU.subtract, [b_a0, b_a1], [b_a0])
        E.act(lbrep[:], a0[:], AF.Sigmoid, [b_a0], [b_lbrep])
        E.act(omlrep[:], a0[:], AF.Sigmoid, [b_a0], [b_omlrep], scale=-1.0)
        fm, b_fm = sb("lbfm", [128, 2, 4], F32)
        E.dma(fm[:], hlbfm_d, [], [b_fm])
        E.tt("dve", fm[:, 0, :], fm[:, 0, :], fm[:, 1, :], ALU.subtract, [b_fm], [b_fm])
        E.act(omlfm[:], fm[:, 0, :], AF.Sigmoid, [b_fm], [b_omlfm], scale=-1.0)
        ncast = [0]

        def cast_scaled(dst, src, col, reads, writes):
            E.ts("dve", dst, src, col, None, ALU.mult, None, reads, writes)
            ncast[0] += 1

        def setup_weights():
            for j0 in range(0, DIN, 512):
                w = min(512, DIN - j0)
                for c in range(8):
                    sg_, b_sg = stg.next()
                    E.dma(sg_[:, 0:w], win_d[:, c, j0:j0 + w], [], [b_sg])
                    cast_scaled(winb[:, c, j0:j0 + w], sg_[:, 0:w], lng[:, c:c + 1], [b_sg, b_lng], [b_wp[j0 // 512]])
                emit_rope(NRP)
                if j0 == 0:
                    for c in range(8):
                        E.ts("pool", wkrr[:, c, 64:80], winb[:, c, KR + 16:KR + 32], -1.0, None, ALU.mult, None, [b_wp[0]], [b_wkrr])
                        E.cp("pool", wkrr[:, c, 80:96], winb[:, c, KR:KR + 16], [b_wp[0]], [b_wkrr])
            for c in range(2):
                for j0 in range(0, 768, 512):
                    w = min(512, 768 - j0)
                    sg_, b_sg = stg.next()
                    E.dma(sg_[:, 0:w], wq_d[:, c, j0:j0 + w], [], [b_sg])
                    cast_scaled(wqb[:, c, j0:j0 + w], sg_[:, 0:w], qg[:, c:c + 1], [b_sg, b_qg], [b_wqb])
                v_in = wqb[:, c, :].rearrange("p (h f) -> p h f", h=8)
                v_out = wqr[:, c, :].rearrange("p (h f) -> p h f", h=8)
                E.ts("pool", v_out[:, :, 64:80], v_in[:, :, 80:96], -1.0, None, ALU.mult, None, [b_wqb], [b_wqr])
                E.cp("pool", v_out[:, :, 80:96], v_in[:, :, 64:80], [b_wqb], [b_wqr])
            for j0 in range(0, 1024, 512):
                sg_, b_sg = stg.next()
                E.dma(sg_[:], wkv_d[:, j0:j0 + 512], [], [b_sg])
                cast_scaled(wkvb[:, j0:j0 + 512], sg_[:], kvg[:, 0:1], [b_sg, b_kvg], [b_wkvb])
        R = slice(64, 96)
        rope_done = [0]

        NRP = (S // 512 + 3) // 4

        def emit_rope(upto):
            while rope_done[0] < min(upto, NRP):
                ps_ = rope_done[0]
                rope_done[0] += 1
                nb4 = min(4, S // 512 - 4 * ps_)
                NP = 32 * nb4
                it, b_it = ri.next()
                u, b_u = tmw.next()
                nf, b_nf = tmw.next()
                for q in range(nb4):
                    blk = 4 * ps_ + q
                    E.dma(it[32 * q:32 * q + 32, :], pos_d[0:1, blk * 512:(blk + 1) * 512].partition_broadcast(32), [], [b_it],
                          key=("ropeld", q))
                E.cp("dve", u[0:NP, :], it[0:NP, :], [b_it], [b_u])
                E.ts("dve", u[0:NP, :], u[0:NP, :], invf[0:NP, 0:1], None, ALU.mult, None, [b_u, b_invf], [b_u])
                for which, dst, bufs in ((0, Sd, bS), (1, Cd, bC)):
                    if which == 1:
                        E.ts("dve", u[0:NP, :], u[0:NP, :], 0.25, None, ALU.add, None, [b_u], [b_u])
                    E.cp("dve", it[0:NP, :], u[0:NP, :], [b_u], [b_it])
                    E.cp("dve", nf[0:NP, :], it[0:NP, :], [b_it], [b_nf])
                    E.tt("dve", nf[0:NP, :], u[0:NP, :], nf[0:NP, :], ALU.subtract, [b_u, b_nf], [b_nf])
                    E.act(nf[0:NP, :], nf[0:NP, :], AF.Sin, [b_nf], [b_nf], scale=2.0 * math.pi)
                    for q in range(nb4):
                        blk = 4 * ps_ + q
                        nbk = 512 // BT
                        E.dma(dst[:, blk * 512:(blk + 1) * 512], nf[32 * q:32 * q + 32, :], [b_nf],
                              [bufs[blk * nbk + k] for k in range(nbk)], key=("ropest", id(b_nf), q))

        emit_rope(1)
        setup_weights()

        MODE = os.environ.get("MK_MODE", "all")

        def stage_pair(specs):
            pt, b_pt = pA.next()
            for g, (col0, M, wt, b_wt) in enumerate(specs):
                for c in range(8):
                    lhs = winb[:, c, col0:col0 + M] if wt is None else wt[:, c, 0:M]
                    E.mm(pt[0:M, g * BT:(g + 1) * BT], lhs, hT[:, c, :], c == 0, c == 7,
                         (wb(col0, M) if wt is None else [b_wt]) + [b_hT], [b_pt])
            return pt, b_pt

        def stage_tm(i, col0):
            pt, b_pt = pA.next()
            for c in range(8):
                E.mm(pt[:, 0:512], hT[:, c, i * 128:(i + 1) * 128], winb[:, c, col0:col0 + 512], c == 0, c == 7,
                     wb(col0, 512) + [b_hT], [b_pt])
            return pt, b_pt

        def rep_rstd(sq_list, ndim, tpool, hpool, pbank, extra_bias=0.0):
            pt, b_pt = pbank
            parts = []
            for (sqt, b_sq) in sq_list:
                hi_, b_hi = hpool.next()
                lo_, b_lo = hpool.next()
                E.cp("act", hi_[:], sqt[:], [b_sq], [b_hi])
                E.tt("dve", lo_[:], sqt[:], hi_[:], ALU.subtract, [b_sq, b_hi], [b_lo])
                parts += [(hi_, b_hi), (lo_, b_lo)]
            for k, (sqt, b_sq) in enumerate(parts):
                E.mm(pt[:, 0:BT], onesf[:], sqt[:], k == 0, k == len(parts) - 1, [b_onesf, b_sq], [b_pt])
            l_, b_l = tpool.next()
            E.act(l_[:], pt[:, 0:BT], AF.Ln, [b_pt], [b_l], bias=EPS, scale=1.0 / ndim)
            r_, b_r = tpool.next()
            E.act(r_[:], l_[:], AF.Exp, [b_l], [b_r], scale=-0.5, bias=extra_bias)
            return r_, b_r

        def load_x(gt):
            t, b = xs[gt % 2]
            E.dma(t[:], x_d[gt * 128:(gt + 1) * 128, :], [], [b])

        load_x(0)
        stores = []

        def genX1(blk):
            k2 = blk % 2
            tok = slice(blk * BT, (blk + 1) * BT)
            E.dma(cosb[k2][0][R, :], Cd[:, tok], [bC[blk]], [cosb[k2][1]])
            E.dma(sinb[k2][0][R, :], Sd[:, tok], [bS[blk]], [sinb[k2][1]])
            for i in range(TPB):
                gt = blk * TPB + i
                if gt + 1 < NT:
                    load_x(gt + 1)
                xt, b_x = xs[gt % 2]
                ht, b_h = hb[gt % 2]
                sq1, b_sq1 = ssq.next()
                E.memset("pool", sq1[:], 0.0, [b_sq1])
                E.act(ht[:], xt[:], AF.Square, [b_x], [b_h, b_sq1], accum=sq1[:, 0:1])
                l1, b_l1 = lnv.next()
                E.act(l1[:], sq1[:], AF.Ln, [b_sq1], [b_l1], bias=EPS, scale=1.0 / D)
                r1, b_r1 = rstd.next()
                E.act(r1[:], l1[:], AF.Exp, [b_l1], [b_r1], scale=-0.5)
                yield
                E.act(ht[:], xt[:], AF.Copy, [b_x, b_r1], [b_h], scale=r1[:, 0:1])
                pt_, b_pT = pA.next()
                pT = pt_[:].bitcast(BF16)
                for c in range(8):
                    E.tr(pT[:, c * 128:(c + 1) * 128], ht[:, c * 128:(c + 1) * 128], ident[:], [b_h, b_ident], [b_pT])
                E.cp("dve", hT[:, :, i * 128:(i + 1) * 128], pT.rearrange("p (c t) -> p c t", c=8), [b_pT], [b_hT])
                yield
            pt, b_pt = stage_pair([(QL, 128, None, None), (QL + 128, 128, None, None)])
            p3 = pt[:, 0:2 * BT].rearrange("p (g t) -> p g t", g=2)
            E.cp("dve", qlf_t[k2][0][:], p3, [b_pt], [qlf_t[k2][1]])
            E.act(sqq_t[k2][0][:], p3, AF.Square, [b_pt], [sqq_t[k2][1]])
            yield
            pt, b_pt = stage_pair([(KVL, 128, None, None)])
            E.cp("dve", kvf[k2][0][:], pt[:, 0:BT], [b_pt], [kvf[k2][1]])
            E.act(sqk[k2][0][:], pt[:, 0:BT], AF.Square, [b_pt], [sqk[k2][1]])
            yield
            pt, b_pt = stage_pair([(KR - 64, 96, None, None), (0, 96, wkrr, b_wkrr)])
            E.tt("dve", kr1[R, :], pt[R, 0:BT], cosb[k2][0][R, :], ALU.mult, [b_pt, cosb[k2][1]], [b_kr1])
            E.tt("dve", kr2[R, :], pt[R, BT:2 * BT], sinb[k2][0][R, :], ALU.mult, [b_pt, sinb[k2][1]], [b_kr2])
            E.tt("dve", kpe[k2][0][R, :], kr1[R, :], kr2[R, :], ALU.add, [b_kr1, b_kr2], [kpe[k2][1]])
            yield
            n = 0
            for (col, dst_t) in ((GM, gmr_t), (HQ, hqr_t), (HF, hfm_t)):
                for p_ in range(2):
                    pt, b_pt = stage_pair([(col + (2 * p_) * 128, 128, None, None), (col + (2 * p_ + 1) * 128, 128, None, None)])
                    E.cp("act" if n % 2 == 0 else "dve", dst_t[k2][0][:, 2 * p_:2 * p_ + 2, :],
                         pt[:, 0:2 * BT].rearrange("p (g t) -> p g t", g=2), [b_pt], [dst_t[k2][1]])
                    n += 1
                    yield
            for i in range(TPB):
                pt, b_pt = stage_tm(i, HF)
                E.cp("dve" if i % 2 == 0 else "act", hft[k2][i][0][:], pt[:, 0:512], [b_pt], [hft[k2][i][1]])
                yield
            for i in range(TPB):
                pt, b_pt = stage_tm(i, HI)
                E.cp("act" if i % 2 == 0 else "dve", vtm[blk % 3][i][0][:], pt[:, 0:512], [b_pt], [vtm[blk % 3][i][1]])
                yield
            for p_ in range(2):
                pt, b_pt = stage_pair([(GH + (2 * p_) * 128, 128, None, None), (GH + (2 * p_ + 1) * 128, 128, None, None)])
                E.cp("act" if p_ % 2 == 0 else "dve", ghr_t[k2][0][:, 2 * p_:2 * p_ + 2, :],
                     pt[:, 0:2 * BT].rearrange("p (g t) -> p g t", g=2), [b_pt], [ghr_t[k2][1]])
                yield

        def genX2(blk):
            k2 = blk % 2
            tok = slice(blk * BT, (blk + 1) * BT)
            cb, b_cb = cosb[k2]
            sn, b_sn = sinb[k2]
            for i in range(TPB):
                t, b_t = hft[k2][i]
                E.act(t[:], t[:], AF.Sigmoid, [b_t], [b_t])
            t, b_t = hfm_t[k2]
            E.act(t[:], t[:], AF.Sigmoid, [b_t], [b_t], scale=-1.0)
            yield
            g_, b_g = go.next()
            E.act(g_[:], gmr_t[k2][0][:], AF.Silu, [gmr_t[k2][1]], [b_g])
            stores.append(E.dma(Gd[:, :, tok].rearrange("j p t -> p j t"), g_[:], [b_g], [bG[0][blk]], store=True))
            E.act(sq_t[0][0][:], hqr_t[k2][0][:], AF.Silu, [hqr_t[k2][1]], [sq_t[0][1]])
            yield
            E.act(ghs_t[k2][0][:], ghr_t[k2][0][:], AF.Silu, [ghr_t[k2][1]], [ghs_t[k2][1]])
            yield "EL"
            def chainA():
                rq, b_rq = rep_rstd(sqq[k2], 256, tmp, hl, pB.next(), extra_bias=math.log(1.0 / math.sqrt(96.0)))
                for g in range(2):
                    E.tt("dve", qn[:, g, :], qlf[k2][g][0][:], rq[:], ALU.mult, [qlf[k2][g][1], b_rq], [b_qn])
                yield
                rk, b_rk = rep_rstd([sqk[k2]], 128, tmp, hl, pB.next())
                E.tt("dve", kvn[:], kvf[k2][0][:], rk[:], ALU.mult, [kvf[k2][1], b_rk], [b_kvn])
                yield
                for h in range(8):
                    pq, b_pq = pB.next()
                    for c in range(2):
                        E.mm(pq[0:96, 0:BT], wqb[:, c, h * 96:(h + 1) * 96], qn[:, c, :], c == 0, c == 1, [b_wqb, b_qn], [b_pq])
                    for c in range(2):
                        E.mm(pq[0:96, BT:2 * BT], wqr[:, c, h * 96:(h + 1) * 96], qn[:, c, :], c == 0, c == 1, [b_wqr, b_qn], [b_pq])
                    qo, b_qo = QTo.next()
                    E.cp("act", qo[0:64, :], pq[0:64, 0:BT], [b_pq], [b_qo])
                    t1, b_t1 = tmp.next()
                    E.tt("dve", t1[R, :], pq[R, 0:BT], cb[R, :], ALU.mult, [b_pq, b_cb], [b_t1])
                    t2, b_t2 = tmp.next()
                    E.tt("dve", t2[R, :], pq[R, BT:2 * BT], sn[R, :], ALU.mult, [b_pq, b_sn], [b_t2])
                    E.tt("pool", qo[R, :], t1[R, :], t2[R, :], ALU.add, [b_t1, b_t2], [b_qo])
                    stores.append(E.dma(Qd[h, :, tok], qo[:], [b_qo], [bQ[h][blk]], store=True))
                    yield
                    if h % 2 == 1:
                        pk, b_pk = pB.next()
                        for j, hh in enumerate((h - 1, h)):
                            E.mm(pk[0:64, j * BT:(j + 1) * BT], wkvb[:, hh * 128:hh * 128 + 64], kvn[:], True, True,
                                 [b_wkvb, b_kvn], [b_pk])
                        for j, hh in enumerate((h - 1, h)):
                            ko, b_ko = KTo.next()
                            E.cp("act", ko[0:64, :], pk[0:64, j * BT:(j + 1) * BT], [b_pk], [b_ko])
                            E.cp("pool", ko[R, :], kpe[k2][0][R, :], [kpe[k2][1]], [b_ko])
                            stores.append(E.dma(Kd[hh, :, tok], ko[:], [b_ko], [bK[hh][blk]], store=True))
                        yield
                wv = wkvb[:].rearrange("p (h f) -> p h f", h=8)[:, :, 64:128]
                for i in range(TPB):
                    gt = blk * TPB + i
                    pv, b_pv = pB.next()
                    E.mm(pv[:, 0:512].rearrange("p (h f) -> p h f", h=8), kvn[:, i * 128:(i + 1) * 128], wv, True, True,
                         [b_kvn, b_wkvb], [b_pv])
                    vb, b_vb = Vb.next()
                    pv4 = pv[:, 0:512].rearrange("p (j f) -> p j f", j=4)
                    E.cp("dve", vb[:, :, 0:64], pv4[:, :, 0:64], [b_pv], [b_vb])
                    E.cp("act", vb[:, :, 96:160], pv4[:, :, 64:128], [b_pv], [b_vb])
                    stores.append(E.dma(Vd[gt], vb[:].rearrange("p j f -> p (j f)"), [b_vb], [bV[gt]], store=True))
                    yield

            def chainB():
                for i in range(TPB):
                    s_, b_s = hft[k2][i]
                    E.tt("dve", s_[:], s_[:], omlrep[:], ALU.mult, [b_s, b_omlrep], [b_s])
                    E.tt("dve", s_[:], s_[:], lbrep[:], ALU.add, [b_s, b_lbrep], [b_s])
                    E.act(gtm[i][0][:], s_[:], AF.Ln, [b_s], [gtm[i][1]])
                    E.cp("act", ghi[i][0][:], gtm[i][0][:], [gtm[i][1]], [ghi[i][1]])
                    E.tt("dve", glo[i][0][:], gtm[i][0][:], ghi[i][0][:], ALU.subtract, [gtm[i][1], ghi[i][1]], [glo[i][1]])
                    yield
                for i in range(TPB):
                    s_, b_s = hft[k2][i]
                    pr_, b_pr = pB.next()
                    E.mm(pr_[:, 0:512], us[:], ghi[i][0][:], True, False, [b_us, ghi[i][1]], [b_pr])
                    E.mm(pr_[:, 0:512], us[:], glo[i][0][:], False, True, [b_us, glo[i][1]], [b_pr])
                    er, b_er = tmw.next()
                    E.act(er[:], pr_[:, 0:512], AF.Exp, [b_pr], [b_er])
                    E.stt("dve", khat[k2][i][0][:], s_[:], 1.0, er[:], ALU.subtract, ALU.mult, [b_s, b_er], [khat[k2][i][1]])
                    yield
                for h in range(4):
                    hs = slice(h * 128, (h + 1) * 128)
                    pd, b_pd = pB.next()
                    for i in range(TPB):
                        E.mm(pd[:, i * 128:(i + 1) * 128], ghi[i][0][:, hs], lm[:], True, False, [ghi[i][1], b_lm], [b_pd])
                        E.mm(pd[:, i * 128:(i + 1) * 128], glo[i][0][:, hs], lm[:], False, True, [glo[i][1], b_lm], [b_pd])
                    for i in range(TPB):
                        E.mm(pd[:, BT + i * 4:BT + (i + 1) * 4], ghi[i][0][:, hs], lcols[:], True, False, [ghi[i][1], b_lcols], [b_pd])
                        E.mm(pd[:, BT + i * 4:BT + (i + 1) * 4], glo[i][0][:, hs], lcols[:], False, True, [glo[i][1], b_lcols], [b_pd])
                    ed, b_ed = tmpB.next()
                    edn, b_edn = tmpB.next()
                    E.act(ed[:], pd[:, 0:BT], AF.Exp, [b_pd], [b_ed])
                    E.act(edn[:], pd[:, 0:BT], AF.Exp, [b_pd], [b_edn], scale=-1.0)
                    E.act(eC[k2][h][0][:], pd[:, BT:BT + 4 * TPB], AF.Exp, [b_pd], [eC[k2][h][1]])
                    E.tt("dve", qtl[k2][h][0][:], sq_[h][0][:], ed[:], ALU.mult, [sq_[h][1], b_ed], [qtl[k2][h][1]])
                    E.stt("dve", ktl[h][0][:], hfm[k2][h][0][:], omlfm[:, h:h + 1], edn[:], ALU.mult, ALU.mult,
                          [hfm[k2][h][1], b_omlfm, b_edn], [ktl[h][1]])
                    yield
                    pa, b_pa = pB.next()
                    for i in range(TPB):
                        ts_ = slice(i * 128, (i + 1) * 128)
                        E.mm(pa[:, ts_], ktl[h][0][:, ts_], qtl[k2][h][0][:, ts_], True, True, [ktl[h][1], qtl[k2][h][1]], [b_pa])
                    E.tt("dve", Abf[k2][h][0][:], pa[:, 0:BT], lrep[:, 0:BT], ALU.mult, [b_pa, b_lrep], [Abf[k2][h][1]])
                    yield

            ga, gb = chainA(), chainB()
            while ga is not None or gb is not None:
                if gb is not None:
                    try:
                        next(gb)
                        yield
                    except StopIteration:
                        gb = None
                if ga is not None:
                    try:
                        next(ga)
                        yield
                    except StopIteration:
                        ga = None

        def genY(blk):
            k2 = blk % 2
            k3 = blk % 3
            tok = slice(blk * BT, (blk + 1) * BT)
            for i in range(TPB):
                ts_ = slice(i * 128, (i + 1) * 128)
                for h in range(4):
                    hs = slice(h * 128, (h + 1) * 128)
                    E.mm(pO[:, hs], vtm[k3][i][0][:, hs], Abf[k2][h][0][:, ts_], True, True, [vtm[k3][i][1], Abf[k2][h][1]],
                         [b_pO[h]])
                for half in range(2):
                    rows = slice(half * 64, half * 64 + 64)
                    c = 2 * i + half
                    cc = slice(c * 64, c * 64 + 64)
                    mid = i * 4 + half * 2
                    for h in range(4):
                        S_, b_S = Sst[h]
                        sp_, b_sp = Sp[h][c % 2]
                        E.ts("dve", sp_[:], S_[:], eC[k2][h][0][:, mid:mid + 1], None, ALU.mult, None, [b_S, eC[k2][h][1]], [b_sp])
                    for h in range(4):
                        hs = slice(h * 128, (h + 1) * 128)
                        sp_, b_sp = Sp[h][c % 2]
                        E.mm(pI[:, h * 128 + half * 64:h * 128 + half * 64 + 64], sp_[:], qtl[k2][h][0][:, cc], True, True,
                             [b_sp, qtl[k2][h][1]], [b_pI[h]])
                        E.mm(pU[:, hs], khat[k2][i][0][rows, hs], vtm[k3][i][0][rows, hs], True, True,
                             [khat[k2][i][1], vtm[k3][i][1]], [b_pU[h]])
                    yield
                    for h in range(4):
                        hs = slice(h * 128, (h + 1) * 128)
                        S_, b_S = Sst[h]
                        E.stt("dve", S_[:], S_[:], eC[k2][h][0][:, mid + 1:mid + 2], pU[:, hs], ALU.mult, ALU.subtract,
                              [b_S, eC[k2][h][1], b_pU[h]], [b_S])
                    yield
                for h in range(4):
                    hs = slice(h * 128, (h + 1) * 128)
                    E.cp("act", oT[h][0][:, ts_], pO[:, hs], [b_pO[h]], [oT[h][1]])
                    E.tt("dve", oT[h][0][:, ts_], pI[:, hs], oT[h][0][:, ts_], ALU.add, [b_pI[h], oT[h][1]], [oT[h][1]])
                yield
            yield "EPI"
            for h in range(4):
                s_, b_s = tmpY.next()
                E.act(s_[:], oT[h][0][:], AF.Square, [oT[h][1]], [b_s])
                ro, b_ro = rep_rstd([(s_, b_s)], 128, tmpY, hlY, (pU, b_pU[0]))
                y1, b_y1 = tmpY.next()
                E.tt("dve", y1[:], oT[h][0][:], ro[:], ALU.mult, [oT[h][1], b_ro], [b_y1])
                y_, b_y = yo.next()
                E.stt("dve", y_[:], y1[:], hgn[:, h:h + 1], ghs[k2][h][0][:], ALU.mult, ALU.mult,
                      [b_y1, b_hgn, ghs[k2][h][1]], [b_y])
                stores.append(E.dma(Yd[h, :, tok], y_[:], [b_y], [bY[h][blk]], store=True))
                yield

        def step(g):
            try:
                return next(g), True
            except StopIteration:
                return None, False

        NIT = NB1 + 2 if MODE != "setup" else 0
        for t in range(NIT):
            g1 = genX1(t) if t < NB1 else None
            g2 = genX2(t - 1) if 0 <= t - 1 < NB1 else None
            g3 = genY(t - 2) if 0 <= t - 2 < NB1 else None
            x2_el = g2 is None
            y_hold = False
            rnd = 0
            while g1 is not None or g2 is not None or g3 is not None:
                rnd += 1
                if g3 is not None and not (y_hold and not x2_el):
                    tag, alive = step(g3)
                    if not alive:
                        g3 = None
                    elif tag == "EPI":
                        y_hold = True
                for _ in range(2 if (rnd > 4 or g1 is None) else 0):
                    if g2 is not None:
                        tag, alive = step(g2)
                        if not alive:
                            g2 = None
                            x2_el = True
                        elif tag == "EL":
                            x2_el = True
                if g1 is not None:
                    tag, alive = step(g1)
                    if not alive:
                        g1 = None
        P.build(nc, st, "a")
        n1 = len(P.ops)

    if MODE in ("setup", "pass1"):
        nc._mk_stats = (n1, 0)
        return nc
    if MODE != "nobar":
        nc.all_engine_barrier()

    with ExitStack() as st:
        P = Prog()
        E = Em(P)

        def sb(name, shape, dt):
            return st.enter_context(nc.sbuf_tensor("b_" + name, shape, dt)), Buf(name)

        def ps(name, shape, dt):
            return st.enter_context(nc.psum_tensor("b_" + name, shape, dt)), Buf(name, True)

        KT = [sb("KT%d" % h, [96, S], BF16) for h in range(8)]
        Vst, _ = sb("Vst", [128, NT, 640], BF16)
        b_Vst = [Buf() for _ in range(NT)]
        woutb, b_woutb = sb("woutb", [128, 8, 1024], BF16)
        fng, b_fng = sb("fng", [128, 1024], F32)
        tri, b_tri = sb("tri", [128, 128], BF16)
        onesf, b_onesf = sb("onesf", [128, 128], BF16)
        stg = Rot([sb("stg%d" % i, [128, 512], F32) for i in range(2)])
        QT = [sb("QT%d" % h, [96, 512], BF16) for h in range(8)]
        gm_t = [sb("gm%d" % k, [128, 4, 512], BF16) for k in range(2)]
        yh_t = [sb("yh%d" % k, [128, 4, 512], BF16) for k in range(2)]
        gm = [[(gm_t[k][0][:, j, :], gm_t[k][1]) for j in range(4)] for k in range(2)]
        yh = [[(yh_t[k][0][:, j, :], yh_t[k][1]) for j in range(4)] for k in range(2)]
        ymla = [[sb("ymla%d_%d" % (k, j), [128, 512], BF16) for j in range(4)] for k in range(2)]
        PT = Rot([sb("PT%d" % i, [128, 512], BF16) for i in range(3)])
        rr = Rot([sb("rr%d" % i, [128, 512], F32) for i in range(4)])
        bc = Rot([sb("bc%d" % i, [128, 512], F32) for i in range(2)])
        yt = Rot([sb("yt%d" % i, [128, 512], F32) for i in range(2)])
        xr = Rot([sb("xr%d" % i, [128, D], F32) for i in range(2)])
        z = Rot([sb("z%d" % i, [128, D], F32) for i in range(2)])
        res = Rot([sb("res%d" % i, [128, D], F32) for i in range(2)])
        ssq = Rot([sb("ssq%d" % i, [128, 1], F32) for i in range(2)])
        lnv = Rot([sb("lnv%d" % i, [128, 1], F32) for i in range(2)])
        rstd = Rot([sb("rstd%d" % i, [128, 1], F32) for i in range(2)])
        pS = Rot([ps("pS%d" % i, [128, 512], F32) for i in range(3)])
        pO = Rot([ps("pO%d" % i, [128, 512], F32) for i in range(3)])
        pOut = Rot([ps("pOut%d" % i, [128, 512], F32) for i in range(2)])

        E.dma(tri[:], tri_d, [], [b_tri])
        E.dma(fng[:], fng_d.partition_broadcast(128), [], [b_fng])
        E.memset("pool", onesf[:], 1.0, [b_onesf])

        def load_q(qb, h):
            E.dma(QT[h][0][:], Qd[h, :, qb * 512:(qb + 1) * 512], [], [QT[h][1]])

        b_KT = [[Buf("KT%d_%d" % (h, q)) for q in range(NB2)] for h in range(8)]

        def load_kv(qb):
            cs = slice(qb * 512, (qb + 1) * 512)
            for h in range(8):
                E.dma(KT[h][0][:, cs], Kd[h, :, cs], [b_KT[h][qb - 1]] if qb > 0 else [], [b_KT[h][qb]], key=("kt", h))
                if h == 0:
                    g0 = 4 * qb
                    E.dma(Vst[:, g0:g0 + 4, :], Vd[g0:g0 + 4].rearrange("t p f -> p t f"),
                          [b_Vst[g0 - 1]] if qb > 0 else [], [b_Vst[t] for t in range(g0, g0 + 4)], key="vst")

        load_q(0, 0)
        load_kv(0)
        for h in range(1, 8):
            load_q(0, h)
        outs = []
        items = [(qb, h, kb) for qb in range(NB2) for h in range(8) for kb in range(4 * qb + 4)]
        NI = len(items)
        LOOK = 2
        FDLY = 14
        st_info = {}
        cur_po = {}
        pend = []
        seqc = [0]
        fin_left = {qb: 8 for qb in range(NB2)}
        b_Rd = [Buf("Rd%d" % i) for i in range(4)]

        def defer(due, fn, g=None):
            pend.append((due, seqc[0], fn, g))
            seqc[0] += 1

        def load_gates(qb):
            qs = slice(qb * 512, (qb + 1) * 512)
            E.dma(gm_t[qb % 2][0][:], Gd[:, :, qs].rearrange("j p t -> p j t"), [], [gm_t[qb % 2][1]])
            E.dma(yh_t[qb % 2][0][:], Yd[:, :, qs].rearrange("j p t -> p j t"), [], [yh_t[qb % 2][1]])

        def emit_st(i):
            qb, h, kb = items[i]
            if h == 0 and kb == 0:
                load_gates(qb)
                if qb + 1 < NB2:
                    load_kv(qb + 1)
            jd = kb - 4 * qb
            qoff = 128 * jd if jd > 0 else 0
            ncol = 512 - qoff
            ps_, b_ps = pS.next()
            E.mm(ps_[:, 0:ncol], KT[h][0][:, kb * 128:(kb + 1) * 128], QT[h][0][:, qoff:512], True, True,
                 [b_KT[h][kb // 4], QT[h][1]], [b_ps])
            st_info[i] = (ps_, b_ps, qoff, ncol, jd)
            if kb == 4 * qb + 3 and qb + 1 < NB2:
                load_q(qb + 1, h)

        def out_proj_chunks(qb, i_now):
            due = i_now + 10
            for i in range(4):
                gt = qb * 4 + i
                ts_ = slice(i * 128, (i + 1) * 128)
                hold = {}

                def c_load(gt=gt, hold=hold):
                    hold["x"] = xr.next()
                    hold["z"] = z.next()
                    E.dma(hold["x"][0][:], x_d[gt * 128:(gt + 1) * 128, :], [], [hold["x"][1]])
                defer(due, c_load)
                for n in range(2):
                    def c_mm(n=n, ts_=ts_, hold=hold, qb=qb):
                        ns = slice(n * 512, (n + 1) * 512)
                        pp, b_pp = pOut.next()
                        for ch in range(8):
                            src = ymla[qb % 2][ch] if ch < 4 else yh[qb % 2][ch - 4]
                            E.mm(pp[:, :], src[0][:, ts_], woutb[:, ch, ns], ch == 0, ch == 7, [src[1], b_woutb], [b_pp])
                        E.tt("dve", hold["z"][0][:, ns], pp[:, :], hold["x"][0][:, ns], ALU.add, [b_pp, hold["x"][1]],
                             [hold["z"][1]])
                    defer(due, c_mm)
                    due += 1

                def c_fin(gt=gt, hold=hold):
                    z_, b_z = hold["z"]
                    r_, b_r = res.next()
                    sq1, b_sq1 = ssq.next()
                    E.memset("pool", sq1[:], 0.0, [b_sq1])
                    E.act(r_[:], z_[:], AF.Square, [b_z], [b_r, b_sq1], accum=sq1[:, 0:1])
                    l1, b_l1 = lnv.next()
                    E.act(l1[:], sq1[:], AF.Ln, [b_sq1], [b_l1], bias=EPS, scale=1.0 / D)
                    r1, b_r1 = rstd.next()
                    E.act(r1[:], l1[:], AF.Exp, [b_l1], [b_r1], scale=-0.5)
                    E.stt("dve", r_[:], z_[:], r1[:, 0:1], fng[:], ALU.mult, ALU.mult, [b_z, b_r1, b_fng], [b_r])
                    outs.append(E.dma(out_d[gt * 128:(gt + 1) * 128, :], r_[:], [b_r], [Buf()], store=True))
                defer(due, c_fin)

        def emit_exp_pv(i):
            qb, h, kb = items[i]
            nkb = 4 * qb + 4
            pair, odd = h // 2, h % 2
            voff = 32 if odd else 0
            ps_, b_ps, qoff, ncol, jd = st_info.pop(i)
            if kb == 0:
                g = qb * 8 + h
                for p in [p for p in pend if p[3] is not None and p[3] <= g - 3]:
                    pend.remove(p)
                    p[2]()
                cur_po[(qb, h)] = pO.next()
            po, b_po = cur_po[(qb, h)]
            pt, b_pt = PT.next()
            E.act(pt[:, 0:ncol], ps_[:, 0:ncol], AF.Exp, [b_ps], [b_pt])
            if jd >= 0:
                E.tt("pool", pt[:, 0:128], pt[:, 0:128], tri[:], ALU.mult, [b_pt, b_tri], [b_pt])
            E.mm(po[:, qoff:512], Vst[:, kb, pair * 160 + voff:pair * 160 + voff + 128], pt[:, 0:ncol],
                 kb == 0, kb == nkb - 1, [b_Vst[kb], b_pt], [b_po])
            if kb != nkb - 1:
                return
            row = 32 if odd else 64
            rows = slice(64, 128) if odd else slice(0, 64)
            r_, b_r = rr.next()
            E.recip(r_[row:row + 1, :], po[row:row + 1, :], [b_po], [b_r])

            slot = (qb * 8 + h) % 4

            def fin1(row=row, r_=r_, b_r=b_r, slot=slot):
                E.dma(Rd[slot:slot + 1, :], r_[row:row + 1, :], [b_r], [b_Rd[slot]], store=True)
            defer(i + 7, fin1, qb * 8 + h)

            def fin2(qb=qb, h=h, pair=pair, row=row, rows=rows, po=po, b_po=b_po, slot=slot):
                bc_, b_bc = bc.next()
                E.dma(bc_[rows, :], Rd[slot:slot + 1, :].partition_broadcast(64), [b_Rd[slot]], [b_bc])
                y_, b_y = yt.next()
                E.tt("dve", y_[rows, :], po[rows, :], bc_[rows, :], ALU.mult, [b_po, b_bc], [b_y])
                E.tt("pool", ymla[qb % 2][pair][0][rows, :], y_[rows, :], gm[qb % 2][pair][0][rows, :], ALU.mult,
                     [b_y, gm[qb % 2][pair][1]], [ymla[qb % 2][pair][1]])
                fin_left[qb] -= 1
                if fin_left[qb] == 0:
                    out_proj_chunks(qb, cur_iter[0])
            defer(i + FDLY, fin2, qb * 8 + h)

        def wout_chunk(c, n):
            def f():
                sg_, b_sg = stg.next()
                E.dma(sg_[:], wout_d[:, c, n * 512:(n + 1) * 512], [], [b_sg])
                E.cp("dve", woutb[:, c, n * 512:(n + 1) * 512], sg_[:], [b_sg], [b_woutb])
            return f
        for c in range(8):
            for n in range(2):
                defer(3 + c * 2 + n, wout_chunk(c, n))

        cur_iter = [0]
        for it in range(NI + LOOK):
            cur_iter[0] = it
            if it < NI:
                emit_st(it)
            if it >= LOOK:
                emit_exp_pv(it - LOOK)
            ready = sorted([p for p in pend if p[0] <= it], key=lambda p: (p[0], p[1]))
            for p in ready[:2]:
                pend.remove(p)
                p[2]()
        cur_iter[0] = NI + LOOK + 10 ** 6
        while pend:
            p = sorted(pend, key=lambda p: (p[0], p[1]))[0]
            pend.remove(p)
            p[2]()
        P.build(nc, st, "b")
        n2 = len(P.ops)
    nc._mk_stats = (n1, n2)
    return nc


def _consts():
    s = np.arange(128)[:, None]
    t = np.arange(128)[None, :]
    same = (s // 64) == (t // 64)
    L = (same & (s <= t)).astype(np.float32)
    ref = (t // 64) * 64 + 31
    Lr = (same & (s <= ref)).astype(np.float32)
    lm = L - Lr
    us = (same & (s > t)).astype(np.float32)
    lcols = np.zeros((128, 4), np.float32)
    lcols[0:32, 0] = 1
    lcols[0:64, 1] = 1
    lcols[64:96, 2] = 1
    lcols[64:128, 3] = 1
    lrep = np.tile(L, (1, 4)).astype(np.float32)
    ident = np.eye(128, dtype=np.float32).astype(ml_dtypes.bfloat16)
    tri = (t >= s).astype(np.float32).astype(ml_dtypes.bfloat16)
    inv = (10000.0 ** (-np.arange(16, dtype=np.float32) / 16.0)).astype(np.float32)
    invf = np.tile(inv / (2 * np.pi), 8).reshape(128, 1).astype(np.float32)
    bf = ml_dtypes.bfloat16
    return dict(c_ident=ident, c_tri=tri, c_lm=lm.astype(bf), c_lcols=lcols.astype(bf), c_us=us.astype(bf), c_lrep=lrep, c_invf=invf)


def _layout_weights(ln_g, w_in, q_a_norm_g, w_q_b, kv_a_norm_g, w_kv_b, hg_lower_bounds, hg_norm_g, w_out, final_norm_g):
    f = lambda a: np.ascontiguousarray(a, dtype=np.float32)
    d = {}
    d["w_in_l"] = f(w_in[0].reshape(8, 128, DIN).transpose(1, 0, 2))
    d["ln_g_l"] = f(ln_g[0].reshape(8, 128).T)
    d["w_q_l"] = f(w_q_b[0].reshape(2, 128, 768).transpose(1, 0, 2))
    d["q_g_l"] = f(q_a_norm_g[0].reshape(2, 128).T)
    d["w_kv_l"] = f(w_kv_b[0])
    d["kv_g_l"] = f(kv_a_norm_g[0].reshape(128, 1))
    d["w_out_l"] = f(w_out[0].reshape(8, 128, 1024).transpose(1, 0, 2))
    d["hgn_l"] = f(hg_norm_g[0].reshape(4, 128).T)
    d["hlb"] = f(hg_lower_bounds)
    d["hlb_fm"] = f(hg_lower_bounds.reshape(2, 4, 128).transpose(2, 0, 1))
    d["fng"] = f(final_norm_g.reshape(1, 1024))
    d.update(_consts())
    return d


_NC_CACHE = {}


def run(x, positions, weights, S, BT=256):
    B = x.shape[0]
    key = (S, BT)
    if key not in _NC_CACHE:
        _NC_CACHE[key] = build(S, BT)
    nc = _NC_CACHE[key]
    shared = _layout_weights(**weights)
    in_maps = []
    for b in range(B):
        m = dict(shared)
        m["x"] = np.ascontiguousarray(x[b], dtype=np.float32)
        m["pos"] = np.ascontiguousarray(positions[b].reshape(1, S), dtype=np.int32)
        in_maps.append(m)
    r = run_bass_kernel_spmd(nc, in_maps, core_ids=list(range(B)))
    return np.stack([np.asarray(r.results[b]["out"]) for b in range(B)], axis=0).astype(np.float32)


def kernel(x, positions, ln_g, w_in, q_a_norm_g, w_q_b, kv_a_norm_g, w_kv_b, hg_lower_bounds, hg_norm_g, w_out,
           final_norm_g):
    x = np.asarray(x)
    weights = dict(ln_g=np.asarray(ln_g), w_in=np.asarray(w_in), q_a_norm_g=np.asarray(q_a_norm_g),
                   w_q_b=np.asarray(w_q_b), kv_a_norm_g=np.asarray(kv_a_norm_g), w_kv_b=np.asarray(w_kv_b),
                   hg_lower_bounds=np.asarray(hg_lower_bounds), hg_norm_g=np.asarray(hg_norm_g),
                   w_out=np.asarray(w_out), final_norm_g=np.asarray(final_norm_g))
    return run(x, np.asarray(positions), weights, x.shape[1])
```

```python
import math
import os
from contextlib import ExitStack

import numpy as np
import ml_dtypes
import concourse.bass as bass
import concourse.mybir as mybir
from concourse.bass_utils import run_bass_kernel_spmd

F32 = mybir.dt.float32
BF16 = mybir.dt.bfloat16
I32 = mybir.dt.int32
AF = mybir.ActivationFunctionType
ALU = mybir.AluOpType

SAME_ENGINE_SYNC = os.environ.get("MK_SES", "1") == "1"
NOSYNC_ENGS = set(os.environ.get("MK_NOSYNC", "").split(","))
EPS = 1e-6
D = 1024
DIN = 2976
QL, KVL, KR, GM, HQ, HF, HI, GH = 0, 256, 384, 416, 928, 1440, 1952, 2464


class Buf:
    __slots__ = ("name", "w", "r", "excl")

    def __init__(self, name="", excl=False):
        self.name = name
        self.w = None
        self.r = []
        self.excl = excl


class Op:
    __slots__ = ("eng", "fn", "deps", "dma", "key", "tok", "has_dep", "id")


class Prog:
    def __init__(self):
        self.ops = []

    def emit(self, eng, fn, reads=(), writes=(), dma=False, key=None):
        op = Op()
        op.eng, op.fn, op.dma, op.id, op.tok, op.has_dep = eng, fn, dma, len(self.ops), None, False
        deps = set()
        writes = list(writes) + [r for r in reads if r.excl]
        reads = [r for r in reads if not r.excl]
        for r in reads:
            if r.w is not None:
                deps.add(r.w)
        for w in writes:
            if w.w is not None:
                deps.add(w.w)
            deps.update(w.r)
        keep = set()
        for d in deps:
            dop = self.ops[d]
            if not dop.dma and dop.eng == eng and (eng == "pe" or not SAME_ENGINE_SYNC or eng in NOSYNC_ENGS):
                continue
            keep.add(d)
        op.deps = keep
        for d in keep:
            self.ops[d].has_dep = True
        for r in reads:
            r.r.append(op.id)
        for w in writes:
            w.w = op.id
            w.r = []
        if dma:
            op.key = key if key is not None else id((list(writes) + list(reads))[0])
        self.ops.append(op)
        return op

    def build(self, nc, st, tag):
        engs = ["pe", "act", "dve", "pool", "sp"]
        ops = self.ops
        last = {}
        for op in ops:
            if not op.dma:
                last[op.eng] = op
        finals = [o for o in ops if o.dma] + [o for e, o in last.items() if e != "sp"]
        for o in finals:
            o.has_dep = True
        esem = {e: st.enter_context(nc.semaphore("s%s_%s" % (tag, e))) for e in engs}
        dsem, dcnt, ecnt = {}, {}, {e: 0 for e in engs}
        for op in ops:
            if op.dma:
                if op.key not in dsem:
                    dsem[op.key] = st.enter_context(nc.semaphore("d%s_%d" % (tag, len(dsem))))
                    dcnt[op.key] = 0
                dcnt[op.key] += 16
                op.tok = (dsem[op.key], dcnt[op.key], 16)
            elif op.has_dep:
                ecnt[op.eng] += 1
                op.tok = (esem[op.eng], ecnt[op.eng], 1)
        self.n_sems = len(dsem) + 5
        block = st.enter_context(nc.Block())

        def run_engine(ename):
            def body(eng):
                waited = {}

                def wait(tok):
                    sem, val, _ = tok
                    if waited.get(sem.num, 0) < val:
                        eng.wait_ge(sem, val)
                        waited[sem.num] = val

                for op in ops:
                    if op.eng != ename:
                        continue
                    for d in sorted(op.deps):
                        wait(ops[d].tok)
                    inst = op.fn(eng)
                    if op.tok is not None:
                        inst.then_inc(op.tok[0], op.tok[2])
                if ename == "sp":
                    best = {}
                    for fo in finals:
                        if fo.tok[0].num not in best or best[fo.tok[0].num][1] < fo.tok[1]:
                            best[fo.tok[0].num] = fo.tok
                    for tk in best.values():
                        wait(tk)
            return body

        block.tensor(run_engine("pe"))
        block.scalar(run_engine("act"))
        block.vector(run_engine("dve"))
        block.gpsimd(run_engine("pool"))
        block.sync(run_engine("sp"))


class Em:
    def __init__(self, P):
        self.P = P

    def mm(self, out, lhsT, rhs, start, stop, reads, writes):
        self.P.emit("pe", lambda e: e.matmul(out, lhsT, rhs, start=start, stop=stop), reads, writes)

    def tr(self, out, in_, ident, reads, writes):
        self.P.emit("pe", lambda e: e.transpose(out, in_, ident), reads, writes)

    def act(self, out, in_, func, reads, writes, bias=0.0, scale=1.0, accum=None):
        if accum is None:
            self.P.emit("act", lambda e: e.activation(out, in_, func, bias=bias, scale=scale), reads, writes)
        else:
            self.P.emit("act", lambda e: e.activation(out, in_, func, bias=bias, scale=scale, accum_out=accum), reads, writes)

    def tt(self, eng, out, a, b, op, reads, writes):
        self.P.emit(eng, lambda e: e.tensor_tensor(out, a, b, op=op), reads, writes)

    def ts(self, eng, out, a, s1, s2, op0, op1, reads, writes):
        if s2 is None:
            self.P.emit(eng, lambda e: e.tensor_scalar(out, a, s1, None, op0=op0), reads, writes)
        else:
            self.P.emit(eng, lambda e: e.tensor_scalar(out, a, s1, s2, op0=op0, op1=op1), reads, writes)

    def stt(self, eng, out, in0, scalar, in1, op0, op1, reads, writes):
        self.P.emit(eng, lambda e: e.scalar_tensor_tensor(out, in0, scalar, in1, op0=op0, op1=op1), reads, writes)

    def cp(self, eng, out, in_, reads, writes):
        if eng == "act":
            self.P.emit("act", lambda e: e.activation(out, in_, AF.Copy), reads, writes)
        else:
            self.P.emit(eng, lambda e: e.tensor_copy(out, in_), reads, writes)

    def recip(self, out, in_, reads, writes):
        self.P.emit("dve", lambda e: e.reciprocal(out, in_), reads, writes)

    def memset(self, eng, ap, val, writes):
        self.P.emit(eng, lambda e: e.memset(ap, val), (), writes)

    def dma(self, out, in_, reads, writes, key=None, store=False):
        if key is None:
            key = id(reads[0]) if store else id(writes[0])
        return self.P.emit("sp", lambda e: e.dma_start(out=out, in_=in_), reads, writes, dma=True, key=key)


class Rot:
    def __init__(self, items):
        self.items = items
        self.i = 0

    def next(self):
        it = self.items[self.i % len(self.items)]
        self.i += 1
        return it


def build(S, BT=256, dbg=False):
    NT = S // 128
    NB1 = S // BT
    TPB = BT // 128
    CPB = BT // 64
    NB2 = S // 512
    nc = bass.Bass("TRN2", target_bir_lowering=False)

    def din(name, shape, dt):
        return nc.dram_tensor(name, shape, dt, kind="ExternalInput").ap()

    def dscr(name, shape, dt):
        return nc.dram_tensor(name, shape, dt, kind="Internal").ap()

    x_d = din("x", [S, D], F32)
    pos_d = din("pos", [1, S], I32)
    win_d = din("w_in_l", [128, 8, DIN], F32)
    lng_d = din("ln_g_l", [128, 8], F32)
    wq_d = din("w_q_l", [128, 2, 768], F32)
    qg_d = din("q_g_l", [128, 2], F32)
    wkv_d = din("w_kv_l", [128, 1024], F32)
    kvg_d = din("kv_g_l", [128, 1], F32)
    wout_d = din("w_out_l", [128, 8, 1024], F32)
    hgn_d = din("hgn_l", [128, 4], F32)
    hlb_d = din("hlb", [2, 512], F32)
    hlbfm_d = din("hlb_fm", [128, 2, 4], F32)
    fng_d = din("fng", [1, 1024], F32)
    ident_d = din("c_ident", [128, 128], BF16)
    tri_d = din("c_tri", [128, 128], BF16)
    lm_d = din("c_lm", [128, 128], BF16)
    lcols_d = din("c_lcols", [128, 4], BF16)
    us_d = din("c_us", [128, 128], BF16)
    lrep_d = din("c_lrep", [128, 512], F32)
    invf_d = din("c_invf", [128, 1], F32)
    out_d = nc.dram_tensor("out", [S, D], F32, kind="ExternalOutput").ap()

    Qd = dscr("Qd", [8, 96, S], BF16)
    Kd = dscr("Kd", [8, 96, S], BF16)
    Vd = dscr("Vd", [NT, 128, 640], BF16)
    Gd = dscr("Gd", [4, 128, S], BF16)
    Yd = dscr("Yd", [4, 128, S], BF16)
    Rd = dscr("Rd", [4, 512], F32)
    Cd = dscr("Cd", [32, S], F32)
    Sd = dscr("Sd", [32, S], F32)
    bQ = [[Buf() for _ in range(NB1)] for _ in range(8)]
    bK = [[Buf() for _ in range(NB1)] for _ in range(8)]
    bV = [Buf() for _ in range(NT)]
    bG = [[Buf() for _ in range(NB1)] for _ in range(4)]
    bY = [[Buf() for _ in range(NB1)] for _ in range(4)]
    bC = [Buf() for _ in range(NB1)]
    bS = [Buf() for _ in range(NB1)]

    with ExitStack() as st:
        P = Prog()
        E = Em(P)

        def sb(name, shape, dt):
            return st.enter_context(nc.sbuf_tensor("a_" + name, shape, dt)), Buf(name)

        def ps(name, shape, dt):
            return st.enter_context(nc.psum_tensor("a_" + name, shape, dt)), Buf(name, True)

        winb, b_winb = sb("winb", [128, 8, DIN], BF16)
        b_wp = [Buf("winb_p%d" % j) for j in range((DIN + 511) // 512)]

        def wb(col0, M):
            return [b_wp[j] for j in range(col0 // 512, (col0 + M - 1) // 512 + 1)]
        wkrr, b_wkrr = sb("wkrr", [128, 8, 96], BF16)
        wqb, b_wqb = sb("wqb", [128, 2, 768], BF16)
        wqr, b_wqr = sb("wqr", [128, 2, 768], BF16)
        wkvb, b_wkvb = sb("wkvb", [128, 1024], BF16)
        ident, b_ident = sb("ident", [128, 128], BF16)
        onesf, b_onesf = sb("onesf", [128, 128], BF16)
        lm, b_lm = sb("lm", [128, 128], BF16)
        lcols, b_lcols = sb("lcols", [128, 4], BF16)
        us, b_us = sb("us", [128, 128], BF16)
        lrep, b_lrep = sb("lrep", [128, 512], F32)
        invf, b_invf = sb("invf", [128, 1], F32)
        lbrep, b_lbrep = sb("lbrep", [128, 512], F32)
        omlrep, b_omlrep = sb("omlrep", [128, 512], F32)
        omlfm, b_omlfm = sb("omlfm", [128, 4], F32)
        hgn, b_hgn = sb("hgn", [128, 4], F32)
        lng, b_lng = sb("lng", [128, 8], F32)
        qg, b_qg = sb("qg", [128, 2], F32)
        kvg, b_kvg = sb("kvg", [128, 1], F32)
        Sst = [sb("S%d" % h, [128, 128], F32) for h in range(4)]
        Sp = [[sb("Sp%d_%d" % (h, k), [128, 128], BF16) for k in range(2)] for h in range(4)]
        tmw = Rot([sb("tmw%d" % i, [128, 512], F32) for i in range(3)])
        tmp = Rot([sb("tmp%d" % i, [128, BT], F32) for i in range(6)])
        hl = Rot([sb("hl%d" % i, [128, BT], BF16) for i in range(6)])
        tmpY = Rot([sb("tmpY%d" % i, [128, BT], F32) for i in range(4)])
        tmpB = Rot([sb("tmpB%d" % i, [128, BT], F32) for i in range(2)])
        kr1, b_kr1 = sb("kr1", [96, BT], F32)
        kr2, b_kr2 = sb("kr2", [96, BT], F32)
        hlY = Rot([sb("hlY%d" % i, [128, BT], BF16) for i in range(2)])
        xs = [sb("xs%d" % i, [128, D], F32) for i in range(2)]
        hb = [sb("hb%d" % i, [128, D], BF16) for i in range(2)]
        ssq = Rot([sb("ssq%d" % i, [128, 1], F32) for i in range(2)])
        lnv = Rot([sb("lnv%d" % i, [128, 1], F32) for i in range(2)])
        rstd = Rot([sb("rstd%d" % i, [128, 1], F32) for i in range(2)])
        hT, b_hT = sb("hT", [128, 8, BT], BF16)
        cosb = [sb("cosb%d" % k, [96, BT], F32) for k in range(2)]
        sinb = [sb("sinb%d" % k, [96, BT], F32) for k in range(2)]
        def grouped(name, n, dt_, sets):
            out_t, out_v = [], []
            for k in range(sets):
                t_, b_ = sb("%s%d" % (name, k), [128, n, BT], dt_)
                out_t.append((t_, b_))
                out_v.append([(t_[:, j, :], b_) for j in range(n)])
            return out_t, out_v
        qlf_t, qlf = grouped("qlf", 2, F32, 2)
        sqq_t, sqq = grouped("sqq", 2, F32, 2)
        kvf = [sb("kvf%d" % k, [128, BT], F32) for k in range(2)]
        sqk = [sb("sqk%d" % k, [128, BT], F32) for k in range(2)]
        kpe = [sb("kpe%d" % k, [96, BT], BF16) for k in range(2)]
        hfm_t, hfm = grouped("hfm", 4, F32, 2)
        hft = [[sb("hft%d_%d" % (k, i), [128, 512], F32) for i in range(TPB)] for k in range(2)]
        hqr_t, hqr = grouped("hqr", 4, F32, 2)
        gmr_t, gmr = grouped("gmr", 4, BF16, 2)
        ghr_t, ghr = grouped("ghr", 4, BF16, 2)
        vtm = [[sb("vtm%d_%d" % (k, i), [128, 512], BF16) for i in range(TPB)] for k in range(3)]
        qn, b_qn = sb("qn", [128, 2, BT], BF16)
        kvn, b_kvn = sb("kvn", [128, BT], BF16)
        QTo = Rot([sb("QTo%d" % i, [96, BT], BF16) for i in range(2)])
        KTo = Rot([sb("KTo%d" % i, [96, BT], BF16) for i in range(2)])
        Vb = Rot([sb("Vb%d" % i, [128, 4, 160], BF16) for i in range(2)])
        go = Rot([sb("go%d" % i, [128, 4, BT], BF16) for i in range(2)])
        gtm = [sb("gtm%d" % i, [128, 512], F32) for i in range(TPB)]
        ghi = [sb("ghi%d" % i, [128, 512], BF16) for i in range(TPB)]
        glo = [sb("glo%d" % i, [128, 512], BF16) for i in range(TPB)]
        sq_t, sq_v = grouped("sq_", 4, BF16, 1)
        sq_ = sq_v[0]
        ktl = [sb("ktl%d" % h, [128, BT], BF16) for h in range(4)]
        khat = [[sb("khat%d_%d" % (k, i), [128, 512], BF16) for i in range(TPB)] for k in range(2)]
        ghs_t, ghs = grouped("ghs", 4, BF16, 2)
        qtl = [[sb("qtl%d_%d" % (k, h), [128, BT], BF16) for h in range(4)] for k in range(2)]
        Abf = [[sb("Abf%d_%d" % (k, h), [128, BT], BF16) for h in range(4)] for k in range(2)]
        eC = [[sb("eC%d_%d" % (k, h), [128, 4 * TPB], F32) for h in range(4)] for k in range(2)]
        oT = [sb("oT%d" % h, [128, BT], F32) for h in range(4)]
        yo = Rot([sb("yo%d" % i, [128, BT], BF16) for i in range(2)])
        ri = Rot([sb("ri%d" % i, [128, 512], I32) for i in range(1)])
        stg = Rot(list(tmw.items) + [hft[k][i] for k in range(2) for i in range(TPB)])
        pA = Rot([ps("pA%d" % i, [128, 512], F32) for i in range(3)])
        pB = Rot([ps("pB%d" % i, [128, 512], F32) for i in range(2)])
        pU, _ = ps("pU", [128, 512], F32)
        b_pU = [Buf("pU", True)] * 4
        pO, _ = ps("pO", [128, 512], F32)
        b_pO = [Buf("pO", True)] * 4
        pI, _ = ps("pI", [128, 512], F32)
        b_pI = [Buf("pI", True)] * 4

        for (t, b), d in [((ident, b_ident), ident_d), ((lm, b_lm), lm_d), ((lcols, b_lcols), lcols_d),
                          ((us, b_us), us_d), ((lrep, b_lrep), lrep_d), ((invf, b_invf), invf_d),
                          ((hgn, b_hgn), hgn_d), ((lng, b_lng), lng_d), ((qg, b_qg), qg_d), ((kvg, b_kvg), kvg_d)]:
            E.dma(t[:], d, [], [b])
        E.memset("pool", onesf[:], 1.0, [b_onesf])
        for h in range(4):
            E.memset("pool", Sst[h][0][:], 0.0, [Sst[h][1]])
        for vb, b_vb in Vb.items:
            E.memset("pool", vb[:], 1.0, [b_vb])
        E.memset("pool", wkrr[:], 0.0, [b_wkrr])
        E.memset("pool", wqr[:], 0.0, [b_wqr])
        a0, b_a0 = tmw.next()
        a1, b_a1 = tmw.next()
        E.dma(a0[:], hlb_d[0:1, :].partition_broadcast(128), [], [b_a0])
        E.dma(a1[:], hlb_d[1:2, :].partition_broadcast(128), [], [b_a1])
        E.tt("dve", a0[:], a0[:], a1[:], ALU.subtract, [b_a0, b_a1], [b_a0])
        E.act(lbrep[:], a0[:], AF.Sigmoid, [b_a0], [b_lbrep])
        E.act(omlrep[:], a0[:], AF.Sigmoid, [b_a0], [b_omlrep], scale=-1.0)
        fm, b_fm = sb("lbfm", [128, 2, 4], F32)
        E.dma(fm[:], hlbfm_d, [], [b_fm])
        E.tt("dve", fm[:, 0, :], fm[:, 0, :], fm[:, 1, :], ALU.subtract, [b_fm], [b_fm])
        E.act(omlfm[:], fm[:, 0, :], AF.Sigmoid, [b_fm], [b_omlfm], scale=-1.0)
        ncast = [0]

        def cast_scaled(dst, src, col, reads, writes):
            E.ts("dve", dst, src, col, None, ALU.mult, None, reads, writes)
            ncast[0] += 1

        def setup_weights():
            for j0 in range(0, DIN, 512):
                w = min(512, DIN - j0)
                for c in range(8):
                    sg_, b_sg = stg.next()
                    E.dma(sg_[:, 0:w], win_d[:, c, j0:j0 + w], [], [b_sg])
                    cast_scaled(winb[:, c, j0:j0 + w], sg_[:, 0:w], lng[:, c:c + 1], [b_sg, b_lng], [b_wp[j0 // 512]])
                emit_rope(NRP)
                if j0 == 0:
                    for c in range(8):
                        E.ts("pool", wkrr[:, c, 64:80], winb[:, c, KR + 16:KR + 32], -1.0, None, ALU.mult, None, [b_wp[0]], [b_wkrr])
                        E.cp("pool", wkrr[:, c, 80:96], winb[:, c, KR:KR + 16], [b_wp[0]], [b_wkrr])
            for c in range(2):
                for j0 in range(0, 768, 512):
                    w = min(512, 768 - j0)
                    sg_, b_sg = stg.next()
                    E.dma(sg_[:, 0:w], wq_d[:, c, j0:j0 + w], [], [b_sg])
                    cast_scaled(wqb[:, c, j0:j0 + w], sg_[:, 0:w], qg[:, c:c + 1], [b_sg, b_qg], [b_wqb])
                v_in = wqb[:, c, :].rearrange("p (h f) -> p h f", h=8)
                v_out = wqr[:, c, :].rearrange("p (h f) -> p h f", h=8)
                E.ts("pool", v_out[:, :, 64:80], v_in[:, :, 80:96], -1.0, None, ALU.mult, None, [b_wqb], [b_wqr])
                E.cp("pool", v_out[:, :, 80:96], v_in[:, :, 64:80], [b_wqb], [b_wqr])
            for j0 in range(0, 1024, 512):
                sg_, b_sg = stg.next()
                E.dma(sg_[:], wkv_d[:, j0:j0 + 512], [], [b_sg])
                cast_scaled(wkvb[:, j0:j0 + 512], sg_[:], kvg[:, 0:1], [b_sg, b_kvg], [b_wkvb])
        R = slice(64, 96)
        rope_done = [0]

        NRP = (S // 512 + 3) // 4

        def emit_rope(upto):
            while rope_done[0] < min(upto, NRP):
                ps_ = rope_done[0]
                rope_done[0] += 1
                nb4 = min(4, S // 512 - 4 * ps_)
                NP = 32 * nb4
                it, b_it = ri.next()
                u, b_u = tmw.next()
                nf, b_nf = tmw.next()
                for q in range(nb4):
                    blk = 4 * ps_ + q
                    E.dma(it[32 * q:32 * q + 32, :], pos_d[0:1, blk * 512:(blk + 1) * 512].partition_broadcast(32), [], [b_it],
                          key=("ropeld", q))
                E.cp("dve", u[0:NP, :], it[0:NP, :], [b_it], [b_u])
                E.ts("dve", u[0:NP, :], u[0:NP, :], invf[0:NP, 0:1], None, ALU.mult, None, [b_u, b_invf], [b_u])
                for which, dst, bufs in ((0, Sd, bS), (1, Cd, bC)):
                    if which == 1:
                        E.ts("dve", u[0:NP, :], u[0:NP, :], 0.25, None, ALU.add, None, [b_u], [b_u])
                    E.cp("dve", it[0:NP, :], u[0:NP, :], [b_u], [b_it])
                    E.cp("dve", nf[0:NP, :], it[0:NP, :], [b_it], [b_nf])
                    E.tt("dve", nf[0:NP, :], u[0:NP, :], nf[0:NP, :], ALU.subtract, [b_u, b_nf], [b_nf])
                    E.act(nf[0:NP, :], nf[0:NP, :], AF.Sin, [b_nf], [b_nf], scale=2.0 * math.pi)
                    for q in range(nb4):
                        blk = 4 * ps_ + q
                        nbk = 512 // BT
                        E.dma(dst[:, blk * 512:(blk + 1) * 512], nf[32 * q:32 * q + 32, :], [b_nf],
                              [bufs[blk * nbk + k] for k in range(nbk)], key=("ropest", id(b_nf), q))

        emit_rope(1)
        setup_weights()

        MODE = os.environ.get("MK_MODE", "all")

        def stage_pair(specs):
            pt, b_pt = pA.next()
            for g, (col0, M, wt, b_wt) in enumerate(specs):
                for c in range(8):
                    lhs = winb[:, c, col0:col0 + M] if wt is None else wt[:, c, 0:M]
                    E.mm(pt[0:M, g * BT:(g + 1) * BT], lhs, hT[:, c, :], c == 0, c == 7,
                         (wb(col0, M) if wt is None else [b_wt]) + [b_hT], [b_pt])
            return pt, b_pt

        def stage_tm(i, col0):
            pt, b_pt = pA.next()
            for c in range(8):
                E.mm(pt[:, 0:512], hT[:, c, i * 128:(i + 1) * 128], winb[:, c, col0:col0 + 512], c == 0, c == 7,
                     wb(col0, 512) + [b_hT], [b_pt])
            return pt, b_pt

        def rep_rstd(sq_list, ndim, tpool, hpool, pbank, extra_bias=0.0):
            pt, b_pt = pbank
            parts = []
            for (sqt, b_sq) in sq_list:
                hi_, b_hi = hpool.next()
                lo_, b_lo = hpool.next()
                E.cp("act", hi_[:], sqt[:], [b_sq], [b_hi])
                E.tt("dve", lo_[:], sqt[:], hi_[:], ALU.subtract, [b_sq, b_hi], [b_lo])
                parts += [(hi_, b_hi), (lo_, b_lo)]
            for k, (sqt, b_sq) in enumerate(parts):
                E.mm(pt[:, 0:BT], onesf[:], sqt[:], k == 0, k == len(parts) - 1, [b_onesf, b_sq], [b_pt])
            l_, b_l = tpool.next()
            E.act(l_[:], pt[:, 0:BT], AF.Ln, [b_pt], [b_l], bias=EPS, scale=1.0 / ndim)
            r_, b_r = tpool.next()
            E.act(r_[:], l_[:], AF.Exp, [b_l], [b_r], scale=-0.5, bias=extra_bias)
            return r_, b_r

        def load_x(gt):
            t, b = xs[gt % 2]
            E.dma(t[:], x_d[gt * 128:(gt + 1) * 128, :], [], [b])

        load_x(0)
        stores = []

        def genX1(blk):
            k2 = blk % 2
            tok = slice(blk * BT, (blk + 1) * BT)
            E.dma(cosb[k2][0][R, :], Cd[:, tok], [bC[blk]], [cosb[k2][1]])
            E.dma(sinb[k2][0][R, :], Sd[:, tok], [bS[blk]], [sinb[k2][1]])
            for i in range(TPB):
                gt = blk * TPB + i
                if gt + 1 < NT:
                    load_x(gt + 1)
                xt, b_x = xs[gt % 2]
                ht, b_h = hb[gt % 2]
                sq1, b_sq1 = ssq.next()
                E.memset("pool", sq1[:], 0.0, [b_sq1])
                E.act(ht[:], xt[:], AF.Square, [b_x], [b_h, b_sq1], accum=sq1[:, 0:1])
                l1, b_l1 = lnv.next()
                E.act(l1[:], sq1[:], AF.Ln, [b_sq1], [b_l1], bias=EPS, scale=1.0 / D)
                r1, b_r1 = rstd.next()
                E.act(r1[:], l1[:], AF.Exp, [b_l1], [b_r1], scale=-0.5)
                yield
                E.act(ht[:], xt[:], AF.Copy, [b_x, b_r1], [b_h], scale=r1[:, 0:1])
                pt_, b_pT = pA.next()
                pT = pt_[:].bitcast(BF16)
                for c in range(8):
                    E.tr(pT[:, c * 128:(c + 1) * 128], ht[:, c * 128:(c + 1) * 128], ident[:], [b_h, b_ident], [b_pT])
                E.cp("dve", hT[:, :, i * 128:(i + 1) * 128], pT.rearrange("p (c t) -> p c t", c=8), [b_pT], [b_hT])
                yield
            pt, b_pt = stage_pair([(QL, 128, None, None), (QL + 128, 128, None, None)])
            p3 = pt[:, 0:2 * BT].rearrange("p (g t) -> p g t", g=2)
            E.cp("dve", qlf_t[k2][0][:], p3, [b_pt], [qlf_t[k2][1]])
            E.act(sqq_t[k2][0][:], p3, AF.Square, [b_pt], [sqq_t[k2][1]])
            yield
            pt, b_pt = stage_pair([(KVL, 128, None, None)])
            E.cp("dve", kvf[k2][0][:], pt[:, 0:BT], [b_pt], [kvf[k2][1]])
            E.act(sqk[k2][0][:], pt[:, 0:BT], AF.Square, [b_pt], [sqk[k2][1]])
            yield
            pt, b_pt = stage_pair([(KR - 64, 96, None, None), (0, 96, wkrr, b_wkrr)])
            E.tt("dve", kr1[R, :], pt[R, 0:BT], cosb[k2][0][R, :], ALU.mult, [b_pt, cosb[k2][1]], [b_kr1])
            E.tt("dve", kr2[R, :], pt[R, BT:2 * BT], sinb[k2][0][R, :], ALU.mult, [b_pt, sinb[k2][1]], [b_kr2])
            E.tt("dve", kpe[k2][0][R, :], kr1[R, :], kr2[R, :], ALU.add, [b_kr1, b_kr2], [kpe[k2][1]])
            yield
            n = 0
            for (col, dst_t) in ((GM, gmr_t), (HQ, hqr_t), (HF, hfm_t)):
                for p_ in range(2):
                    pt, b_pt = stage_pair([(col + (2 * p_) * 128, 128, None, None), (col + (2 * p_ + 1) * 128, 128, None, None)])
                    E.cp("act" if n % 2 == 0 else "dve", dst_t[k2][0][:, 2 * p_:2 * p_ + 2, :],
                         pt[:, 0:2 * BT].rearrange("p (g t) -> p g t", g=2), [b_pt], [dst_t[k2][1]])
                    n += 1
                    yield
            for i in range(TPB):
                pt, b_pt = stage_tm(i, HF)
                E.cp("dve" if i % 2 == 0 else "act", hft[k2][i][0][:], pt[:, 0:512], [b_pt], [hft[k2][i][1]])
                yield
            for i in range(TPB):
                pt, b_pt = stage_tm(i, HI)
                E.cp("act" if i % 2 == 0 else "dve", vtm[blk % 3][i][0][:], pt[:, 0:512], [b_pt], [vtm[blk % 3][i][1]])
                yield
            for p_ in range(2):
                pt, b_pt = stage_pair([(GH + (2 * p_) * 128, 128, None, None), (GH + (2 * p_ + 1) * 128, 128, None, None)])
                E.cp("act" if p_ % 2 == 0 else "dve", ghr_t[k2][0][:, 2 * p_:2 * p_ + 2, :],
                     pt[:, 0:2 * BT].rearrange("p (g t) -> p g t", g=2), [b_pt], [ghr_t[k2][1]])
                yield

        def genX2(blk):
            k2 = blk % 2
            tok = slice(blk * BT, (blk + 1) * BT)
            cb, b_cb = cosb[k2]
            sn, b_sn = sinb[k2]
            for i in range(TPB):
                t, b_t = hft[k2][i]
                E.act(t[:], t[:], AF.Sigmoid, [b_t], [b_t])
            t, b_t = hfm_t[k2]
            E.act(t[:], t[:], AF.Sigmoid, [b_t], [b_t], scale=-1.0)
            yield
            g_, b_g = go.next()
            E.act(g_[:], gmr_t[k2][0][:], AF.Silu, [gmr_t[k2][1]], [b_g])
            stores.append(E.dma(Gd[:, :, tok].rearrange("j p t -> p j t"), g_[:], [b_g], [bG[0][blk]], store=True))
            E.act(sq_t[0][0][:], hqr_t[k2][0][:], AF.Silu, [hqr_t[k2][1]], [sq_t[0][1]])
            yield
            E.act(ghs_t[k2][0][:], ghr_t[k2][0][:], AF.Silu, [ghr_t[k2][1]], [ghs_t[k2][1]])
            yield "EL"
            def chainA():
                rq, b_rq = rep_rstd(sqq[k2], 256, tmp, hl, pB.next(), extra_bias=math.log(1.0 / math.sqrt(96.0)))
                for g in range(2):
                    E.tt("dve", qn[:, g, :], qlf[k2][g][0][:], rq[:], ALU.mult, [qlf[k2][g][1], b_rq], [b_qn])
                yield
                rk, b_rk = rep_rstd([sqk[k2]], 128, tmp, hl, pB.next())
                E.tt("dve", kvn[:], kvf[k2][0][:], rk[:], ALU.mult, [kvf[k2][1], b_rk], [b_kvn])
                yield
                for h in range(8):
                    pq, b_pq = pB.next()
                    for c in range(2):
                        E.mm(pq[0:96, 0:BT], wqb[:, c, h * 96:(h + 1) * 96], qn[:, c, :], c == 0, c == 1, [b_wqb, b_qn], [b_pq])
                    for c in range(2):
                        E.mm(pq[0:96, BT:2 * BT], wqr[:, c, h * 96:(h + 1) * 96], qn[:, c, :], c == 0, c == 1, [b_wqr, b_qn], [b_pq])
                    qo, b_qo = QTo.next()
                    E.cp("act", qo[0:64, :], pq[0:64, 0:BT], [b_pq], [b_qo])
                    t1, b_t1 = tmp.next()
                    E.tt("dve", t1[R, :], pq[R, 0:BT], cb[R, :], ALU.mult, [b_pq, b_cb], [b_t1])
                    t2, b_t2 = tmp.next()
                    E.tt("dve", t2[R, :], pq[R, BT:2 * BT], sn[R, :], ALU.mult, [b_pq, b_sn], [b_t2])
                    E.tt("pool", qo[R, :], t1[R, :], t2[R, :], ALU.add, [b_t1, b_t2], [b_qo])
                    stores.append(E.dma(Qd[h, :, tok], qo[:], [b_qo], [bQ[h][blk]], store=True))
                    yield
                    if h % 2 == 1:
                        pk, b_pk = pB.next()
                        for j, hh in enumerate((h - 1, h)):
                            E.mm(pk[0:64, j * BT:(j + 1) * BT], wkvb[:, hh * 128:hh * 128 + 64], kvn[:], True, True,
                                 [b_wkvb, b_kvn], [b_pk])
                        for j, hh in enumerate((h - 1, h)):
                            ko, b_ko = KTo.next()
                            E.cp("act", ko[0:64, :], pk[0:64, j * BT:(j + 1) * BT], [b_pk], [b_ko])
                            E.cp("pool", ko[R, :], kpe[k2][0][R, :], [kpe[k2][1]], [b_ko])
                            stores.append(E.dma(Kd[hh, :, tok], ko[:], [b_ko], [bK[hh][blk]], store=True))
                        yield
                wv = wkvb[:].rearrange("p (h f) -> p h f", h=8)[:, :, 64:128]
                for i in range(TPB):
                    gt = blk * TPB + i
                    pv, b_pv = pB.next()
                    E.mm(pv[:, 0:512].rearrange("p (h f) -> p h f", h=8), kvn[:, i * 128:(i + 1) * 128], wv, True, True,
                         [b_kvn, b_wkvb], [b_pv])
                    vb, b_vb = Vb.next()
                    pv4 = pv[:, 0:512].rearrange("p (j f) -> p j f", j=4)
                    E.cp("dve", vb[:, :, 0:64], pv4[:, :, 0:64], [b_pv], [b_vb])
                    E.cp("act", vb[:, :, 96:160], pv4[:, :, 64:128], [b_pv], [b_vb])
                    stores.append(E.dma(Vd[gt], vb[:].rearrange("p j f -> p (j f)"), [b_vb], [bV[gt]], store=True))
                    yield

            def chainB():
                for i in range(TPB):
                    s_, b_s = hft[k2][i]
                    E.tt("dve", s_[:], s_[:], omlrep[:], ALU.mult, [b_s, b_omlrep], [b_s])
                    E.tt("dve", s_[:], s_[:], lbrep[:], ALU.add, [b_s, b_lbrep], [b_s])
                    E.act(gtm[i][0][:], s_[:], AF.Ln, [b_s], [gtm[i][1]])
                    E.cp("act", ghi[i][0][:], gtm[i][0][:], [gtm[i][1]], [ghi[i][1]])
                    E.tt("dve", glo[i][0][:], gtm[i][0][:], ghi[i][0][:], ALU.subtract, [gtm[i][1], ghi[i][1]], [glo[i][1]])
                    yield
                for i in range(TPB):
                    s_, b_s = hft[k2][i]
                    pr_, b_pr = pB.next()
                    E.mm(pr_[:, 0:512], us[:], ghi[i][0][:], True, False, [b_us, ghi[i][1]], [b_pr])
                    E.mm(pr_[:, 0:512], us[:], glo[i][0][:], False, True, [b_us, glo[i][1]], [b_pr])
                    er, b_er = tmw.next()
                    E.act(er[:], pr_[:, 0:512], AF.Exp, [b_pr], [b_er])
                    E.stt("dve", khat[k2][i][0][:], s_[:], 1.0, er[:], ALU.subtract, ALU.mult, [b_s, b_er], [khat[k2][i][1]])
                    yield
                for h in range(4):
                    hs = slice(h * 128, (h + 1) * 128)
                    pd, b_pd = pB.next()
                    for i in range(TPB):
                        E.mm(pd[:, i * 128:(i + 1) * 128], ghi[i][0][:, hs], lm[:], True, False, [ghi[i][1], b_lm], [b_pd])
                        E.mm(pd[:, i * 128:(i + 1) * 128], glo[i][0][:, hs], lm[:], False, True, [glo[i][1], b_lm], [b_pd])
                    for i in range(TPB):
                        E.mm(pd[:, BT + i * 4:BT + (i + 1) * 4], ghi[i][0][:, hs], lcols[:], True, False, [ghi[i][1], b_lcols], [b_pd])
                        E.mm(pd[:, BT + i * 4:BT + (i + 1) * 4], glo[i][0][:, hs], lcols[:], False, True, [glo[i][1], b_lcols], [b_pd])
                    ed, b_ed = tmpB.next()
                    edn, b_edn = tmpB.next()
                    E.act(ed[:], pd[:, 0:BT], AF.Exp, [b_pd], [b_ed])
                    E.act(edn[:], pd[:, 0:BT], AF.Exp, [b_pd], [b_edn], scale=-1.0)
                    E.act(eC[k2][h][0][:], pd[:, BT:BT + 4 * TPB], AF.Exp, [b_pd], [eC[k2][h][1]])
                    E.tt("dve", qtl[k2][h][0][:], sq_[h][0][:], ed[:], ALU.mult, [sq_[h][1], b_ed], [qtl[k2][h][1]])
                    E.stt("dve", ktl[h][0][:], hfm[k2][h][0][:], omlfm[:, h:h + 1], edn[:], ALU.mult, ALU.mult,
                          [hfm[k2][h][1], b_omlfm, b_edn], [ktl[h][1]])
                    yield
                    pa, b_pa = pB.next()
                    for i in range(TPB):
                        ts_ = slice(i * 128, (i + 1) * 128)
                        E.mm(pa[:, ts_], ktl[h][0][:, ts_], qtl[k2][h][0][:, ts_], True, True, [ktl[h][1], qtl[k2][h][1]], [b_pa])
                    E.tt("dve", Abf[k2][h][0][:], pa[:, 0:BT], lrep[:, 0:BT], ALU.mult, [b_pa, b_lrep], [Abf[k2][h][1]])
                    yield

            ga, gb = chainA(), chainB()
            while ga is not None or gb is not None:
                if gb is not None:
                    try:
                        next(gb)
                        yield
                    except StopIteration:
                        gb = None
                if ga is not None:
                    try:
                        next(ga)
                        yield
                    except StopIteration:
                        ga = None

        def genY(blk):
            k2 = blk % 2
            k3 = blk % 3
            tok = slice(blk * BT, (blk + 1) * BT)
            for i in range(TPB):
                ts_ = slice(i * 128, (i + 1) * 128)
                for h in range(4):
                    hs = slice(h * 128, (h + 1) * 128)
                    E.mm(pO[:, hs], vtm[k3][i][0][:, hs], Abf[k2][h][0][:, ts_], True, True, [vtm[k3][i][1], Abf[k2][h][1]],
                         [b_pO[h]])
                for half in range(2):
                    rows = slice(half * 64, half * 64 + 64)
                    c = 2 * i + half
                    cc = slice(c * 64, c * 64 + 64)
                    mid = i * 4 + half * 2
                    for h in range(4):
                        S_, b_S = Sst[h]
                        sp_, b_sp = Sp[h][c % 2]
                        E.ts("dve", sp_[:], S_[:], eC[k2][h][0][:, mid:mid + 1], None, ALU.mult, None, [b_S, eC[k2][h][1]], [b_sp])
                    for h in range(4):
                        hs = slice(h * 128, (h + 1) * 128)
                        sp_, b_sp = Sp[h][c % 2]
                        E.mm(pI[:, h * 128 + half * 64:h * 128 + half * 64 + 64], sp_[:], qtl[k2][h][0][:, cc], True, True,
                             [b_sp, qtl[k2][h][1]], [b_pI[h]])
                        E.mm(pU[:, hs], khat[k2][i][0][rows, hs], vtm[k3][i][0][rows, hs], True, True,
                             [khat[k2][i][1], vtm[k3][i][1]], [b_pU[h]])
                    yield
                    for h in range(4):
                        hs = slice(h * 128, (h + 1) * 128)
                        S_, b_S = Sst[h]
                        E.stt("dve", S_[:], S_[:], eC[k2][h][0][:, mid + 1:mid + 2], pU[:, hs], ALU.mult, ALU.subtract,
                              [b_S, eC[k2][h][1], b_pU[h]], [b_S])
                    yield
                for h in range(4):
                    hs = slice(h * 128, (h + 1) * 128)
                    E.cp("act", oT[h][0][:, ts_], pO[:, hs], [b_pO[h]], [oT[h][1]])
                    E.tt("dve", oT[h][0][:, ts_], pI[:, hs], oT[h][0][:, ts_], ALU.add, [b_pI[h], oT[h][1]], [oT[h][1]])
                yield
            yield "EPI"
            for h in range(4):
                s_, b_s = tmpY.next()
                E.act(s_[:], oT[h][0][:], AF.Square, [oT[h][1]], [b_s])
                ro, b_ro = rep_rstd([(s_, b_s)], 128, tmpY, hlY, (pU, b_pU[0]))
                y1, b_y1 = tmpY.next()
                E.tt("dve", y1[:], oT[h][0][:], ro[:], ALU.mult, [oT[h][1], b_ro], [b_y1])
                y_, b_y = yo.next()
                E.stt("dve", y_[:], y1[:], hgn[:, h:h + 1], ghs[k2][h][0][:], ALU.mult, ALU.mult,
                      [b_y1, b_hgn, ghs[k2][h][1]], [b_y])
                stores.append(E.dma(Yd[h, :, tok], y_[:], [b_y], [bY[h][blk]], store=True))
                yield

        def step(g):
            try:
                return next(g), True
            except StopIteration:
                return None, False

        NIT = NB1 + 2 if MODE != "setup" else 0
        for t in range(NIT):
            g1 = genX1(t) if t < NB1 else None
            g2 = genX2(t - 1) if 0 <= t - 1 < NB1 else None
            g3 = genY(t - 2) if 0 <= t - 2 < NB1 else None
            x2_el = g2 is None
            y_hold = False
            rnd = 0
            while g1 is not None or g2 is not None or g3 is not None:
                rnd += 1
                if g3 is not None and not (y_hold and not x2_el):
                    tag, alive = step(g3)
                    if not alive:
                        g3 = None
                    elif tag == "EPI":
                        y_hold = True
                for _ in range(2 if (rnd > 4 or g1 is None) else 0):
                    if g2 is not None:
                        tag, alive = step(g2)
                        if not alive:
                            g2 = None
                            x2_el = True
                        elif tag == "EL":
                            x2_el = True
                if g1 is not None:
                    tag, alive = step(g1)
                    if not alive:
                        g1 = None
        P.build(nc, st, "a")
        n1 = len(P.ops)

    if MODE in ("setup", "pass1"):
        nc._mk_stats = (n1, 0)
        return nc
    if MODE != "nobar":
        nc.all_engine_barrier()

    with ExitStack() as st:
        P = Prog()
        E = Em(P)

        def sb(name, shape, dt):
            return st.enter_context(nc.sbuf_tensor("b_" + name, shape, dt)), Buf(name)

        def ps(name, shape, dt):
            return st.enter_context(nc.psum_tensor("b_" + name, shape, dt)), Buf(name, True)

        KT = [sb("KT%d" % h, [96, S], BF16) for h in range(8)]
        Vst, _ = sb("Vst", [128, NT, 640], BF16)
        b_Vst = [Buf() for _ in range(NT)]
        woutb, b_woutb = sb("woutb", [128, 8, 1024], BF16)
        fng, b_fng = sb("fng", [128, 1024], F32)
        tri, b_tri = sb("tri", [128, 128], BF16)
        onesf, b_onesf = sb("onesf", [128, 128], BF16)
        stg = Rot([sb("stg%d" % i, [128, 512], F32) for i in range(2)])
        QT = [sb("QT%d" % h, [96, 512], BF16) for h in range(8)]
        gm_t = [sb("gm%d" % k, [128, 4, 512], BF16) for k in range(2)]
        yh_t = [sb("yh%d" % k, [128, 4, 512], BF16) for k in range(2)]
        gm = [[(gm_t[k][0][:, j, :], gm_t[k][1]) for j in range(4)] for k in range(2)]
        yh = [[(yh_t[k][0][:, j, :], yh_t[k][1]) for j in range(4)] for k in range(2)]
        ymla = [[sb("ymla%d_%d" % (k, j), [128, 512], BF16) for j in range(4)] for k in range(2)]
        PT = Rot([sb("PT%d" % i, [128, 512], BF16) for i in range(3)])
        rr = Rot([sb("rr%d" % i, [128, 512], F32) for i in range(4)])
        bc = Rot([sb("bc%d" % i, [128, 512], F32) for i in range(2)])
        yt = Rot([sb("yt%d" % i, [128, 512], F32) for i in range(2)])
        xr = Rot([sb("xr%d" % i, [128, D], F32) for i in range(2)])
        z = Rot([sb("z%d" % i, [128, D], F32) for i in range(2)])
        res = Rot([sb("res%d" % i, [128, D], F32) for i in range(2)])
        ssq = Rot([sb("ssq%d" % i, [128, 1], F32) for i in range(2)])
        lnv = Rot([sb("lnv%d" % i, [128, 1], F32) for i in range(2)])
        rstd = Rot([sb("rstd%d" % i, [128, 1], F32) for i in range(2)])
        pS = Rot([ps("pS%d" % i, [128, 512], F32) for i in range(3)])
        pO = Rot([ps("pO%d" % i, [128, 512], F32) for i in range(3)])
        pOut = Rot([ps("pOut%d" % i, [128, 512], F32) for i in range(2)])

        E.dma(tri[:], tri_d, [], [b_tri])
        E.dma(fng[:], fng_d.partition_broadcast(128), [], [b_fng])
        E.memset("pool", onesf[:], 1.0, [b_onesf])

        def load_q(qb, h):
            E.dma(QT[h][0][:], Qd[h, :, qb * 512:(qb + 1) * 512], [], [QT[h][1]])

        b_KT = [[Buf("KT%d_%d" % (h, q)) for q in range(NB2)] for h in range(8)]

        def load_kv(qb):
            cs = slice(qb * 512, (qb + 1) * 512)
            for h in range(8):
                E.dma(KT[h][0][:, cs], Kd[h, :, cs], [b_KT[h][qb - 1]] if qb > 0 else [], [b_KT[h][qb]], key=("kt", h))
                if h == 0:
                    g0 = 4 * qb
                    E.dma(Vst[:, g0:g0 + 4, :], Vd[g0:g0 + 4].rearrange("t p f -> p t f"),
                          [b_Vst[g0 - 1]] if qb > 0 else [], [b_Vst[t] for t in range(g0, g0 + 4)], key="vst")

        load_q(0, 0)
        load_kv(0)
        for h in range(1, 8):
            load_q(0, h)
        outs = []
        items = [(qb, h, kb) for qb in range(NB2) for h in range(8) for kb in range(4 * qb + 4)]
        NI = len(items)
        LOOK = 2
        FDLY = 14
        st_info = {}
        cur_po = {}
        pend = []
        seqc = [0]
        fin_left = {qb: 8 for qb in range(NB2)}
        b_Rd = [Buf("Rd%d" % i) for i in range(4)]

        def defer(due, fn, g=None):
            pend.append((due, seqc[0], fn, g))
            seqc[0] += 1

        def load_gates(qb):
            qs = slice(qb * 512, (qb + 1) * 512)
            E.dma(gm_t[qb % 2][0][:], Gd[:, :, qs].rearrange("j p t -> p j t"), [], [gm_t[qb % 2][1]])
            E.dma(yh_t[qb % 2][0][:], Yd[:, :, qs].rearrange("j p t -> p j t"), [], [yh_t[qb % 2][1]])

        def emit_st(i):
            qb, h, kb = items[i]
            if h == 0 and kb == 0:
                load_gates(qb)
                if qb + 1 < NB2:
                    load_kv(qb + 1)
            jd = kb - 4 * qb
            qoff = 128 * jd if jd > 0 else 0
            ncol = 512 - qoff
            ps_, b_ps = pS.next()
            E.mm(ps_[:, 0:ncol], KT[h][0][:, kb * 128:(kb + 1) * 128], QT[h][0][:, qoff:512], True, True,
                 [b_KT[h][kb // 4], QT[h][1]], [b_ps])
            st_info[i] = (ps_, b_ps, qoff, ncol, jd)
            if kb == 4 * qb + 3 and qb + 1 < NB2:
                load_q(qb + 1, h)

        def out_proj_chunks(qb, i_now):
            due = i_now + 20
            for i in range(4):
                gt = qb * 4 + i
                ts_ = slice(i * 128, (i + 1) * 128)
                hold = {}

                def c_load(gt=gt, hold=hold):
                    hold["x"] = xr.next()
                    hold["z"] = z.next()
                    E.dma(hold["x"][0][:], x_d[gt * 128:(gt + 1) * 128, :], [], [hold["x"][1]])
                defer(due, c_load)
                for n in range(2):
                    def c_mm(n=n, ts_=ts_, hold=hold, qb=qb):
                        ns = slice(n * 512, (n + 1) * 512)
                        pp, b_pp = pOut.next()
                        for ch in range(8):
                            src = ymla[qb % 2][ch] if ch < 4 else yh[qb % 2][ch - 4]
                            E.mm(pp[:, :], src[0][:, ts_], woutb[:, ch, ns], ch == 0, ch == 7, [src[1], b_woutb], [b_pp])
                        E.tt("dve", hold["z"][0][:, ns], pp[:, :], hold["x"][0][:, ns], ALU.add, [b_pp, hold["x"][1]],
                             [hold["z"][1]])
                    defer(due, c_mm)
                    due += 1

                def c_fin(gt=gt, hold=hold):
                    z_, b_z = hold["z"]
                    r_, b_r = res.next()
                    sq1, b_sq1 = ssq.next()
                    E.memset("pool", sq1[:], 0.0, [b_sq1])
                    E.act(r_[:], z_[:], AF.Square, [b_z], [b_r, b_sq1], accum=sq1[:, 0:1])
                    l1, b_l1 = lnv.next()
                    E.act(l1[:], sq1[:], AF.Ln, [b_sq1], [b_l1], bias=EPS, scale=1.0 / D)
                    r1, b_r1 = rstd.next()
                    E.act(r1[:], l1[:], AF.Exp, [b_l1], [b_r1], scale=-0.5)
                    E.stt("dve", r_[:], z_[:], r1[:, 0:1], fng[:], ALU.mult, ALU.mult, [b_z, b_r1, b_fng], [b_r])
                    outs.append(E.dma(out_d[gt * 128:(gt + 1) * 128, :], r_[:], [b_r], [Buf()], store=True))
                defer(due, c_fin)

        def emit_exp_pv(i):
            qb, h, kb = items[i]
            nkb = 4 * qb + 4
            pair, odd = h // 2, h % 2
            voff = 32 if odd else 0
            ps_, b_ps, qoff, ncol, jd = st_info.pop(i)
            if kb == 0:
                g = qb * 8 + h
                for p in [p for p in pend if p[3] is not None and p[3] <= g - 3]:
                    pend.remove(p)
                    p[2]()
                cur_po[(qb, h)] = pO.next()
            po, b_po = cur_po[(qb, h)]
            pt, b_pt = PT.next()
            E.act(pt[:, 0:ncol], ps_[:, 0:ncol], AF.Exp, [b_ps], [b_pt])
            if jd >= 0:
                E.tt("pool", pt[:, 0:128], pt[:, 0:128], tri[:], ALU.mult, [b_pt, b_tri], [b_pt])
            E.mm(po[:, qoff:512], Vst[:, kb, pair * 160 + voff:pair * 160 + voff + 128], pt[:, 0:ncol],
                 kb == 0, kb == nkb - 1, [b_Vst[kb], b_pt], [b_po])
            if kb != nkb - 1:
                return
            row = 32 if odd else 64
            rows = slice(64, 128) if odd else slice(0, 64)
            r_, b_r = rr.next()
            E.recip(r_[row:row + 1, :], po[row:row + 1, :], [b_po], [b_r])

            slot = (qb * 8 + h) % 4

            def fin1(row=row, r_=r_, b_r=b_r, slot=slot):
                E.dma(Rd[slot:slot + 1, :], r_[row:row + 1, :], [b_r], [b_Rd[slot]], store=True)
            defer(i + 7, fin1, qb * 8 + h)

            def fin2(qb=qb, h=h, pair=pair, row=row, rows=rows, po=po, b_po=b_po, slot=slot):
                bc_, b_bc = bc.next()
                E.dma(bc_[rows, :], Rd[slot:slot + 1, :].partition_broadcast(64), [b_Rd[slot]], [b_bc])
                y_, b_y = yt.next()
                E.tt("dve", y_[rows, :], po[rows, :], bc_[rows, :], ALU.mult, [b_po, b_bc], [b_y])
                E.tt("pool", ymla[qb % 2][pair][0][rows, :], y_[rows, :], gm[qb % 2][pair][0][rows, :], ALU.mult,
                     [b_y, gm[qb % 2][pair][1]], [ymla[qb % 2][pair][1]])
                fin_left[qb] -= 1
                if fin_left[qb] == 0:
                    out_proj_chunks(qb, cur_iter[0])
            defer(i + FDLY, fin2, qb * 8 + h)

        def wout_chunk(c, n):
            def f():
                sg_, b_sg = stg.next()
                E.dma(sg_[:], wout_d[:, c, n * 512:(n + 1) * 512], [], [b_sg])
                E.cp("dve", woutb[:, c, n * 512:(n + 1) * 512], sg_[:], [b_sg], [b_woutb])
            return f
        for c in range(8):
            for n in range(2):
                defer(3 + c * 2 + n, wout_chunk(c, n))

        cur_iter = [0]
        for it in range(NI + LOOK):
            cur_iter[0] = it
            if it < NI:
                emit_st(it)
            if it >= LOOK:
                emit_exp_pv(it - LOOK)
            ready = sorted([p for p in pend if p[0] <= it], key=lambda p: (p[0], p[1]))
            for p in ready[:2]:
                pend.remove(p)
                p[2]()
        cur_iter[0] = NI + LOOK + 10 ** 6
        while pend:
            p = sorted(pend, key=lambda p: (p[0], p[1]))[0]
            pend.remove(p)
            p[2]()
        P.build(nc, st, "b")
        n2 = len(P.ops)
    nc._mk_stats = (n1, n2)
    return nc


def _consts():
    s = np.arange(128)[:, None]
    t = np.arange(128)[None, :]
    same = (s // 64) == (t // 64)
    L = (same & (s <= t)).astype(np.float32)
    ref = (t // 64) * 64 + 31
    Lr = (same & (s <= ref)).astype(np.float32)
    lm = L - Lr
    us = (same & (s > t)).astype(np.float32)
    lcols = np.zeros((128, 4), np.float32)
    lcols[0:32, 0] = 1
    lcols[0:64, 1] = 1
    lcols[64:96, 2] = 1
    lcols[64:128, 3] = 1
    lrep = np.tile(L, (1, 4)).astype(np.float32)
    ident = np.eye(128, dtype=np.float32).astype(ml_dtypes.bfloat16)
    tri = (t >= s).astype(np.float32).astype(ml_dtypes.bfloat16)
    inv = (10000.0 ** (-np.arange(16, dtype=np.float32) / 16.0)).astype(np.float32)
    invf = np.tile(inv / (2 * np.pi), 8).reshape(128, 1).astype(np.float32)
    bf = ml_dtypes.bfloat16
    return dict(c_ident=ident, c_tri=tri, c_lm=lm.astype(bf), c_lcols=lcols.astype(bf), c_us=us.astype(bf), c_lrep=lrep, c_invf=invf)


def _layout_weights(ln_g, w_in, q_a_norm_g, w_q_b, kv_a_norm_g, w_kv_b, hg_lower_bounds, hg_norm_g, w_out, final_norm_g):
    f = lambda a: np.ascontiguousarray(a, dtype=np.float32)
    d = {}
    d["w_in_l"] = f(w_in[0].reshape(8, 128, DIN).transpose(1, 0, 2))
    d["ln_g_l"] = f(ln_g[0].reshape(8, 128).T)
    d["w_q_l"] = f(w_q_b[0].reshape(2, 128, 768).transpose(1, 0, 2))
    d["q_g_l"] = f(q_a_norm_g[0].reshape(2, 128).T)
    d["w_kv_l"] = f(w_kv_b[0])
    d["kv_g_l"] = f(kv_a_norm_g[0].reshape(128, 1))
    d["w_out_l"] = f(w_out[0].reshape(8, 128, 1024).transpose(1, 0, 2))
    d["hgn_l"] = f(hg_norm_g[0].reshape(4, 128).T)
    d["hlb"] = f(hg_lower_bounds)
    d["hlb_fm"] = f(hg_lower_bounds.reshape(2, 4, 128).transpose(2, 0, 1))
    d["fng"] = f(final_norm_g.reshape(1, 1024))
    d.update(_consts())
    return d


_NC_CACHE = {}


def run(x, positions, weights, S, BT=256):
    B = x.shape[0]
    key = (S, BT)
    if key not in _NC_CACHE:
        _NC_CACHE[key] = build(S, BT)
    nc = _NC_CACHE[key]
    shared = _layout_weights(**weights)
    in_maps = []
    for b in range(B):
        m = dict(shared)
        m["x"] = np.ascontiguousarray(x[b], dtype=np.float32)
        m["pos"] = np.ascontiguousarray(positions[b].reshape(1, S), dtype=np.int32)
        in_maps.append(m)
    r = run_bass_kernel_spmd(nc, in_maps, core_ids=list(range(B)))
    return np.stack([np.asarray(r.results[b]["out"]) for b in range(B)], axis=0).astype(np.float32)


def kernel(x, positions, ln_g, w_in, q_a_norm_g, w_q_b, kv_a_norm_g, w_kv_b, hg_lower_bounds, hg_norm_g, w_out,
           final_norm_g):
    x = np.asarray(x)
    weights = dict(ln_g=np.asarray(ln_g), w_in=np.asarray(w_in), q_a_norm_g=np.asarray(q_a_norm_g),
                   w_q_b=np.asarray(w_q_b), kv_a_norm_g=np.asarray(kv_a_norm_g), w_kv_b=np.asarray(w_kv_b),
                   hg_lower_bounds=np.asarray(hg_lower_bounds), hg_norm_g=np.asarray(hg_norm_g),
                   w_out=np.asarray(w_out), final_norm_g=np.asarray(final_norm_g))
    return run(x, np.asarray(positions), weights, x.shape[1])
```

```python
import math
import os
from contextlib import ExitStack

import numpy as np
import ml_dtypes
import concourse.bass as bass
import concourse.mybir as mybir
from concourse.bass_utils import run_bass_kernel_spmd

F32 = mybir.dt.float32
BF16 = mybir.dt.bfloat16
I32 = mybir.dt.int32
AF = mybir.ActivationFunctionType
ALU = mybir.AluOpType

SAME_ENGINE_SYNC = os.environ.get("MK_SES", "1") == "1"
NOSYNC_ENGS = set(os.environ.get("MK_NOSYNC", "").split(","))
EPS = 1e-6
D = 1024
DIN = 2976
QL, KVL, KR, GM, HQ, HF, HI, GH = 0, 256, 384, 416, 928, 1440, 1952, 2464


class Buf:
    __slots__ = ("name", "w", "r", "excl")

    def __init__(self, name="", excl=False):
        self.name = name
        self.w = None
        self.r = []
        self.excl = excl


class Op:
    __slots__ = ("eng", "fn", "deps", "dma", "key", "tok", "has_dep", "id")


class Prog:
    def __init__(self):
        self.ops = []

    def emit(self, eng, fn, reads=(), writes=(), dma=False, key=None):
        op = Op()
        op.eng, op.fn, op.dma, op.id, op.tok, op.has_dep = eng, fn, dma, len(self.ops), None, False
        deps = set()
        writes = list(writes) + [r for r in reads if r.excl]
        reads = [r for r in reads if not r.excl]
        for r in reads:
            if r.w is not None:
                deps.add(r.w)
        for w in writes:
            if w.w is not None:
                deps.add(w.w)
            deps.update(w.r)
        keep = set()
        for d in deps:
            dop = self.ops[d]
            if not dop.dma and dop.eng == eng and (eng == "pe" or not SAME_ENGINE_SYNC or eng in NOSYNC_ENGS):
                continue
            keep.add(d)
        op.deps = keep
        for d in keep:
            self.ops[d].has_dep = True
        for r in reads:
            r.r.append(op.id)
        for w in writes:
            w.w = op.id
            w.r = []
        if dma:
            op.key = key if key is not None else id((list(writes) + list(reads))[0])
        self.ops.append(op)
        return op

    def build(self, nc, st, tag):
        engs = ["pe", "act", "dve", "pool", "sp"]
        ops = self.ops
        last = {}
        for op in ops:
            if not op.dma:
                last[op.eng] = op
        finals = [o for o in ops if o.dma] + [o for e, o in last.items() if e != "sp"]
        for o in finals:
            o.has_dep = True
        esem = {e: st.enter_context(nc.semaphore("s%s_%s" % (tag, e))) for e in engs}
        dsem, dcnt, ecnt = {}, {}, {e: 0 for e in engs}
        for op in ops:
            if op.dma:
                if op.key not in dsem:
                    dsem[op.key] = st.enter_context(nc.semaphore("d%s_%d" % (tag, len(dsem))))
                    dcnt[op.key] = 0
                dcnt[op.key] += 16
                op.tok = (dsem[op.key], dcnt[op.key], 16)
            elif op.has_dep:
                ecnt[op.eng] += 1
                op.tok = (esem[op.eng], ecnt[op.eng], 1)
        self.n_sems = len(dsem) + 5
        block = st.enter_context(nc.Block())

        def run_engine(ename):
            def body(eng):
                waited = {}

                def wait(tok):
                    sem, val, _ = tok
                    if waited.get(sem.num, 0) < val:
                        eng.wait_ge(sem, val)
                        waited[sem.num] = val

                for op in ops:
                    if op.eng != ename:
                        continue
                    for d in sorted(op.deps):
                        wait(ops[d].tok)
                    inst = op.fn(eng)
                    if op.tok is not None:
                        inst.then_inc(op.tok[0], op.tok[2])
                if ename == "sp":
                    best = {}
                    for fo in finals:
                        if fo.tok[0].num not in best or best[fo.tok[0].num][1] < fo.tok[1]:
                            best[fo.tok[0].num] = fo.tok
                    for tk in best.values():
                        wait(tk)
            return body

        block.tensor(run_engine("pe"))
        block.scalar(run_engine("act"))
        block.vector(run_engine("dve"))
        block.gpsimd(run_engine("pool"))
        block.sync(run_engine("sp"))


class Em:
    def __init__(self, P):
        self.P = P

    def mm(self, out, lhsT, rhs, start, stop, reads, writes):
        self.P.emit("pe", lambda e: e.matmul(out, lhsT, rhs, start=start, stop=stop), reads, writes)

    def tr(self, out, in_, ident, reads, writes):
        self.P.emit("pe", lambda e: e.transpose(out, in_, ident), reads, writes)

    def act(self, out, in_, func, reads, writes, bias=0.0, scale=1.0, accum=None):
        if accum is None:
            self.P.emit("act", lambda e: e.activation(out, in_, func, bias=bias, scale=scale), reads, writes)
        else:
            self.P.emit("act", lambda e: e.activation(out, in_, func, bias=bias, scale=scale, accum_out=accum), reads, writes)

    def tt(self, eng, out, a, b, op, reads, writes):
        self.P.emit(eng, lambda e: e.tensor_tensor(out, a, b, op=op), reads, writes)

    def ts(self, eng, out, a, s1, s2, op0, op1, reads, writes):
        if s2 is None:
            self.P.emit(eng, lambda e: e.tensor_scalar(out, a, s1, None, op0=op0), reads, writes)
        else:
            self.P.emit(eng, lambda e: e.tensor_scalar(out, a, s1, s2, op0=op0, op1=op1), reads, writes)

    def stt(self, eng, out, in0, scalar, in1, op0, op1, reads, writes):
        self.P.emit(eng, lambda e: e.scalar_tensor_tensor(out, in0, scalar, in1, op0=op0, op1=op1), reads, writes)

    def cp(self, eng, out, in_, reads, writes):
        if eng == "act":
            self.P.emit("act", lambda e: e.activation(out, in_, AF.Copy), reads, writes)
        else:
            self.P.emit(eng, lambda e: e.tensor_copy(out, in_), reads, writes)

    def recip(self, out, in_, reads, writes):
        self.P.emit("dve", lambda e: e.reciprocal(out, in_), reads, writes)

    def memset(self, eng, ap, val, writes):
        self.P.emit(eng, lambda e: e.memset(ap, val), (), writes)

    def dma(self, out, in_, reads, writes, key=None, store=False):
        if key is None:
            key = id(reads[0]) if store else id(writes[0])
        return self.P.emit("sp", lambda e: e.dma_start(out=out, in_=in_), reads, writes, dma=True, key=key)


class Rot:
    def __init__(self, items):
        self.items = items
        self.i = 0

    def next(self):
        it = self.items[self.i % len(self.items)]
        self.i += 1
        return it


def build(S, BT=256, dbg=False):
    NT = S // 128
    NB1 = S // BT
    TPB = BT // 128
    CPB = BT // 64
    NB2 = S // 512
    nc = bass.Bass("TRN2", target_bir_lowering=False)

    def din(name, shape, dt):
        return nc.dram_tensor(name, shape, dt, kind="ExternalInput").ap()

    def dscr(name, shape, dt):
        return nc.dram_tensor(name, shape, dt, kind="Internal").ap()

    x_d = din("x", [S, D], F32)
    pos_d = din("pos", [1, S], I32)
    win_d = din("w_in_l", [128, 8, DIN], F32)
    lng_d = din("ln_g_l", [128, 8], F32)
    wq_d = din("w_q_l", [128, 2, 768], F32)
    qg_d = din("q_g_l", [128, 2], F32)
    wkv_d = din("w_kv_l", [128, 1024], F32)
    kvg_d = din("kv_g_l", [128, 1], F32)
    wout_d = din("w_out_l", [128, 8, 1024], F32)
    hgn_d = din("hgn_l", [128, 4], F32)
    hlb_d = din("hlb", [2, 512], F32)
    hlbfm_d = din("hlb_fm", [128, 2, 4], F32)
    fng_d = din("fng", [1, 1024], F32)
    ident_d = din("c_ident", [128, 128], BF16)
    tri_d = din("c_tri", [128, 128], BF16)
    lm_d = din("c_lm", [128, 128], BF16)
    lcols_d = din("c_lcols", [128, 4], BF16)
    us_d = din("c_us", [128, 128], BF16)
    lrep_d = din("c_lrep", [128, 512], F32)
    invf_d = din("c_invf", [128, 1], F32)
    out_d = nc.dram_tensor("out", [S, D], F32, kind="ExternalOutput").ap()

    Qd = dscr("Qd", [8, 96, S], BF16)
    Kd = dscr("Kd", [8, 96, S], BF16)
    Vd = dscr("Vd", [NT, 128, 640], BF16)
    Gd = dscr("Gd", [4, 128, S], BF16)
    Yd = dscr("Yd", [4, 128, S], BF16)
    Rd = dscr("Rd", [4, 512], F32)
    Cd = dscr("Cd", [32, S], F32)
    Sd = dscr("Sd", [32, S], F32)
    bQ = [[Buf() for _ in range(NB1)] for _ in range(8)]
    bK = [[Buf() for _ in range(NB1)] for _ in range(8)]
    bV = [Buf() for _ in range(NT)]
    bG = [[Buf() for _ in range(NB1)] for _ in range(4)]
    bY = [[Buf() for _ in range(NB1)] for _ in range(4)]
    bC = [Buf() for _ in range(NB1)]
    bS = [Buf() for _ in range(NB1)]

    with ExitStack() as st:
        P = Prog()
        E = Em(P)

        def sb(name, shape, dt):
            return st.enter_context(nc.sbuf_tensor("a_" + name, shape, dt)), Buf(name)

        def ps(name, shape, dt):
            return st.enter_context(nc.psum_tensor("a_" + name, shape, dt)), Buf(name, True)

        winb, b_winb = sb("winb", [128, 8, DIN], BF16)
        b_wp = [Buf("winb_p%d" % j) for j in range((DIN + 511) // 512)]

        def wb(col0, M):
            return [b_wp[j] for j in range(col0 // 512, (col0 + M - 1) // 512 + 1)]
        wkrr, b_wkrr = sb("wkrr", [128, 8, 96], BF16)
        wqb, b_wqb = sb("wqb", [128, 2, 768], BF16)
        wqr, b_wqr = sb("wqr", [128, 2, 768], BF16)
        wkvb, b_wkvb = sb("wkvb", [128, 1024], BF16)
        ident, b_ident = sb("ident", [128, 128], BF16)
        onesf, b_onesf = sb("onesf", [128, 128], BF16)
        lm, b_lm = sb("lm", [128, 128], BF16)
        lcols, b_lcols = sb("lcols", [128, 4], BF16)
        us, b_us = sb("us", [128, 128], BF16)
        lrep, b_lrep = sb("lrep", [128, 512], F32)
        invf, b_invf = sb("invf", [128, 1], F32)
        lbrep, b_lbrep = sb("lbrep", [128, 512], F32)
        omlrep, b_omlrep = sb("omlrep", [128, 512], F32)
        omlfm, b_omlfm = sb("omlfm", [128, 4], F32)
        hgn, b_hgn = sb("hgn", [128, 4], F32)
        lng, b_lng = sb("lng", [128, 8], F32)
        qg, b_qg = sb("qg", [128, 2], F32)
        kvg, b_kvg = sb("kvg", [128, 1], F32)
        Sst = [sb("S%d" % h, [128, 128], F32) for h in range(4)]
        Sp = [[sb("Sp%d_%d" % (h, k), [128, 128], BF16) for k in range(2)] for h in range(4)]
        tmw = Rot([sb("tmw%d" % i, [128, 512], F32) for i in range(3)])
        tmp = Rot([sb("tmp%d" % i, [128, BT], F32) for i in range(6)])
        hl = Rot([sb("hl%d" % i, [128, BT], BF16) for i in range(6)])
        tmpY = Rot([sb("tmpY%d" % i, [128, BT], F32) for i in range(4)])
        tmpB = Rot([sb("tmpB%d" % i, [128, BT], F32) for i in range(2)])
        kr1, b_kr1 = sb("kr1", [96, BT], F32)
        kr2, b_kr2 = sb("kr2", [96, BT], F32)
        hlY = Rot([sb("hlY%d" % i, [128, BT], BF16) for i in range(2)])
        xs = [sb("xs%d" % i, [128, D], F32) for i in range(2)]
        hb = [sb("hb%d" % i, [128, D], BF16) for i in range(2)]
        ssq = Rot([sb("ssq%d" % i, [128, 1], F32) for i in range(2)])
        lnv = Rot([sb("lnv%d" % i, [128, 1], F32) for i in range(2)])
        rstd = Rot([sb("rstd%d" % i, [128, 1], F32) for i in range(2)])
        hT, b_hT = sb("hT", [128, 8, BT], BF16)
        cosb = [sb("cosb%d" % k, [96, BT], F32) for k in range(2)]
        sinb = [sb("sinb%d" % k, [96, BT], F32) for k in range(2)]
        def grouped(name, n, dt_, sets):
            out_t, out_v = [], []
            for k in range(sets):
                t_, b_ = sb("%s%d" % (name, k), [128, n, BT], dt_)
                out_t.append((t_, b_))
                out_v.append([(t_[:, j, :], b_) for j in range(n)])
            return out_t, out_v
        qlf_t, qlf = grouped("qlf", 2, F32, 2)
        sqq_t, sqq = grouped("sqq", 2, F32, 2)
        kvf = [sb("kvf%d" % k, [128, BT], F32) for k in range(2)]
        sqk = [sb("sqk%d" % k, [128, BT], F32) for k in range(2)]
        kpe = [sb("kpe%d" % k, [96, BT], BF16) for k in range(2)]
        hfm_t, hfm = grouped("hfm", 4, F32, 2)
        hft = [[sb("hft%d_%d" % (k, i), [128, 512], F32) for i in range(TPB)] for k in range(2)]
        hqr_t, hqr = grouped("hqr", 4, F32, 2)
        gmr_t, gmr = grouped("gmr", 4, BF16, 2)
        ghr_t, ghr = grouped("ghr", 4, BF16, 2)
        vtm = [[sb("vtm%d_%d" % (k, i), [128, 512], BF16) for i in range(TPB)] for k in range(3)]
        qn, b_qn = sb("qn", [128, 2, BT], BF16)
        kvn, b_kvn = sb("kvn", [128, BT], BF16)
        QTo = Rot([sb("QTo%d" % i, [96, BT], BF16) for i in range(2)])
        KTo = Rot([sb("KTo%d" % i, [96, BT], BF16) for i in range(2)])
        Vb = Rot([sb("Vb%d" % i, [128, 4, 160], BF16) for i in range(2)])
        go = Rot([sb("go%d" % i, [128, 4, BT], BF16) for i in range(2)])
        gtm = [sb("gtm%d" % i, [128, 512], F32) for i in range(TPB)]
        ghi = [sb("ghi%d" % i, [128, 512], BF16) for i in range(TPB)]
        glo = [sb("glo%d" % i, [128, 512], BF16) for i in range(TPB)]
        sq_t, sq_v = grouped("sq_", 4, BF16, 1)
        sq_ = sq_v[0]
        ktl = [sb("ktl%d" % h, [128, BT], BF16) for h in range(4)]
        khat = [[sb("khat%d_%d" % (k, i), [128, 512], BF16) for i in range(TPB)] for k in range(2)]
        ghs_t, ghs = grouped("ghs", 4, BF16, 2)
        qtl = [[sb("qtl%d_%d" % (k, h), [128, BT], BF16) for h in range(4)] for k in range(2)]
        Abf = [[sb("Abf%d_%d" % (k, h), [128, BT], BF16) for h in range(4)] for k in range(2)]
        eC = [[sb("eC%d_%d" % (k, h), [128, 4 * TPB], F32) for h in range(4)] for k in range(2)]
        oT = [sb("oT%d" % h, [128, BT], F32) for h in range(4)]
        yo = Rot([sb("yo%d" % i, [128, BT], BF16) for i in range(2)])
        ri = Rot([sb("ri%d" % i, [128, 512], I32) for i in range(1)])
        stg = Rot(list(tmw.items) + [hft[k][i] for k in range(2) for i in range(TPB)])
        pA = Rot([ps("pA%d" % i, [128, 512], F32) for i in range(3)])
        pB = Rot([ps("pB%d" % i, [128, 512], F32) for i in range(2)])
        pU, _ = ps("pU", [128, 512], F32)
        b_pU = [Buf("pU", True)] * 4
        pO, _ = ps("pO", [128, 512], F32)
        b_pO = [Buf("pO", True)] * 4
        pI, _ = ps("pI", [128, 512], F32)
        b_pI = [Buf("pI", True)] * 4

        for (t, b), d in [((ident, b_ident), ident_d), ((lm, b_lm), lm_d), ((lcols, b_lcols), lcols_d),
                          ((us, b_us), us_d), ((lrep, b_lrep), lrep_d), ((invf, b_invf), invf_d),
                          ((hgn, b_hgn), hgn_d), ((lng, b_lng), lng_d), ((qg, b_qg), qg_d), ((kvg, b_kvg), kvg_d)]:
            E.dma(t[:], d, [], [b])
        E.memset("pool", onesf[:], 1.0, [b_onesf])
        for h in range(4):
            E.memset("pool", Sst[h][0][:], 0.0, [Sst[h][1]])
        for vb, b_vb in Vb.items:
            E.memset("pool", vb[:], 1.0, [b_vb])
        E.memset("pool", wkrr[:], 0.0, [b_wkrr])
        E.memset("pool", wqr[:], 0.0, [b_wqr])
        a0, b_a0 = tmw.next()
        a1, b_a1 = tmw.next()
        E.dma(a0[:], hlb_d[0:1, :].partition_broadcast(128), [], [b_a0])
        E.dma(a1[:], hlb_d[1:2, :].partition_broadcast(128), [], [b_a1])
        E.tt("dve", a0[:], a0[:], a1[:], ALU.subtract, [b_a0, b_a1], [b_a0])
        E.act(lbrep[:], a0[:], AF.Sigmoid, [b_a0], [b_lbrep])
        E.act(omlrep[:], a0[:], AF.Sigmoid, [b_a0], [b_omlrep], scale=-1.0)
        fm, b_fm = sb("lbfm", [128, 2, 4], F32)
        E.dma(fm[:], hlbfm_d, [], [b_fm])
        E.tt("dve", fm[:, 0, :], fm[:, 0, :], fm[:, 1, :], ALU.subtract, [b_fm], [b_fm])
        E.act(omlfm[:], fm[:, 0, :], AF.Sigmoid, [b_fm], [b_omlfm], scale=-1.0)
        ncast = [0]

        def cast_scaled(dst, src, col, reads, writes):
            E.ts("dve", dst, src, col, None, ALU.mult, None, reads, writes)
            ncast[0] += 1

        def setup_weights():
            for j0 in range(0, DIN, 512):
                w = min(512, DIN - j0)
                for c in range(8):
                    sg_, b_sg = stg.next()
                    E.dma(sg_[:, 0:w], win_d[:, c, j0:j0 + w], [], [b_sg])
                    cast_scaled(winb[:, c, j0:j0 + w], sg_[:, 0:w], lng[:, c:c + 1], [b_sg, b_lng], [b_wp[j0 // 512]])
                emit_rope(NRP)
                if j0 == 0:
                    for c in range(8):
                        E.ts("pool", wkrr[:, c, 64:80], winb[:, c, KR + 16:KR + 32], -1.0, None, ALU.mult, None, [b_wp[0]], [b_wkrr])
                        E.cp("pool", wkrr[:, c, 80:96], winb[:, c, KR:KR + 16], [b_wp[0]], [b_wkrr])
            for c in range(2):
                for j0 in range(0, 768, 512):
                    w = min(512, 768 - j0)
                    sg_, b_sg = stg.next()
                    E.dma(sg_[:, 0:w], wq_d[:, c, j0:j0 + w], [], [b_sg])
                    cast_scaled(wqb[:, c, j0:j0 + w], sg_[:, 0:w], qg[:, c:c + 1], [b_sg, b_qg], [b_wqb])
                v_in = wqb[:, c, :].rearrange("p (h f) -> p h f", h=8)
                v_out = wqr[:, c, :].rearrange("p (h f) -> p h f", h=8)
                E.ts("pool", v_out[:, :, 64:80], v_in[:, :, 80:96], -1.0, None, ALU.mult, None, [b_wqb], [b_wqr])
                E.cp("pool", v_out[:, :, 80:96], v_in[:, :, 64:80], [b_wqb], [b_wqr])
            for j0 in range(0, 1024, 512):
                sg_, b_sg = stg.next()
                E.dma(sg_[:], wkv_d[:, j0:j0 + 512], [], [b_sg])
                cast_scaled(wkvb[:, j0:j0 + 512], sg_[:], kvg[:, 0:1], [b_sg, b_kvg], [b_wkvb])
        R = slice(64, 96)
        rope_done = [0]

        NRP = (S // 512 + 3) // 4

        def emit_rope(upto):
            while rope_done[0] < min(upto, NRP):
                ps_ = rope_done[0]
                rope_done[0] += 1
                nb4 = min(4, S // 512 - 4 * ps_)
                NP = 32 * nb4
                it, b_it = ri.next()
                u, b_u = tmw.next()
                nf, b_nf = tmw.next()
                for q in range(nb4):
                    blk = 4 * ps_ + q
                    E.dma(it[32 * q:32 * q + 32, :], pos_d[0:1, blk * 512:(blk + 1) * 512].partition_broadcast(32), [], [b_it],
                          key=("ropeld", q))
                E.cp("dve", u[0:NP, :], it[0:NP, :], [b_it], [b_u])
                E.ts("dve", u[0:NP, :], u[0:NP, :], invf[0:NP, 0:1], None, ALU.mult, None, [b_u, b_invf], [b_u])
                for which, dst, bufs in ((0, Sd, bS), (1, Cd, bC)):
                    if which == 1:
                        E.ts("dve", u[0:NP, :], u[0:NP, :], 0.25, None, ALU.add, None, [b_u], [b_u])
                    E.cp("dve", it[0:NP, :], u[0:NP, :], [b_u], [b_it])
                    E.cp("dve", nf[0:NP, :], it[0:NP, :], [b_it], [b_nf])
                    E.tt("dve", nf[0:NP, :], u[0:NP, :], nf[0:NP, :], ALU.subtract, [b_u, b_nf], [b_nf])
                    E.act(nf[0:NP, :], nf[0:NP, :], AF.Sin, [b_nf], [b_nf], scale=2.0 * math.pi)
                    for q in range(nb4):
                        blk = 4 * ps_ + q
                        nbk = 512 // BT
                        E.dma(dst[:, blk * 512:(blk + 1) * 512], nf[32 * q:32 * q + 32, :], [b_nf],
                              [bufs[blk * nbk + k] for k in range(nbk)], key=("ropest", id(b_nf), q))

        emit_rope(1)
        setup_weights()

        MODE = os.environ.get("MK_MODE", "all")

        def stage_pair(specs):
            pt, b_pt = pA.next()
            for g, (col0, M, wt, b_wt) in enumerate(specs):
                for c in range(8):
                    lhs = winb[:, c, col0:col0 + M] if wt is None else wt[:, c, 0:M]
                    E.mm(pt[0:M, g * BT:(g + 1) * BT], lhs, hT[:, c, :], c == 0, c == 7,
                         (wb(col0, M) if wt is None else [b_wt]) + [b_hT], [b_pt])
            return pt, b_pt

        def stage_tm(i, col0):
            pt, b_pt = pA.next()
            for c in range(8):
                E.mm(pt[:, 0:512], hT[:, c, i * 128:(i + 1) * 128], winb[:, c, col0:col0 + 512], c == 0, c == 7,
                     wb(col0, 512) + [b_hT], [b_pt])
            return pt, b_pt

        def rep_rstd(sq_list, ndim, tpool, hpool, pbank, extra_bias=0.0):
            pt, b_pt = pbank
            parts = []
            for (sqt, b_sq) in sq_list:
                hi_, b_hi = hpool.next()
                lo_, b_lo = hpool.next()
                E.cp("act", hi_[:], sqt[:], [b_sq], [b_hi])
                E.tt("dve", lo_[:], sqt[:], hi_[:], ALU.subtract, [b_sq, b_hi], [b_lo])
                parts += [(hi_, b_hi), (lo_, b_lo)]
            for k, (sqt, b_sq) in enumerate(parts):
                E.mm(pt[:, 0:BT], onesf[:], sqt[:], k == 0, k == len(parts) - 1, [b_onesf, b_sq], [b_pt])
            l_, b_l = tpool.next()
            E.act(l_[:], pt[:, 0:BT], AF.Ln, [b_pt], [b_l], bias=EPS, scale=1.0 / ndim)
            r_, b_r = tpool.next()
            E.act(r_[:], l_[:], AF.Exp, [b_l], [b_r], scale=-0.5, bias=extra_bias)
            return r_, b_r

        def load_x(gt):
            t, b = xs[gt % 2]
            E.dma(t[:], x_d[gt * 128:(gt + 1) * 128, :], [], [b])

        load_x(0)
        stores = []

        def genX1(blk):
            k2 = blk % 2
            tok = slice(blk * BT, (blk + 1) * BT)
            E.dma(cosb[k2][0][R, :], Cd[:, tok], [bC[blk]], [cosb[k2][1]])
            E.dma(sinb[k2][0][R, :], Sd[:, tok], [bS[blk]], [sinb[k2][1]])
            for i in range(TPB):
                gt = blk * TPB + i
                if gt + 1 < NT:
                    load_x(gt + 1)
                xt, b_x = xs[gt % 2]
                ht, b_h = hb[gt % 2]
                sq1, b_sq1 = ssq.next()
                E.memset("pool", sq1[:], 0.0, [b_sq1])
                E.act(ht[:], xt[:], AF.Square, [b_x], [b_h, b_sq1], accum=sq1[:, 0:1])
                l1, b_l1 = lnv.next()
                E.act(l1[:], sq1[:], AF.Ln, [b_sq1], [b_l1], bias=EPS, scale=1.0 / D)
                r1, b_r1 = rstd.next()
                E.act(r1[:], l1[:], AF.Exp, [b_l1], [b_r1], scale=-0.5)
                yield
                E.act(ht[:], xt[:], AF.Copy, [b_x, b_r1], [b_h], scale=r1[:, 0:1])
                pt_, b_pT = pA.next()
                pT = pt_[:].bitcast(BF16)
                for c in range(8):
                    E.tr(pT[:, c * 128:(c + 1) * 128], ht[:, c * 128:(c + 1) * 128], ident[:], [b_h, b_ident], [b_pT])
                E.cp("dve", hT[:, :, i * 128:(i + 1) * 128], pT.rearrange("p (c t) -> p c t", c=8), [b_pT], [b_hT])
                yield
            pt, b_pt = stage_pair([(QL, 128, None, None), (QL + 128, 128, None, None)])
            p3 = pt[:, 0:2 * BT].rearrange("p (g t) -> p g t", g=2)
            E.cp("dve", qlf_t[k2][0][:], p3, [b_pt], [qlf_t[k2][1]])
            E.act(sqq_t[k2][0][:], p3, AF.Square, [b_pt], [sqq_t[k2][1]])
            yield
            pt, b_pt = stage_pair([(KVL, 128, None, None)])
            E.cp("dve", kvf[k2][0][:], pt[:, 0:BT], [b_pt], [kvf[k2][1]])
            E.act(sqk[k2][0][:], pt[:, 0:BT], AF.Square, [b_pt], [sqk[k2][1]])
            yield
            pt, b_pt = stage_pair([(KR - 64, 96, None, None), (0, 96, wkrr, b_wkrr)])
            E.tt("dve", kr1[R, :], pt[R, 0:BT], cosb[k2][0][R, :], ALU.mult, [b_pt, cosb[k2][1]], [b_kr1])
            E.tt("dve", kr2[R, :], pt[R, BT:2 * BT], sinb[k2][0][R, :], ALU.mult, [b_pt, sinb[k2][1]], [b_kr2])
            E.tt("dve", kpe[k2][0][R, :], kr1[R, :], kr2[R, :], ALU.add, [b_kr1, b_kr2], [kpe[k2][1]])
            yield
            n = 0
            for (col, dst_t) in ((GM, gmr_t), (HQ, hqr_t), (HF, hfm_t)):
                for p_ in range(2):
                    pt, b_pt = stage_pair([(col + (2 * p_) * 128, 128, None, None), (col + (2 * p_ + 1) * 128, 128, None, None)])
                    E.cp("act" if n % 2 == 0 else "dve", dst_t[k2][0][:, 2 * p_:2 * p_ + 2, :],
                         pt[:, 0:2 * BT].rearrange("p (g t) -> p g t", g=2), [b_pt], [dst_t[k2][1]])
                    n += 1
                    yield
            for i in range(TPB):
                pt, b_pt = stage_tm(i, HF)
                E.cp("dve" if i % 2 == 0 else "act", hft[k2][i][0][:], pt[:, 0:512], [b_pt], [hft[k2][i][1]])
                yield
            for i in range(TPB):
                pt, b_pt = stage_tm(i, HI)
                E.cp("act" if i % 2 == 0 else "dve", vtm[blk % 3][i][0][:], pt[:, 0:512], [b_pt], [vtm[blk % 3][i][1]])
                yield
            for p_ in range(2):
                pt, b_pt = stage_pair([(GH + (2 * p_) * 128, 128, None, None), (GH + (2 * p_ + 1) * 128, 128, None, None)])
                E.cp("act" if p_ % 2 == 0 else "dve", ghr_t[k2][0][:, 2 * p_:2 * p_ + 2, :],
                     pt[:, 0:2 * BT].rearrange("p (g t) -> p g t", g=2), [b_pt], [ghr_t[k2][1]])
                yield

        def genX2(blk):
            k2 = blk % 2
            tok = slice(blk * BT, (blk + 1) * BT)
            cb, b_cb = cosb[k2]
            sn, b_sn = sinb[k2]
            for i in range(TPB):
                t, b_t = hft[k2][i]
                E.act(t[:], t[:], AF.Sigmoid, [b_t], [b_t])
            t, b_t = hfm_t[k2]
            E.act(t[:], t[:], AF.Sigmoid, [b_t], [b_t], scale=-1.0)
            yield
            g_, b_g = go.next()
            E.act(g_[:], gmr_t[k2][0][:], AF.Silu, [gmr_t[k2][1]], [b_g])
            stores.append(E.dma(Gd[:, :, tok].rearrange("j p t -> p j t"), g_[:], [b_g], [bG[0][blk]], store=True))
            E.act(sq_t[0][0][:], hqr_t[k2][0][:], AF.Silu, [hqr_t[k2][1]], [sq_t[0][1]])
            yield
            E.act(ghs_t[k2][0][:], ghr_t[k2][0][:], AF.Silu, [ghr_t[k2][1]], [ghs_t[k2][1]])
            yield "EL"
            def chainA():
                rq, b_rq = rep_rstd(sqq[k2], 256, tmp, hl, pB.next(), extra_bias=math.log(1.0 / math.sqrt(96.0)))
                for g in range(2):
                    E.tt("dve", qn[:, g, :], qlf[k2][g][0][:], rq[:], ALU.mult, [qlf[k2][g][1], b_rq], [b_qn])
                yield
                rk, b_rk = rep_rstd([sqk[k2]], 128, tmp, hl, pB.next())
                E.tt("dve", kvn[:], kvf[k2][0][:], rk[:], ALU.mult, [kvf[k2][1], b_rk], [b_kvn])
                yield
                for h in range(8):
                    pq, b_pq = pB.next()
                    for c in range(2):
                        E.mm(pq[0:96, 0:BT], wqb[:, c, h * 96:(h + 1) * 96], qn[:, c, :], c == 0, c == 1, [b_wqb, b_qn], [b_pq])
                    for c in range(2):
                        E.mm(pq[0:96, BT:2 * BT], wqr[:, c, h * 96:(h + 1) * 96], qn[:, c, :], c == 0, c == 1, [b_wqr, b_qn], [b_pq])
                    qo, b_qo = QTo.next()
                    E.cp("act", qo[0:64, :], pq[0:64, 0:BT], [b_pq], [b_qo])
                    t1, b_t1 = tmp.next()
                    E.tt("dve", t1[R, :], pq[R, 0:BT], cb[R, :], ALU.mult, [b_pq, b_cb], [b_t1])
                    t2, b_t2 = tmp.next()
                    E.tt("dve", t2[R, :], pq[R, BT:2 * BT], sn[R, :], ALU.mult, [b_pq, b_sn], [b_t2])
                    E.tt("pool", qo[R, :], t1[R, :], t2[R, :], ALU.add, [b_t1, b_t2], [b_qo])
                    stores.append(E.dma(Qd[h, :, tok], qo[:], [b_qo], [bQ[h][blk]], store=True))
                    yield
                    if h % 2 == 1:
                        pk, b_pk = pB.next()
                        for j, hh in enumerate((h - 1, h)):
                            E.mm(pk[0:64, j * BT:(j + 1) * BT], wkvb[:, hh * 128:hh * 128 + 64], kvn[:], True, True,
                                 [b_wkvb, b_kvn], [b_pk])
                        for j, hh in enumerate((h - 1, h)):
                            ko, b_ko = KTo.next()
                            E.cp("act", ko[0:64, :], pk[0:64, j * BT:(j + 1) * BT], [b_pk], [b_ko])
                            E.cp("pool", ko[R, :], kpe[k2][0][R, :], [kpe[k2][1]], [b_ko])
                            stores.append(E.dma(Kd[hh, :, tok], ko[:], [b_ko], [bK[hh][blk]], store=True))
                        yield
                wv = wkvb[:].rearrange("p (h f) -> p h f", h=8)[:, :, 64:128]
                for i in range(TPB):
                    gt = blk * TPB + i
                    pv, b_pv = pB.next()
                    E.mm(pv[:, 0:512].rearrange("p (h f) -> p h f", h=8), kvn[:, i * 128:(i + 1) * 128], wv, True, True,
                         [b_kvn, b_wkvb], [b_pv])
                    vb, b_vb = Vb.next()
                    pv4 = pv[:, 0:512].rearrange("p (j f) -> p j f", j=4)
                    E.cp("dve", vb[:, :, 0:64], pv4[:, :, 0:64], [b_pv], [b_vb])
                    E.cp("act", vb[:, :, 96:160], pv4[:, :, 64:128], [b_pv], [b_vb])
                    stores.append(E.dma(Vd[gt], vb[:].rearrange("p j f -> p (j f)"), [b_vb], [bV[gt]], store=True))
                    yield

            def chainB():
                for i in range(TPB):
                    s_, b_s = hft[k2][i]
                    E.tt("dve", s_[:], s_[:], omlrep[:], ALU.mult, [b_s, b_omlrep], [b_s])
                    E.tt("dve", s_[:], s_[:], lbrep[:], ALU.add, [b_s, b_lbrep], [b_s])
                    E.act(gtm[i][0][:], s_[:], AF.Ln, [b_s], [gtm[i][1]])
                    E.cp("act", ghi[i][0][:], gtm[i][0][:], [gtm[i][1]], [ghi[i][1]])
                    E.tt("dve", glo[i][0][:], gtm[i][0][:], ghi[i][0][:], ALU.subtract, [gtm[i][1], ghi[i][1]], [glo[i][1]])
                    yield
                for i in range(TPB):
                    s_, b_s = hft[k2][i]
                    pr_, b_pr = pB.next()
                    E.mm(pr_[:, 0:512], us[:], ghi[i][0][:], True, False, [b_us, ghi[i][1]], [b_pr])
                    E.mm(pr_[:, 0:512], us[:], glo[i][0][:], False, True, [b_us, glo[i][1]], [b_pr])
                    er, b_er = tmw.next()
                    E.act(er[:], pr_[:, 0:512], AF.Exp, [b_pr], [b_er])
                    E.stt("dve", khat[k2][i][0][:], s_[:], 1.0, er[:], ALU.subtract, ALU.mult, [b_s, b_er], [khat[k2][i][1]])
                    yield
                for h in range(4):
                    hs = slice(h * 128, (h + 1) * 128)
                    pd, b_pd = pB.next()
                    for i in range(TPB):
                        E.mm(pd[:, i * 128:(i + 1) * 128], ghi[i][0][:, hs], lm[:], True, False, [ghi[i][1], b_lm], [b_pd])
                        E.mm(pd[:, i * 128:(i + 1) * 128], glo[i][0][:, hs], lm[:], False, True, [glo[i][1], b_lm], [b_pd])
                    for i in range(TPB):
                        E.mm(pd[:, BT + i * 4:BT + (i + 1) * 4], ghi[i][0][:, hs], lcols[:], True, False, [ghi[i][1], b_lcols], [b_pd])
                        E.mm(pd[:, BT + i * 4:BT + (i + 1) * 4], glo[i][0][:, hs], lcols[:], False, True, [glo[i][1], b_lcols], [b_pd])
                    ed, b_ed = tmpB.next()
                    edn, b_edn = tmpB.next()
                    E.act(ed[:], pd[:, 0:BT], AF.Exp, [b_pd], [b_ed])
                    E.act(edn[:], pd[:, 0:BT], AF.Exp, [b_pd], [b_edn], scale=-1.0)
                    E.act(eC[k2][h][0][:], pd[:, BT:BT + 4 * TPB], AF.Exp, [b_pd], [eC[k2][h][1]])
                    E.tt("dve", qtl[k2][h][0][:], sq_[h][0][:], ed[:], ALU.mult, [sq_[h][1], b_ed], [qtl[k2][h][1]])
                    E.stt("dve", ktl[h][0][:], hfm[k2][h][0][:], omlfm[:, h:h + 1], edn[:], ALU.mult, ALU.mult,
                          [hfm[k2][h][1], b_omlfm, b_edn], [ktl[h][1]])
                    yield
                    pa, b_pa = pB.next()
                    for i in range(TPB):
                        ts_ = slice(i * 128, (i + 1) * 128)
                        E.mm(pa[:, ts_], ktl[h][0][:, ts_], qtl[k2][h][0][:, ts_], True, True, [ktl[h][1], qtl[k2][h][1]], [b_pa])
                    E.tt("dve", Abf[k2][h][0][:], pa[:, 0:BT], lrep[:, 0:BT], ALU.mult, [b_pa, b_lrep], [Abf[k2][h][1]])
                    yield

            ga, gb = chainA(), chainB()
            while ga is not None or gb is not None:
                if gb is not None:
                    try:
                        next(gb)
                        yield
                    except StopIteration:
                        gb = None
                if ga is not None:
                    try:
                        next(ga)
                        yield
                    except StopIteration:
                        ga = None

        def genY(blk):
            k2 = blk % 2
            k3 = blk % 3
            tok = slice(blk * BT, (blk + 1) * BT)
            for i in range(TPB):
                ts_ = slice(i * 128, (i + 1) * 128)
                for h in range(4):
                    hs = slice(h * 128, (h + 1) * 128)
                    E.mm(pO[:, hs], vtm[k3][i][0][:, hs], Abf[k2][h][0][:, ts_], True, True, [vtm[k3][i][1], Abf[k2][h][1]],
                         [b_pO[h]])
                for half in range(2):
                    rows = slice(half * 64, half * 64 + 64)
                    c = 2 * i + half
                    cc = slice(c * 64, c * 64 + 64)
                    mid = i * 4 + half * 2
                    for h in range(4):
                        S_, b_S = Sst[h]
                        sp_, b_sp = Sp[h][c % 2]
                        E.ts("dve", sp_[:], S_[:], eC[k2][h][0][:, mid:mid + 1], None, ALU.mult, None, [b_S, eC[k2][h][1]], [b_sp])
                    for h in range(4):
                        hs = slice(h * 128, (h + 1) * 128)
                        sp_, b_sp = Sp[h][c % 2]
                        E.mm(pI[:, h * 128 + half * 64:h * 128 + half * 64 + 64], sp_[:], qtl[k2][h][0][:, cc], True, True,
                             [b_sp, qtl[k2][h][1]], [b_pI[h]])
                        E.mm(pU[:, hs], khat[k2][i][0][rows, hs], vtm[k3][i][0][rows, hs], True, True,
                             [khat[k2][i][1], vtm[k3][i][1]], [b_pU[h]])
                    yield
                    for h in range(4):
                        hs = slice(h * 128, (h + 1) * 128)
                        S_, b_S = Sst[h]
                        E.stt("dve", S_[:], S_[:], eC[k2][h][0][:, mid + 1:mid + 2], pU[:, hs], ALU.mult, ALU.subtract,
                              [b_S, eC[k2][h][1], b_pU[h]], [b_S])
                    yield
                for h in range(4):
                    hs = slice(h * 128, (h + 1) * 128)
                    E.cp("act", oT[h][0][:, ts_], pO[:, hs], [b_pO[h]], [oT[h][1]])
                    E.tt("dve", oT[h][0][:, ts_], pI[:, hs], oT[h][0][:, ts_], ALU.add, [b_pI[h], oT[h][1]], [oT[h][1]])
                yield
            yield "EPI"
            for h in range(4):
                s_, b_s = tmpY.next()
                E.act(s_[:], oT[h][0][:], AF.Square, [oT[h][1]], [b_s])
                ro, b_ro = rep_rstd([(s_, b_s)], 128, tmpY, hlY, (pU, b_pU[0]))
                y1, b_y1 = tmpY.next()
                E.tt("dve", y1[:], oT[h][0][:], ro[:], ALU.mult, [oT[h][1], b_ro], [b_y1])
                y_, b_y = yo.next()
                E.stt("dve", y_[:], y1[:], hgn[:, h:h + 1], ghs[k2][h][0][:], ALU.mult, ALU.mult,
                      [b_y1, b_hgn, ghs[k2][h][1]], [b_y])
                stores.append(E.dma(Yd[h, :, tok], y_[:], [b_y], [bY[h][blk]], store=True))
                yield

        def step(g):
            try:
                return next(g), True
            except StopIteration:
                return None, False

        NIT = NB1 + 2 if MODE != "setup" else 0
        for t in range(NIT):
            g1 = genX1(t) if t < NB1 else None
            g2 = genX2(t - 1) if 0 <= t - 1 < NB1 else None
            g3 = genY(t - 2) if 0 <= t - 2 < NB1 else None
            x2_el = g2 is None
            y_hold = False
            rnd = 0
            while g1 is not None or g2 is not None or g3 is not None:
                rnd += 1
                if g3 is not None and not (y_hold and not x2_el):
                    tag, alive = step(g3)
                    if not alive:
                        g3 = None
                    elif tag == "EPI":
                        y_hold = True
                for _ in range(2 if (rnd > 4 or g1 is None) else 0):
                    if g2 is not None:
                        tag, alive = step(g2)
                        if not alive:
                            g2 = None
                            x2_el = True
                        elif tag == "EL":
                            x2_el = True
                if g1 is not None:
                    tag, alive = step(g1)
                    if not alive:
                        g1 = None
        P.build(nc, st, "a")
        n1 = len(P.ops)

    if MODE in ("setup", "pass1"):
        nc._mk_stats = (n1, 0)
        return nc
    if MODE != "nobar":
        nc.all_engine_barrier()

    with ExitStack() as st:
        P = Prog()
        E = Em(P)

        def sb(name, shape, dt):
            return st.enter_context(nc.sbuf_tensor("b_" + name, shape, dt)), Buf(name)

        def ps(name, shape, dt):
            return st.enter_context(nc.psum_tensor("b_" + name, shape, dt)), Buf(name, True)

        KT = [sb("KT%d" % h, [96, S], BF16) for h in range(8)]
        Vst, _ = sb("Vst", [128, NT, 640], BF16)
        b_Vst = [Buf() for _ in range(NT)]
        woutb, b_woutb = sb("woutb", [128, 8, 1024], BF16)
        fng, b_fng = sb("fng", [128, 1024], F32)
        tri, b_tri = sb("tri", [128, 128], BF16)
        identb, b_identb = sb("identb", [128, 128], BF16)
        onesf, b_onesf = sb("onesf", [128, 128], BF16)
        stg = Rot([sb("stg%d" % i, [128, 512], F32) for i in range(2)])
        QT = [sb("QT%d" % h, [96, 512], BF16) for h in range(8)]
        gm_t = [sb("gm%d" % k, [128, 4, 512], BF16) for k in range(2)]
        yh_t = [sb("yh%d" % k, [128, 4, 512], BF16) for k in range(2)]
        gm = [[(gm_t[k][0][:, j, :], gm_t[k][1]) for j in range(4)] for k in range(2)]
        yh = [[(yh_t[k][0][:, j, :], yh_t[k][1]) for j in range(4)] for k in range(2)]
        ymla = [[sb("ymla%d_%d" % (k, j), [128, 512], BF16) for j in range(4)] for k in range(2)]
        PT = Rot([sb("PT%d" % i, [128, 512], BF16) for i in range(3)])
        rr = Rot([sb("rr%d" % i, [128, 512], F32) for i in range(4)])
        bc = Rot([sb("bc%d" % i, [128, 512], F32) for i in range(2)])
        yt = Rot([sb("yt%d" % i, [128, 512], F32) for i in range(2)])
        xr = Rot([sb("xr%d" % i, [128, D], F32) for i in range(2)])
        z = Rot([sb("z%d" % i, [128, D], F32) for i in range(2)])
        res = Rot([sb("res%d" % i, [128, D], F32) for i in range(2)])
        ssq = Rot([sb("ssq%d" % i, [128, 1], F32) for i in range(2)])
        lnv = Rot([sb("lnv%d" % i, [128, 1], F32) for i in range(2)])
        rstd = Rot([sb("rstd%d" % i, [128, 1], F32) for i in range(2)])
        pS = Rot([ps("pS%d" % i, [128, 512], F32) for i in range(3)])
        pO = Rot([ps("pO%d" % i, [128, 512], F32) for i in range(3)])
        pOut = Rot([ps("pOut%d" % i, [128, 512], F32) for i in range(2)])

        E.dma(tri[:], tri_d, [], [b_tri])
        E.dma(identb[:], ident_d, [], [b_identb])
        E.dma(fng[:], fng_d.partition_broadcast(128), [], [b_fng])
        E.memset("pool", onesf[:], 1.0, [b_onesf])

        def load_q(qb, h):
            E.dma(QT[h][0][:], Qd[h, :, qb * 512:(qb + 1) * 512], [], [QT[h][1]])

        b_KT = [[Buf("KT%d_%d" % (h, q)) for q in range(NB2)] for h in range(8)]

        def load_kv(qb):
            cs = slice(qb * 512, (qb + 1) * 512)
            for h in range(8):
                E.dma(KT[h][0][:, cs], Kd[h, :, cs], [b_KT[h][qb - 1]] if qb > 0 else [], [b_KT[h][qb]], key=("kt", h))
                if h == 0:
                    g0 = 4 * qb
                    E.dma(Vst[:, g0:g0 + 4, :], Vd[g0:g0 + 4].rearrange("t p f -> p t f"),
                          [b_Vst[g0 - 1]] if qb > 0 else [], [b_Vst[t] for t in range(g0, g0 + 4)], key="vst")

        load_q(0, 0)
        load_kv(0)
        for h in range(1, 8):
            load_q(0, h)
        outs = []
        items = [(qb, h, kb) for qb in range(NB2) for h in range(8) for kb in range(4 * qb + 4)]
        NI = len(items)
        LOOK = 2
        FDLY = 14
        st_info = {}
        cur_po = {}
        pend = []
        seqc = [0]
        fin_left = {qb: 8 for qb in range(NB2)}
        b_Rd = [Buf("Rd%d" % i) for i in range(4)]

        def defer(due, fn, g=None):
            pend.append((due, seqc[0], fn, g))
            seqc[0] += 1

        def load_gates(qb):
            qs = slice(qb * 512, (qb + 1) * 512)
            E.dma(gm_t[qb % 2][0][:], Gd[:, :, qs].rearrange("j p t -> p j t"), [], [gm_t[qb % 2][1]])
            E.dma(yh_t[qb % 2][0][:], Yd[:, :, qs].rearrange("j p t -> p j t"), [], [yh_t[qb % 2][1]])

        def emit_st(i):
            qb, h, kb = items[i]
            if h == 0 and kb == 0:
                load_gates(qb)
                if qb + 1 < NB2:
                    load_kv(qb + 1)
            jd = kb - 4 * qb
            qoff = 128 * jd if jd > 0 else 0
            ncol = 512 - qoff
            ps_, b_ps = pS.next()
            E.mm(ps_[:, 0:ncol], KT[h][0][:, kb * 128:(kb + 1) * 128], QT[h][0][:, qoff:512], True, jd < 0,
                 [b_KT[h][kb // 4], QT[h][1]], [b_ps])
            if jd >= 0:
                E.mm(ps_[:, 0:128], identb[:], tri[:], False, True, [b_identb, b_tri], [b_ps])
            st_info[i] = (ps_, b_ps, qoff, ncol, jd)
            if kb == 4 * qb + 3 and qb + 1 < NB2:
                load_q(qb + 1, h)

        def out_proj_chunks(qb, i_now):
            due = i_now + 20
            for i in range(4):
                gt = qb * 4 + i
                ts_ = slice(i * 128, (i + 1) * 128)
                hold = {}

                def c_load(gt=gt, hold=hold):
                    hold["x"] = xr.next()
                    hold["z"] = z.next()
                    E.dma(hold["x"][0][:], x_d[gt * 128:(gt + 1) * 128, :], [], [hold["x"][1]])
                defer(due, c_load)
                for n in range(2):
                    def c_mm(n=n, ts_=ts_, hold=hold, qb=qb):
                        ns = slice(n * 512, (n + 1) * 512)
                        pp, b_pp = pOut.next()
                        for ch in range(8):
                            src = ymla[qb % 2][ch] if ch < 4 else yh[qb % 2][ch - 4]
                            E.mm(pp[:, :], src[0][:, ts_], woutb[:, ch, ns], ch == 0, ch == 7, [src[1], b_woutb], [b_pp])
                        E.tt("dve", hold["z"][0][:, ns], pp[:, :], hold["x"][0][:, ns], ALU.add, [b_pp, hold["x"][1]],
                             [hold["z"][1]])
                    defer(due, c_mm)
                    due += 1

                def c_fin(gt=gt, hold=hold):
                    z_, b_z = hold["z"]
                    r_, b_r = res.next()
                    sq1, b_sq1 = ssq.next()
                    E.memset("pool", sq1[:], 0.0, [b_sq1])
                    E.act(r_[:], z_[:], AF.Square, [b_z], [b_r, b_sq1], accum=sq1[:, 0:1])
                    l1, b_l1 = lnv.next()
                    E.act(l1[:], sq1[:], AF.Ln, [b_sq1], [b_l1], bias=EPS, scale=1.0 / D)
                    r1, b_r1 = rstd.next()
                    E.act(r1[:], l1[:], AF.Exp, [b_l1], [b_r1], scale=-0.5)
                    E.stt("dve", r_[:], z_[:], r1[:, 0:1], fng[:], ALU.mult, ALU.mult, [b_z, b_r1, b_fng], [b_r])
                    outs.append(E.dma(out_d[gt * 128:(gt + 1) * 128, :], r_[:], [b_r], [Buf()], store=True))
                defer(due, c_fin)

        def emit_exp_pv(i):
            qb, h, kb = items[i]
            nkb = 4 * qb + 4
            pair, odd = h // 2, h % 2
            voff = 32 if odd else 0
            ps_, b_ps, qoff, ncol, jd = st_info.pop(i)
            if kb == 0:
                g = qb * 8 + h
                for p in [p for p in pend if p[3] is not None and p[3] <= g - 3]:
                    pend.remove(p)
                    p[2]()
                cur_po[(qb, h)] = pO.next()
            po, b_po = cur_po[(qb, h)]
            pt, b_pt = PT.next()
            E.act(pt[:, 0:ncol], ps_[:, 0:ncol], AF.Exp, [b_ps], [b_pt])
            E.mm(po[:, qoff:512], Vst[:, kb, pair * 160 + voff:pair * 160 + voff + 128], pt[:, 0:ncol],
                 kb == 0, kb == nkb - 1, [b_Vst[kb], b_pt], [b_po])
            if kb != nkb - 1:
                return
            row = 32 if odd else 64
            rows = slice(64, 128) if odd else slice(0, 64)
            r_, b_r = rr.next()
            E.recip(r_[row:row + 1, :], po[row:row + 1, :], [b_po], [b_r])

            slot = (qb * 8 + h) % 4

            def fin1(row=row, r_=r_, b_r=b_r, slot=slot):
                E.dma(Rd[slot:slot + 1, :], r_[row:row + 1, :], [b_r], [b_Rd[slot]], store=True)
            defer(i + 7, fin1, qb * 8 + h)

            def fin2(qb=qb, h=h, pair=pair, row=row, rows=rows, po=po, b_po=b_po, slot=slot):
                bc_, b_bc = bc.next()
                E.dma(bc_[rows, :], Rd[slot:slot + 1, :].partition_broadcast(64), [b_Rd[slot]], [b_bc])
                y_, b_y = yt.next()
                E.tt("dve", y_[rows, :], po[rows, :], bc_[rows, :], ALU.mult, [b_po, b_bc], [b_y])
                E.tt("pool", ymla[qb % 2][pair][0][rows, :], y_[rows, :], gm[qb % 2][pair][0][rows, :], ALU.mult,
                     [b_y, gm[qb % 2][pair][1]], [ymla[qb % 2][pair][1]])
                fin_left[qb] -= 1
                if fin_left[qb] == 0:
                    out_proj_chunks(qb, cur_iter[0])
            defer(i + FDLY, fin2, qb * 8 + h)

        def wout_chunk(c, n):
            def f():
                sg_, b_sg = stg.next()
                E.dma(sg_[:], wout_d[:, c, n * 512:(n + 1) * 512], [], [b_sg])
                E.cp("dve", woutb[:, c, n * 512:(n + 1) * 512], sg_[:], [b_sg], [b_woutb])
            return f
        for c in range(8):
            for n in range(2):
                defer(3 + c * 2 + n, wout_chunk(c, n))

        cur_iter = [0]
        for it in range(NI + LOOK):
            cur_iter[0] = it
            if it < NI:
                emit_st(it)
            if it >= LOOK:
                emit_exp_pv(it - LOOK)
            ready = sorted([p for p in pend if p[0] <= it], key=lambda p: (p[0], p[1]))
            for p in ready[:2]:
                pend.remove(p)
                p[2]()
        cur_iter[0] = NI + LOOK + 10 ** 6
        while pend:
            p = sorted(pend, key=lambda p: (p[0], p[1]))[0]
            pend.remove(p)
            p[2]()
        P.build(nc, st, "b")
        n2 = len(P.ops)
    nc._mk_stats = (n1, n2)
    return nc


def _consts():
    s = np.arange(128)[:, None]
    t = np.arange(128)[None, :]
    same = (s // 64) == (t // 64)
    L = (same & (s <= t)).astype(np.float32)
    ref = (t // 64) * 64 + 31
    Lr = (same & (s <= ref)).astype(np.float32)
    lm = L - Lr
    us = (same & (s > t)).astype(np.float32)
    lcols = np.zeros((128, 4), np.float32)
    lcols[0:32, 0] = 1
    lcols[0:64, 1] = 1
    lcols[64:96, 2] = 1
    lcols[64:128, 3] = 1
    lrep = np.tile(L, (1, 4)).astype(np.float32)
    ident = np.eye(128, dtype=np.float32).astype(ml_dtypes.bfloat16)
    tri = np.where(t >= s, 0.0, -30000.0).astype(np.float32).astype(ml_dtypes.bfloat16)
    inv = (10000.0 ** (-np.arange(16, dtype=np.float32) / 16.0)).astype(np.float32)
    invf = np.tile(inv / (2 * np.pi), 8).reshape(128, 1).astype(np.float32)
    bf = ml_dtypes.bfloat16
    return dict(c_ident=ident, c_tri=tri, c_lm=lm.astype(bf), c_lcols=lcols.astype(bf), c_us=us.astype(bf), c_lrep=lrep, c_invf=invf)


def _layout_weights(ln_g, w_in, q_a_norm_g, w_q_b, kv_a_norm_g, w_kv_b, hg_lower_bounds, hg_norm_g, w_out, final_norm_g):
    f = lambda a: np.ascontiguousarray(a, dtype=np.float32)
    d = {}
    d["w_in_l"] = f(w_in[0].reshape(8, 128, DIN).transpose(1, 0, 2))
    d["ln_g_l"] = f(ln_g[0].reshape(8, 128).T)
    d["w_q_l"] = f(w_q_b[0].reshape(2, 128, 768).transpose(1, 0, 2))
    d["q_g_l"] = f(q_a_norm_g[0].reshape(2, 128).T)
    d["w_kv_l"] = f(w_kv_b[0])
    d["kv_g_l"] = f(kv_a_norm_g[0].reshape(128, 1))
    d["w_out_l"] = f(w_out[0].reshape(8, 128, 1024).transpose(1, 0, 2))
    d["hgn_l"] = f(hg_norm_g[0].reshape(4, 128).T)
    d["hlb"] = f(hg_lower_bounds)
    d["hlb_fm"] = f(hg_lower_bounds.reshape(2, 4, 128).transpose(2, 0, 1))
    d["fng"] = f(final_norm_g.reshape(1, 1024))
    d.update(_consts())
    return d


_NC_CACHE = {}


def run(x, positions, weights, S, BT=256):
    B = x.shape[0]
    key = (S, BT)
    if key not in _NC_CACHE:
        _NC_CACHE[key] = build(S, BT)
    nc = _NC_CACHE[key]
    shared = _layout_weights(**weights)
    in_maps = []
    for b in range(B):
        m = dict(shared)
        m["x"] = np.ascontiguousarray(x[b], dtype=np.float32)
        m["pos"] = np.ascontiguousarray(positions[b].reshape(1, S), dtype=np.int32)
        in_maps.append(m)
    r = run_bass_kernel_spmd(nc, in_maps, core_ids=list(range(B)))
    return np.stack([np.asarray(r.results[b]["out"]) for b in range(B)], axis=0).astype(np.float32)


def kernel(x, positions, ln_g, w_in, q_a_norm_g, w_q_b, kv_a_norm_g, w_kv_b, hg_lower_bounds, hg_norm_g, w_out,
           final_norm_g):
    x = np.asarray(x)
    weights = dict(ln_g=np.asarray(ln_g), w_in=np.asarray(w_in), q_a_norm_g=np.asarray(q_a_norm_g),
                   w_q_b=np.asarray(w_q_b), kv_a_norm_g=np.asarray(kv_a_norm_g), w_kv_b=np.asarray(w_kv_b),
                   hg_lower_bounds=np.asarray(hg_lower_bounds), hg_norm_g=np.asarray(hg_norm_g),
                   w_out=np.asarray(w_out), final_norm_g=np.asarray(final_norm_g))
    return run(x, np.asarray(positions), weights, x.shape[1])
```

```python
import math
import os
from contextlib import ExitStack

import numpy as np
import ml_dtypes
import concourse.bass as bass
import concourse.mybir as mybir
from concourse.bass_utils import run_bass_kernel_spmd

F32 = mybir.dt.float32
BF16 = mybir.dt.bfloat16
I32 = mybir.dt.int32
AF = mybir.ActivationFunctionType
ALU = mybir.AluOpType

SAME_ENGINE_SYNC = os.environ.get("MK_SES", "1") == "1"
NOSYNC_ENGS = set(os.environ.get("MK_NOSYNC", "").split(","))
EPS = 1e-6
D = 1024
DIN = 2976
QL, KVL, KR, GM, HQ, HF, HI, GH = 0, 256, 384, 416, 928, 1440, 1952, 2464


class Buf:
    __slots__ = ("name", "w", "r", "excl")

    def __init__(self, name="", excl=False):
        self.name = name
        self.w = None
        self.r = []
        self.excl = excl


class Op:
    __slots__ = ("eng", "fn", "deps", "dma", "key", "tok", "has_dep", "id")


class Prog:
    def __init__(self):
        self.ops = []

    def emit(self, eng, fn, reads=(), writes=(), dma=False, key=None):
        op = Op()
        op.eng, op.fn, op.dma, op.id, op.tok, op.has_dep = eng, fn, dma, len(self.ops), None, False
        deps = set()
        writes = list(writes) + [r for r in reads if r.excl]
        reads = [r for r in reads if not r.excl]
        for r in reads:
            if r.w is not None:
                deps.add(r.w)
        for w in writes:
            if w.w is not None:
                deps.add(w.w)
            deps.update(w.r)
        keep = set()
        for d in deps:
            dop = self.ops[d]
            if not dop.dma and dop.eng == eng and (eng == "pe" or not SAME_ENGINE_SYNC or eng in NOSYNC_ENGS):
                continue
            keep.add(d)
        op.deps = keep
        for d in keep:
            self.ops[d].has_dep = True
        for r in reads:
            r.r.append(op.id)
        for w in writes:
            w.w = op.id
            w.r = []
        if dma:
            op.key = key if key is not None else id((list(writes) + list(reads))[0])
        self.ops.append(op)
        return op

    def build(self, nc, st, tag):
        engs = ["pe", "act", "dve", "pool", "sp"]
        ops = self.ops
        last = {}
        for op in ops:
            if not op.dma:
                last[op.eng] = op
        finals = [o for o in ops if o.dma] + [o for e, o in last.items() if e != "sp"]
        for o in finals:
            o.has_dep = True
        esem = {e: st.enter_context(nc.semaphore("s%s_%s" % (tag, e))) for e in engs}
        dsem, dcnt, ecnt = {}, {}, {e: 0 for e in engs}
        for op in ops:
            if op.dma:
                if op.key not in dsem:
                    dsem[op.key] = st.enter_context(nc.semaphore("d%s_%d" % (tag, len(dsem))))
                    dcnt[op.key] = 0
                dcnt[op.key] += 16
                op.tok = (dsem[op.key], dcnt[op.key], 16)
            elif op.has_dep:
                ecnt[op.eng] += 1
                op.tok = (esem[op.eng], ecnt[op.eng], 1)
        self.n_sems = len(dsem) + 5
        block = st.enter_context(nc.Block())

        def run_engine(ename):
            def body(eng):
                waited = {}

                def wait(tok):
                    sem, val, _ = tok
                    if waited.get(sem.num, 0) < val:
                        eng.wait_ge(sem, val)
                        waited[sem.num] = val

                for op in ops:
                    if op.eng != ename:
                        continue
                    for d in sorted(op.deps):
                        wait(ops[d].tok)
                    inst = op.fn(eng)
                    if op.tok is not None:
                        inst.then_inc(op.tok[0], op.tok[2])
                if ename == "sp":
                    best = {}
                    for fo in finals:
                        if fo.tok[0].num not in best or best[fo.tok[0].num][1] < fo.tok[1]:
                            best[fo.tok[0].num] = fo.tok
                    for tk in best.values():
                        wait(tk)
            return body

        block.tensor(run_engine("pe"))
        block.scalar(run_engine("act"))
        block.vector(run_engine("dve"))
        block.gpsimd(run_engine("pool"))
        block.sync(run_engine("sp"))


class Em:
    def __init__(self, P):
        self.P = P

    def mm(self, out, lhsT, rhs, start, stop, reads, writes):
        self.P.emit("pe", lambda e: e.matmul(out, lhsT, rhs, start=start, stop=stop), reads, writes)

    def tr(self, out, in_, ident, reads, writes):
        self.P.emit("pe", lambda e: e.transpose(out, in_, ident), reads, writes)

    def act(self, out, in_, func, reads, writes, bias=0.0, scale=1.0, accum=None):
        if accum is None:
            self.P.emit("act", lambda e: e.activation(out, in_, func, bias=bias, scale=scale), reads, writes)
        else:
            self.P.emit("act", lambda e: e.activation(out, in_, func, bias=bias, scale=scale, accum_out=accum), reads, writes)

    def tt(self, eng, out, a, b, op, reads, writes):
        self.P.emit(eng, lambda e: e.tensor_tensor(out, a, b, op=op), reads, writes)

    def ts(self, eng, out, a, s1, s2, op0, op1, reads, writes):
        if s2 is None:
            self.P.emit(eng, lambda e: e.tensor_scalar(out, a, s1, None, op0=op0), reads, writes)
        else:
            self.P.emit(eng, lambda e: e.tensor_scalar(out, a, s1, s2, op0=op0, op1=op1), reads, writes)

    def stt(self, eng, out, in0, scalar, in1, op0, op1, reads, writes):
        self.P.emit(eng, lambda e: e.scalar_tensor_tensor(out, in0, scalar, in1, op0=op0, op1=op1), reads, writes)

    def cp(self, eng, out, in_, reads, writes):
        if eng == "act":
            self.P.emit("act", lambda e: e.activation(out, in_, AF.Copy), reads, writes)
        else:
            self.P.emit(eng, lambda e: e.tensor_copy(out, in_), reads, writes)

    def recip(self, out, in_, reads, writes):
        self.P.emit("dve", lambda e: e.reciprocal(out, in_), reads, writes)

    def memset(self, eng, ap, val, writes):
        self.P.emit(eng, lambda e: e.memset(ap, val), (), writes)

    def dma(self, out, in_, reads, writes, key=None, store=False):
        if key is None:
            key = id(reads[0]) if store else id(writes[0])
        return self.P.emit("sp", lambda e: e.dma_start(out=out, in_=in_), reads, writes, dma=True, key=key)


class Rot:
    def __init__(self, items):
        self.items = items
        self.i = 0

    def next(self):
        it = self.items[self.i % len(self.items)]
        self.i += 1
        return it


def build(S, BT=256, dbg=False):
    NT = S // 128
    NB1 = S // BT
    TPB = BT // 128
    CPB = BT // 64
    NB2 = S // 512
    nc = bass.Bass("TRN2", target_bir_lowering=False)

    def din(name, shape, dt):
        return nc.dram_tensor(name, shape, dt, kind="ExternalInput").ap()

    def dscr(name, shape, dt):
        return nc.dram_tensor(name, shape, dt, kind="Internal").ap()

    x_d = din("x", [S, D], F32)
    pos_d = din("pos", [1, S], I32)
    win_d = din("w_in_l", [128, 8, DIN], F32)
    lng_d = din("ln_g_l", [128, 8], F32)
    wq_d = din("w_q_l", [128, 2, 768], F32)
    qg_d = din("q_g_l", [128, 2], F32)
    wkv_d = din("w_kv_l", [128, 1024], F32)
    kvg_d = din("kv_g_l", [128, 1], F32)
    wout_d = din("w_out_l", [128, 8, 1024], F32)
    hgn_d = din("hgn_l", [128, 4], F32)
    hlb_d = din("hlb", [2, 512], F32)
    hlbfm_d = din("hlb_fm", [128, 2, 4], F32)
    fng_d = din("fng", [1, 1024], F32)
    ident_d = din("c_ident", [128, 128], BF16)
    tri_d = din("c_tri", [128, 128], BF16)
    lm_d = din("c_lm", [128, 128], BF16)
    lcols_d = din("c_lcols", [128, 4], BF16)
    us_d = din("c_us", [128, 128], BF16)
    lrep_d = din("c_lrep", [128, 512], F32)
    invf_d = din("c_invf", [128, 1], F32)
    out_d = nc.dram_tensor("out", [S, D], F32, kind="ExternalOutput").ap()

    Qd = dscr("Qd", [8, 96, S], BF16)
    Kd = dscr("Kd", [8, 96, S], BF16)
    Vd = dscr("Vd", [NT, 128, 640], BF16)
    Gd = dscr("Gd", [4, 128, S], BF16)
    Yd = dscr("Yd", [4, 128, S], BF16)
    Rd = dscr("Rd", [4, 512], F32)
    Cd = dscr("Cd", [32, S], F32)
    Sd = dscr("Sd", [32, S], F32)
    bQ = [[Buf() for _ in range(NB1)] for _ in range(8)]
    bK = [[Buf() for _ in range(NB1)] for _ in range(8)]
    bV = [Buf() for _ in range(NT)]
    bG = [[Buf() for _ in range(NB1)] for _ in range(4)]
    bY = [[Buf() for _ in range(NB1)] for _ in range(4)]
    bC = [Buf() for _ in range(NB1)]
    bS = [Buf() for _ in range(NB1)]

    with ExitStack() as st:
        P = Prog()
        E = Em(P)

        def sb(name, shape, dt):
            return st.enter_context(nc.sbuf_tensor("a_" + name, shape, dt)), Buf(name)

        def ps(name, shape, dt):
            return st.enter_context(nc.psum_tensor("a_" + name, shape, dt)), Buf(name, True)

        winb, b_winb = sb("winb", [128, 8, DIN], BF16)
        b_wp = [Buf("winb_p%d" % j) for j in range((DIN + 511) // 512)]

        def wb(col0, M):
            return [b_wp[j] for j in range(col0 // 512, (col0 + M - 1) // 512 + 1)]
        wkrr, b_wkrr = sb("wkrr", [128, 8, 96], BF16)
        wqb, b_wqb = sb("wqb", [128, 2, 768], BF16)
        wqr, b_wqr = sb("wqr", [128, 2, 768], BF16)
        wkvb, b_wkvb = sb("wkvb", [128, 1024], BF16)
        ident, b_ident = sb("ident", [128, 128], BF16)
        onesf, b_onesf = sb("onesf", [128, 128], BF16)
        lm, b_lm = sb("lm", [128, 128], BF16)
        lcols, b_lcols = sb("lcols", [128, 4], BF16)
        us, b_us = sb("us", [128, 128], BF16)
        lrep, b_lrep = sb("lrep", [128, 512], F32)
        invf, b_invf = sb("invf", [128, 1], F32)
        lbrep, b_lbrep = sb("lbrep", [128, 512], F32)
        omlrep, b_omlrep = sb("omlrep", [128, 512], F32)
        omlfm, b_omlfm = sb("omlfm", [128, 4], F32)
        hgn, b_hgn = sb("hgn", [128, 4], F32)
        lng, b_lng = sb("lng", [128, 8], F32)
        qg, b_qg = sb("qg", [128, 2], F32)
        kvg, b_kvg = sb("kvg", [128, 1], F32)
        Sst = [sb("S%d" % h, [128, 128], F32) for h in range(4)]
        Sp = [[sb("Sp%d_%d" % (h, k), [128, 128], BF16) for k in range(2)] for h in range(4)]
        tmw = Rot([sb("tmw%d" % i, [128, 512], F32) for i in range(3)])
        tmp = Rot([sb("tmp%d" % i, [128, BT], F32) for i in range(6)])
        hl = Rot([sb("hl%d" % i, [128, BT], BF16) for i in range(6)])
        tmpY = Rot([sb("tmpY%d" % i, [128, BT], F32) for i in range(4)])
        tmpB = Rot([sb("tmpB%d" % i, [128, BT], F32) for i in range(2)])
        kr1, b_kr1 = sb("kr1", [96, BT], F32)
        kr2, b_kr2 = sb("kr2", [96, BT], F32)
        hlY = Rot([sb("hlY%d" % i, [128, BT], BF16) for i in range(2)])
        xs = [sb("xs%d" % i, [128, D], F32) for i in range(2)]
        hb = [sb("hb%d" % i, [128, D], BF16) for i in range(2)]
        ssq = Rot([sb("ssq%d" % i, [128, 1], F32) for i in range(2)])
        lnv = Rot([sb("lnv%d" % i, [128, 1], F32) for i in range(2)])
        rstd = Rot([sb("rstd%d" % i, [128, 1], F32) for i in range(2)])
        hT, b_hT = sb("hT", [128, 8, BT], BF16)
        cosb = [sb("cosb%d" % k, [96, BT], F32) for k in range(2)]
        sinb = [sb("sinb%d" % k, [96, BT], F32) for k in range(2)]
        def grouped(name, n, dt_, sets):
            out_t, out_v = [], []
            for k in range(sets):
                t_, b_ = sb("%s%d" % (name, k), [128, n, BT], dt_)
                out_t.append((t_, b_))
                out_v.append([(t_[:, j, :], b_) for j in range(n)])
            return out_t, out_v
        qlf_t, qlf = grouped("qlf", 2, F32, 2)
        sqq_t, sqq = grouped("sqq", 2, F32, 2)
        kvf = [sb("kvf%d" % k, [128, BT], F32) for k in range(2)]
        sqk = [sb("sqk%d" % k, [128, BT], F32) for k in range(2)]
        kpe = [sb("kpe%d" % k, [96, BT], BF16) for k in range(2)]
        hfm_t, hfm = grouped("hfm", 4, F32, 2)
        hft = [[sb("hft%d_%d" % (k, i), [128, 512], F32) for i in range(TPB)] for k in range(2)]
        hqr_t, hqr = grouped("hqr", 4, F32, 2)
        gmr_t, gmr = grouped("gmr", 4, BF16, 2)
        ghr_t, ghr = grouped("ghr", 4, BF16, 2)
        vtm = [[sb("vtm%d_%d" % (k, i), [128, 512], BF16) for i in range(TPB)] for k in range(3)]
        qn, b_qn = sb("qn", [128, 2, BT], BF16)
        kvn, b_kvn = sb("kvn", [128, BT], BF16)
        QTo = Rot([sb("QTo%d" % i, [96, BT], BF16) for i in range(2)])
        KTo = Rot([sb("KTo%d" % i, [96, BT], BF16) for i in range(2)])
        Vb = Rot([sb("Vb%d" % i, [128, 4, 160], BF16) for i in range(2)])
        go = Rot([sb("go%d" % i, [128, 4, BT], BF16) for i in range(2)])
        gtm = [sb("gtm%d" % i, [128, 512], F32) for i in range(TPB)]
        ghi = [sb("ghi%d" % i, [128, 512], BF16) for i in range(TPB)]
        glo = [sb("glo%d" % i, [128, 512], BF16) for i in range(TPB)]
        sq_t, sq_v = grouped("sq_", 4, BF16, 1)
        sq_ = sq_v[0]
        ktl = [sb("ktl%d" % h, [128, BT], BF16) for h in range(4)]
        khat = [[sb("khat%d_%d" % (k, i), [128, 512], BF16) for i in range(TPB)] for k in range(2)]
        ghs_t, ghs = grouped("ghs", 4, BF16, 2)
        qtl = [[sb("qtl%d_%d" % (k, h), [128, BT], BF16) for h in range(4)] for k in range(2)]
        Abf = [[sb("Abf%d_%d" % (k, h), [128, BT], BF16) for h in range(4)] for k in range(2)]
        eC = [[sb("eC%d_%d" % (k, h), [128, 4 * TPB], F32) for h in range(4)] for k in range(2)]
        oT = [sb("oT%d" % h, [128, BT], F32) for h in range(4)]
        yo = Rot([sb("yo%d" % i, [128, BT], BF16) for i in range(2)])
        ri = Rot([sb("ri%d" % i, [128, 512], I32) for i in range(1)])
        stg = Rot(list(tmw.items) + [hft[k][i] for k in range(2) for i in range(TPB)])
        pA = Rot([ps("pA%d" % i, [128, 512], F32) for i in range(3)])
        pB = Rot([ps("pB%d" % i, [128, 512], F32) for i in range(2)])
        pU, _ = ps("pU", [128, 512], F32)
        b_pU = [Buf("pU", True)] * 4
        pO, _ = ps("pO", [128, 512], F32)
        b_pO = [Buf("pO", True)] * 4
        pI, _ = ps("pI", [128, 512], F32)
        b_pI = [Buf("pI", True)] * 4

        for (t, b), d in [((ident, b_ident), ident_d), ((lm, b_lm), lm_d), ((lcols, b_lcols), lcols_d),
                          ((us, b_us), us_d), ((lrep, b_lrep), lrep_d), ((invf, b_invf), invf_d),
                          ((hgn, b_hgn), hgn_d), ((lng, b_lng), lng_d), ((qg, b_qg), qg_d), ((kvg, b_kvg), kvg_d)]:
            E.dma(t[:], d, [], [b])
        E.memset("pool", onesf[:], 1.0, [b_onesf])
        for h in range(4):
            E.memset("pool", Sst[h][0][:], 0.0, [Sst[h][1]])
        for vb, b_vb in Vb.items:
            E.memset("pool", vb[:], 1.0, [b_vb])
        E.memset("pool", wkrr[:], 0.0, [b_wkrr])
        E.memset("pool", wqr[:], 0.0, [b_wqr])
        a0, b_a0 = tmw.next()
        a1, b_a1 = tmw.next()
        E.dma(a0[:], hlb_d[0:1, :].partition_broadcast(128), [], [b_a0])
        E.dma(a1[:], hlb_d[1:2, :].partition_broadcast(128), [], [b_a1])
        E.tt("dve", a0[:], a0[:], a1[:], ALU.subtract, [b_a0, b_a1], [b_a0])
        E.act(lbrep[:], a0[:], AF.Sigmoid, [b_a0], [b_lbrep])
        E.act(omlrep[:], a0[:], AF.Sigmoid, [b_a0], [b_omlrep], scale=-1.0)
        fm, b_fm = sb("lbfm", [128, 2, 4], F32)
        E.dma(fm[:], hlbfm_d, [], [b_fm])
        E.tt("dve", fm[:, 0, :], fm[:, 0, :], fm[:, 1, :], ALU.subtract, [b_fm], [b_fm])
        E.act(omlfm[:], fm[:, 0, :], AF.Sigmoid, [b_fm], [b_omlfm], scale=-1.0)
        ncast = [0]

        def cast_scaled(dst, src, col, reads, writes):
            E.ts("dve", dst, src, col, None, ALU.mult, None, reads, writes)
            ncast[0] += 1

        def setup_weights():
            for j0 in range(0, DIN, 512):
                w = min(512, DIN - j0)
                for c in range(8):
                    sg_, b_sg = stg.next()
                    E.dma(sg_[:, 0:w], win_d[:, c, j0:j0 + w], [], [b_sg])
                    cast_scaled(winb[:, c, j0:j0 + w], sg_[:, 0:w], lng[:, c:c + 1], [b_sg, b_lng], [b_wp[j0 // 512]])
                emit_rope(NRP)
                if j0 == 0:
                    for c in range(8):
                        E.ts("pool", wkrr[:, c, 64:80], winb[:, c, KR + 16:KR + 32], -1.0, None, ALU.mult, None, [b_wp[0]], [b_wkrr])
                        E.cp("pool", wkrr[:, c, 80:96], winb[:, c, KR:KR + 16], [b_wp[0]], [b_wkrr])
            for c in range(2):
                for j0 in range(0, 768, 512):
                    w = min(512, 768 - j0)
                    sg_, b_sg = stg.next()
                    E.dma(sg_[:, 0:w], wq_d[:, c, j0:j0 + w], [], [b_sg])
                    cast_scaled(wqb[:, c, j0:j0 + w], sg_[:, 0:w], qg[:, c:c + 1], [b_sg, b_qg], [b_wqb])
                v_in = wqb[:, c, :].rearrange("p (h f) -> p h f", h=8)
                v_out = wqr[:, c, :].rearrange("p (h f) -> p h f", h=8)
                E.ts("pool", v_out[:, :, 64:80], v_in[:, :, 80:96], -1.0, None, ALU.mult, None, [b_wqb], [b_wqr])
                E.cp("pool", v_out[:, :, 80:96], v_in[:, :, 64:80], [b_wqb], [b_wqr])
            for j0 in range(0, 1024, 512):
                sg_, b_sg = stg.next()
                E.dma(sg_[:], wkv_d[:, j0:j0 + 512], [], [b_sg])
                cast_scaled(wkvb[:, j0:j0 + 512], sg_[:], kvg[:, 0:1], [b_sg, b_kvg], [b_wkvb])
        R = slice(64, 96)
        rope_done = [0]

        NRP = (S // 512 + 3) // 4

        def emit_rope(upto):
            while rope_done[0] < min(upto, NRP):
                ps_ = rope_done[0]
                rope_done[0] += 1
                nb4 = min(4, S // 512 - 4 * ps_)
                NP = 32 * nb4
                it, b_it = ri.next()
                u, b_u = tmw.next()
                nf, b_nf = tmw.next()
                for q in range(nb4):
                    blk = 4 * ps_ + q
                    E.dma(it[32 * q:32 * q + 32, :], pos_d[0:1, blk * 512:(blk + 1) * 512].partition_broadcast(32), [], [b_it],
                          key=("ropeld", q))
                E.cp("dve", u[0:NP, :], it[0:NP, :], [b_it], [b_u])
                E.ts("dve", u[0:NP, :], u[0:NP, :], invf[0:NP, 0:1], None, ALU.mult, None, [b_u, b_invf], [b_u])
                for which, dst, bufs in ((0, Sd, bS), (1, Cd, bC)):
                    if which == 1:
                        E.ts("dve", u[0:NP, :], u[0:NP, :], 0.25, None, ALU.add, None, [b_u], [b_u])
                    E.cp("dve", it[0:NP, :], u[0:NP, :], [b_u], [b_it])
                    E.cp("dve", nf[0:NP, :], it[0:NP, :], [b_it], [b_nf])
                    E.tt("dve", nf[0:NP, :], u[0:NP, :], nf[0:NP, :], ALU.subtract, [b_u, b_nf], [b_nf])
                    E.act(nf[0:NP, :], nf[0:NP, :], AF.Sin, [b_nf], [b_nf], scale=2.0 * math.pi)
                    for q in range(nb4):
                        blk = 4 * ps_ + q
                        nbk = 512 // BT
                        E.dma(dst[:, blk * 512:(blk + 1) * 512], nf[32 * q:32 * q + 32, :], [b_nf],
                              [bufs[blk * nbk + k] for k in range(nbk)], key=("ropest", id(b_nf), q))

        emit_rope(1)
        setup_weights()

        MODE = os.environ.get("MK_MODE", "all")

        def stage_pair(specs):
            pt, b_pt = pA.next()
            for g, (col0, M, wt, b_wt) in enumerate(specs):
                for c in range(8):
                    lhs = winb[:, c, col0:col0 + M] if wt is None else wt[:, c, 0:M]
                    E.mm(pt[0:M, g * BT:(g + 1) * BT], lhs, hT[:, c, :], c == 0, c == 7,
                         (wb(col0, M) if wt is None else [b_wt]) + [b_hT], [b_pt])
            return pt, b_pt

        def stage_tm(i, col0):
            pt, b_pt = pA.next()
            for c in range(8):
                E.mm(pt[:, 0:512], hT[:, c, i * 128:(i + 1) * 128], winb[:, c, col0:col0 + 512], c == 0, c == 7,
                     wb(col0, 512) + [b_hT], [b_pt])
            return pt, b_pt

        def rep_rstd(sq_list, ndim, tpool, hpool, pbank, extra_bias=0.0):
            pt, b_pt = pbank
            parts = []
            for (sqt, b_sq) in sq_list:
                hi_, b_hi = hpool.next()
                lo_, b_lo = hpool.next()
                E.cp("act", hi_[:], sqt[:], [b_sq], [b_hi])
                E.tt("dve", lo_[:], sqt[:], hi_[:], ALU.subtract, [b_sq, b_hi], [b_lo])
                parts += [(hi_, b_hi), (lo_, b_lo)]
            for k, (sqt, b_sq) in enumerate(parts):
                E.mm(pt[:, 0:BT], onesf[:], sqt[:], k == 0, k == len(parts) - 1, [b_onesf, b_sq], [b_pt])
            l_, b_l = tpool.next()
            E.act(l_[:], pt[:, 0:BT], AF.Ln, [b_pt], [b_l], bias=EPS, scale=1.0 / ndim)
            r_, b_r = tpool.next()
            E.act(r_[:], l_[:], AF.Exp, [b_l], [b_r], scale=-0.5, bias=extra_bias)
            return r_, b_r

        def load_x(gt):
            t, b = xs[gt % 2]
            E.dma(t[:], x_d[gt * 128:(gt + 1) * 128, :], [], [b])

        load_x(0)
        stores = []

        def genX1(blk):
            k2 = blk % 2
            tok = slice(blk * BT, (blk + 1) * BT)
            E.dma(cosb[k2][0][R, :], Cd[:, tok], [bC[blk]], [cosb[k2][1]])
            E.dma(sinb[k2][0][R, :], Sd[:, tok], [bS[blk]], [sinb[k2][1]])
            for i in range(TPB):
                gt = blk * TPB + i
                if gt + 1 < NT:
                    load_x(gt + 1)
                xt, b_x = xs[gt % 2]
                ht, b_h = hb[gt % 2]
                sq1, b_sq1 = ssq.next()
                E.memset("pool", sq1[:], 0.0, [b_sq1])
                E.act(ht[:], xt[:], AF.Square, [b_x], [b_h, b_sq1], accum=sq1[:, 0:1])
                l1, b_l1 = lnv.next()
                E.act(l1[:], sq1[:], AF.Ln, [b_sq1], [b_l1], bias=EPS, scale=1.0 / D)
                r1, b_r1 = rstd.next()
                E.act(r1[:], l1[:], AF.Exp, [b_l1], [b_r1], scale=-0.5)
                yield
                E.act(ht[:], xt[:], AF.Copy, [b_x, b_r1], [b_h], scale=r1[:, 0:1])
                pt_, b_pT = pA.next()
                pT = pt_[:].bitcast(BF16)
                for c in range(8):
                    E.tr(pT[:, c * 128:(c + 1) * 128], ht[:, c * 128:(c + 1) * 128], ident[:], [b_h, b_ident], [b_pT])
                E.cp("dve", hT[:, :, i * 128:(i + 1) * 128], pT.rearrange("p (c t) -> p c t", c=8), [b_pT], [b_hT])
                yield
            pt, b_pt = stage_pair([(QL, 128, None, None), (QL + 128, 128, None, None)])
            p3 = pt[:, 0:2 * BT].rearrange("p (g t) -> p g t", g=2)
            E.cp("dve", qlf_t[k2][0][:], p3, [b_pt], [qlf_t[k2][1]])
            E.act(sqq_t[k2][0][:], p3, AF.Square, [b_pt], [sqq_t[k2][1]])
            yield
            pt, b_pt = stage_pair([(KVL, 128, None, None)])
            E.cp("dve", kvf[k2][0][:], pt[:, 0:BT], [b_pt], [kvf[k2][1]])
            E.act(sqk[k2][0][:], pt[:, 0:BT], AF.Square, [b_pt], [sqk[k2][1]])
            yield
            pt, b_pt = stage_pair([(KR - 64, 96, None, None), (0, 96, wkrr, b_wkrr)])
            E.tt("dve", kr1[R, :], pt[R, 0:BT], cosb[k2][0][R, :], ALU.mult, [b_pt, cosb[k2][1]], [b_kr1])
            E.tt("dve", kr2[R, :], pt[R, BT:2 * BT], sinb[k2][0][R, :], ALU.mult, [b_pt, sinb[k2][1]], [b_kr2])
            E.tt("dve", kpe[k2][0][R, :], kr1[R, :], kr2[R, :], ALU.add, [b_kr1, b_kr2], [kpe[k2][1]])
            yield
            n = 0
            for (col, dst_t) in ((GM, gmr_t), (HQ, hqr_t), (HF, hfm_t)):
                for p_ in range(2):
                    pt, b_pt = stage_pair([(col + (2 * p_) * 128, 128, None, None), (col + (2 * p_ + 1) * 128, 128, None, None)])
                    E.cp("act" if n % 2 == 0 else "dve", dst_t[k2][0][:, 2 * p_:2 * p_ + 2, :],
                         pt[:, 0:2 * BT].rearrange("p (g t) -> p g t", g=2), [b_pt], [dst_t[k2][1]])
                    n += 1
                    yield
            for i in range(TPB):
                pt, b_pt = stage_tm(i, HF)
                E.cp("dve" if i % 2 == 0 else "act", hft[k2][i][0][:], pt[:, 0:512], [b_pt], [hft[k2][i][1]])
                yield
            for i in range(TPB):
                pt, b_pt = stage_tm(i, HI)
                E.cp("act" if i % 2 == 0 else "dve", vtm[blk % 3][i][0][:], pt[:, 0:512], [b_pt], [vtm[blk % 3][i][1]])
                yield
            for p_ in range(2):
                pt, b_pt = stage_pair([(GH + (2 * p_) * 128, 128, None, None), (GH + (2 * p_ + 1) * 128, 128, None, None)])
                E.cp("act" if p_ % 2 == 0 else "dve", ghr_t[k2][0][:, 2 * p_:2 * p_ + 2, :],
                     pt[:, 0:2 * BT].rearrange("p (g t) -> p g t", g=2), [b_pt], [ghr_t[k2][1]])
                yield

        def genX2(blk):
            k2 = blk % 2
            tok = slice(blk * BT, (blk + 1) * BT)
            cb, b_cb = cosb[k2]
            sn, b_sn = sinb[k2]
            for i in range(TPB):
                t, b_t = hft[k2][i]
                E.act(t[:], t[:], AF.Sigmoid, [b_t], [b_t])
            t, b_t = hfm_t[k2]
            E.act(t[:], t[:], AF.Sigmoid, [b_t], [b_t], scale=-1.0)
            yield
            g_, b_g = go.next()
            E.act(g_[:], gmr_t[k2][0][:], AF.Silu, [gmr_t[k2][1]], [b_g])
            stores.append(E.dma(Gd[:, :, tok].rearrange("j p t -> p j t"), g_[:], [b_g], [bG[0][blk]], store=True))
            E.act(sq_t[0][0][:], hqr_t[k2][0][:], AF.Silu, [hqr_t[k2][1]], [sq_t[0][1]])
            yield
            E.act(ghs_t[k2][0][:], ghr_t[k2][0][:], AF.Silu, [ghr_t[k2][1]], [ghs_t[k2][1]])
            yield "EL"
            def chainA():
                rq, b_rq = rep_rstd(sqq[k2], 256, tmp, hl, pB.next(), extra_bias=math.log(1.0 / math.sqrt(96.0)))
                for g in range(2):
                    E.tt("dve", qn[:, g, :], qlf[k2][g][0][:], rq[:], ALU.mult, [qlf[k2][g][1], b_rq], [b_qn])
                yield
                rk, b_rk = rep_rstd([sqk[k2]], 128, tmp, hl, pB.next())
                E.tt("dve", kvn[:], kvf[k2][0][:], rk[:], ALU.mult, [kvf[k2][1], b_rk], [b_kvn])
                yield
                for h in range(8):
                    pq, b_pq = pB.next()
                    for c in range(2):
                        E.mm(pq[0:96, 0:BT], wqb[:, c, h * 96:(h + 1) * 96], qn[:, c, :], c == 0, c == 1, [b_wqb, b_qn], [b_pq])
                    for c in range(2):
                        E.mm(pq[0:96, BT:2 * BT], wqr[:, c, h * 96:(h + 1) * 96], qn[:, c, :], c == 0, c == 1, [b_wqr, b_qn], [b_pq])
                    qo, b_qo = QTo.next()
                    E.cp("act", qo[0:64, :], pq[0:64, 0:BT], [b_pq], [b_qo])
                    t1, b_t1 = tmp.next()
                    E.tt("dve", t1[R, :], pq[R, 0:BT], cb[R, :], ALU.mult, [b_pq, b_cb], [b_t1])
                    t2, b_t2 = tmp.next()
                    E.tt("dve", t2[R, :], pq[R, BT:2 * BT], sn[R, :], ALU.mult, [b_pq, b_sn], [b_t2])
                    E.tt("pool", qo[R, :], t1[R, :], t2[R, :], ALU.add, [b_t1, b_t2], [b_qo])
                    stores.append(E.dma(Qd[h, :, tok], qo[:], [b_qo], [bQ[h][blk]], store=True))
                    yield
                    if h % 2 == 1:
                        pk, b_pk = pB.next()
                        for j, hh in enumerate((h - 1, h)):
                            E.mm(pk[0:64, j * BT:(j + 1) * BT], wkvb[:, hh * 128:hh * 128 + 64], kvn[:], True, True,
                                 [b_wkvb, b_kvn], [b_pk])
                        for j, hh in enumerate((h - 1, h)):
                            ko, b_ko = KTo.next()
                            E.cp("act", ko[0:64, :], pk[0:64, j * BT:(j + 1) * BT], [b_pk], [b_ko])
                            E.cp("pool", ko[R, :], kpe[k2][0][R, :], [kpe[k2][1]], [b_ko])
                            stores.append(E.dma(Kd[hh, :, tok], ko[:], [b_ko], [bK[hh][blk]], store=True))
                        yield
                wv = wkvb[:].rearrange("p (h f) -> p h f", h=8)[:, :, 64:128]
                for i in range(TPB):
                    gt = blk * TPB + i
                    pv, b_pv = pB.next()
                    E.mm(pv[:, 0:512].rearrange("p (h f) -> p h f", h=8), kvn[:, i * 128:(i + 1) * 128], wv, True, True,
                         [b_kvn, b_wkvb], [b_pv])
                    vb, b_vb = Vb.next()
                    pv4 = pv[:, 0:512].rearrange("p (j f) -> p j f", j=4)
                    E.cp("dve", vb[:, :, 0:64], pv4[:, :, 0:64], [b_pv], [b_vb])
                    E.cp("act", vb[:, :, 96:160], pv4[:, :, 64:128], [b_pv], [b_vb])
                    stores.append(E.dma(Vd[gt], vb[:].rearrange("p j f -> p (j f)"), [b_vb], [bV[gt]], store=True))
                    yield

            def chainB():
                for i in range(TPB):
                    s_, b_s = hft[k2][i]
                    E.tt("dve", s_[:], s_[:], omlrep[:], ALU.mult, [b_s, b_omlrep], [b_s])
                    E.tt("dve", s_[:], s_[:], lbrep[:], ALU.add, [b_s, b_lbrep], [b_s])
                    E.act(gtm[i][0][:], s_[:], AF.Ln, [b_s], [gtm[i][1]])
                    E.cp("act", ghi[i][0][:], gtm[i][0][:], [gtm[i][1]], [ghi[i][1]])
                    E.tt("dve", glo[i][0][:], gtm[i][0][:], ghi[i][0][:], ALU.subtract, [gtm[i][1], ghi[i][1]], [glo[i][1]])
                    yield
                for i in range(TPB):
                    s_, b_s = hft[k2][i]
                    pr_, b_pr = pB.next()
                    E.mm(pr_[:, 0:512], us[:], ghi[i][0][:], True, False, [b_us, ghi[i][1]], [b_pr])
                    E.mm(pr_[:, 0:512], us[:], glo[i][0][:], False, True, [b_us, glo[i][1]], [b_pr])
                    er, b_er = tmw.next()
                    E.act(er[:], pr_[:, 0:512], AF.Exp, [b_pr], [b_er])
                    E.stt("dve", khat[k2][i][0][:], s_[:], 1.0, er[:], ALU.subtract, ALU.mult, [b_s, b_er], [khat[k2][i][1]])
                    yield
                for h in range(4):
                    hs = slice(h * 128, (h + 1) * 128)
                    pd, b_pd = pB.next()
                    for i in range(TPB):
                        E.mm(pd[:, i * 128:(i + 1) * 128], ghi[i][0][:, hs], lm[:], True, False, [ghi[i][1], b_lm], [b_pd])
                        E.mm(pd[:, i * 128:(i + 1) * 128], glo[i][0][:, hs], lm[:], False, True, [glo[i][1], b_lm], [b_pd])
                    for i in range(TPB):
                        E.mm(pd[:, BT + i * 4:BT + (i + 1) * 4], ghi[i][0][:, hs], lcols[:], True, False, [ghi[i][1], b_lcols], [b_pd])
                        E.mm(pd[:, BT + i * 4:BT + (i + 1) * 4], glo[i][0][:, hs], lcols[:], False, True, [glo[i][1], b_lcols], [b_pd])
                    ed, b_ed = tmpB.next()
                    edn, b_edn = tmpB.next()
                    E.act(ed[:], pd[:, 0:BT], AF.Exp, [b_pd], [b_ed])
                    E.act(edn[:], pd[:, 0:BT], AF.Exp, [b_pd], [b_edn], scale=-1.0)
                    E.act(eC[k2][h][0][:], pd[:, BT:BT + 4 * TPB], AF.Exp, [b_pd], [eC[k2][h][1]])
                    E.tt("dve", qtl[k2][h][0][:], sq_[h][0][:], ed[:], ALU.mult, [sq_[h][1], b_ed], [qtl[k2][h][1]])
                    E.stt("dve", ktl[h][0][:], hfm[k2][h][0][:], omlfm[:, h:h + 1], edn[:], ALU.mult, ALU.mult,
                          [hfm[k2][h][1], b_omlfm, b_edn], [ktl[h][1]])
                    yield
                    pa, b_pa = pB.next()
                    for i in range(TPB):
                        ts_ = slice(i * 128, (i + 1) * 128)
                        E.mm(pa[:, ts_], ktl[h][0][:, ts_], qtl[k2][h][0][:, ts_], True, True, [ktl[h][1], qtl[k2][h][1]], [b_pa])
                    E.tt("dve", Abf[k2][h][0][:], pa[:, 0:BT], lrep[:, 0:BT], ALU.mult, [b_pa, b_lrep], [Abf[k2][h][1]])
                    yield

            ga, gb = chainA(), chainB()
            while ga is not None or gb is not None:
                if gb is not None:
                    try:
                        next(gb)
                        yield
                    except StopIteration:
                        gb = None
                if ga is not None:
                    try:
                        next(ga)
                        yield
                    except StopIteration:
                        ga = None

        def genY(blk):
            k2 = blk % 2
            k3 = blk % 3
            tok = slice(blk * BT, (blk + 1) * BT)
            for i in range(TPB):
                ts_ = slice(i * 128, (i + 1) * 128)
                for h in range(4):
                    hs = slice(h * 128, (h + 1) * 128)
                    E.mm(pO[:, hs], vtm[k3][i][0][:, hs], Abf[k2][h][0][:, ts_], True, True, [vtm[k3][i][1], Abf[k2][h][1]],
                         [b_pO[h]])
                for half in range(2):
                    rows = slice(half * 64, half * 64 + 64)
                    c = 2 * i + half
                    cc = slice(c * 64, c * 64 + 64)
                    mid = i * 4 + half * 2
                    for h in range(4):
                        S_, b_S = Sst[h]
                        sp_, b_sp = Sp[h][c % 2]
                        E.ts("dve", sp_[:], S_[:], eC[k2][h][0][:, mid:mid + 1], None, ALU.mult, None, [b_S, eC[k2][h][1]], [b_sp])
                    for h in range(4):
                        hs = slice(h * 128, (h + 1) * 128)
                        sp_, b_sp = Sp[h][c % 2]
                        E.mm(pI[:, h * 128 + half * 64:h * 128 + half * 64 + 64], sp_[:], qtl[k2][h][0][:, cc], True, True,
                             [b_sp, qtl[k2][h][1]], [b_pI[h]])
                        E.mm(pU[:, hs], khat[k2][i][0][rows, hs], vtm[k3][i][0][rows, hs], True, True,
                             [khat[k2][i][1], vtm[k3][i][1]], [b_pU[h]])
                    yield
                    for h in range(4):
                        hs = slice(h * 128, (h + 1) * 128)
                        S_, b_S = Sst[h]
                        E.stt("dve", S_[:], S_[:], eC[k2][h][0][:, mid + 1:mid + 2], pU[:, hs], ALU.mult, ALU.subtract,
                              [b_S, eC[k2][h][1], b_pU[h]], [b_S])
                    yield
                for h in range(4):
                    hs = slice(h * 128, (h + 1) * 128)
                    E.cp("act", oT[h][0][:, ts_], pO[:, hs], [b_pO[h]], [oT[h][1]])
                    E.tt("dve", oT[h][0][:, ts_], pI[:, hs], oT[h][0][:, ts_], ALU.add, [b_pI[h], oT[h][1]], [oT[h][1]])
                yield
            yield "EPI"
            for h in range(4):
                s_, b_s = tmpY.next()
                E.act(s_[:], oT[h][0][:], AF.Square, [oT[h][1]], [b_s])
                ro, b_ro = rep_rstd([(s_, b_s)], 128, tmpY, hlY, (pU, b_pU[0]))
                y1, b_y1 = tmpY.next()
                E.tt("dve", y1[:], oT[h][0][:], ro[:], ALU.mult, [oT[h][1], b_ro], [b_y1])
                y_, b_y = yo.next()
                E.stt("dve", y_[:], y1[:], hgn[:, h:h + 1], ghs[k2][h][0][:], ALU.mult, ALU.mult,
                      [b_y1, b_hgn, ghs[k2][h][1]], [b_y])
                stores.append(E.dma(Yd[h, :, tok], y_[:], [b_y], [bY[h][blk]], store=True))
                yield

        def step(g):
            try:
                return next(g), True
            except StopIteration:
                return None, False

        NIT = NB1 + 2 if MODE != "setup" else 0
        for t in range(NIT):
            g1 = genX1(t) if t < NB1 else None
            g2 = genX2(t - 1) if 0 <= t - 1 < NB1 else None
            g3 = genY(t - 2) if 0 <= t - 2 < NB1 else None
            x2_el = g2 is None
            y_hold = False
            rnd = 0
            while g1 is not None or g2 is not None or g3 is not None:
                rnd += 1
                if g3 is not None and not (y_hold and not x2_el):
                    tag, alive = step(g3)
                    if not alive:
                        g3 = None
                    elif tag == "EPI":
                        y_hold = True
                for _ in range(2 if (rnd > 4 or g1 is None) else 0):
                    if g2 is not None:
                        tag, alive = step(g2)
                        if not alive:
                            g2 = None
                            x2_el = True
                        elif tag == "EL":
                            x2_el = True
                if g1 is not None:
                    tag, alive = step(g1)
                    if not alive:
                        g1 = None
        P.build(nc, st, "a")
        n1 = len(P.ops)

    if MODE in ("setup", "pass1"):
        nc._mk_stats = (n1, 0)
        return nc
    if MODE != "nobar":
        nc.all_engine_barrier()

    with ExitStack() as st:
        P = Prog()
        E = Em(P)

        def sb(name, shape, dt):
            return st.enter_context(nc.sbuf_tensor("b_" + name, shape, dt)), Buf(name)

        def ps(name, shape, dt):
            return st.enter_context(nc.psum_tensor("b_" + name, shape, dt)), Buf(name, True)

        KT = [sb("KT%d" % h, [96, S], BF16) for h in range(8)]
        Vst, _ = sb("Vst", [128, NT, 640], BF16)
        b_Vst = [Buf() for _ in range(NT)]
        woutb, b_woutb = sb("woutb", [128, 8, 1024], BF16)
        fng, b_fng = sb("fng", [128, 1024], F32)
        tri, b_tri = sb("tri", [128, 128], BF16)
        identb, b_identb = sb("identb", [128, 128], BF16)
        onesf, b_onesf = sb("onesf", [128, 128], BF16)
        stg = Rot([sb("stg%d" % i, [128, 512], F32) for i in range(2)])
        QT = [sb("QT%d" % h, [96, 512], BF16) for h in range(8)]
        gm_t = [sb("gm%d" % k, [128, 4, 512], BF16) for k in range(2)]
        yh_t = [sb("yh%d" % k, [128, 4, 512], BF16) for k in range(2)]
        gm = [[(gm_t[k][0][:, j, :], gm_t[k][1]) for j in range(4)] for k in range(2)]
        yh = [[(yh_t[k][0][:, j, :], yh_t[k][1]) for j in range(4)] for k in range(2)]
        ymla = [[sb("ymla%d_%d" % (k, j), [128, 512], BF16) for j in range(4)] for k in range(2)]
        PT = Rot([sb("PT%d" % i, [128, 512], BF16) for i in range(3)])
        rr = Rot([sb("rr%d" % i, [128, 512], F32) for i in range(2)])
        bc = Rot([sb("bc%d" % i, [128, 512], F32) for i in range(2)])
        yt = Rot([sb("yt%d" % i, [128, 512], F32) for i in range(2)])
        xr = Rot([sb("xr%d" % i, [128, D], F32) for i in range(3)])
        z = Rot([sb("z%d" % i, [128, D], F32) for i in range(2)])
        res = Rot([sb("res%d" % i, [128, D], F32) for i in range(2)])
        ssq = Rot([sb("ssq%d" % i, [128, 1], F32) for i in range(2)])
        lnv = Rot([sb("lnv%d" % i, [128, 1], F32) for i in range(2)])
        rstd = Rot([sb("rstd%d" % i, [128, 1], F32) for i in range(2)])
        pS = Rot([ps("pS%d" % i, [128, 512], F32) for i in range(3)])
        pO = Rot([ps("pO%d" % i, [128, 512], F32) for i in range(3)])
        pOut = Rot([ps("pOut%d" % i, [128, 512], F32) for i in range(2)])

        E.dma(tri[:], tri_d, [], [b_tri])
        E.dma(identb[:], ident_d, [], [b_identb])
        E.dma(fng[:], fng_d.partition_broadcast(128), [], [b_fng])
        E.memset("pool", onesf[:], 1.0, [b_onesf])

        def load_q(qb, h):
            E.dma(QT[h][0][:], Qd[h, :, qb * 512:(qb + 1) * 512], [], [QT[h][1]])

        b_KT = [[Buf("KT%d_%d" % (h, q)) for q in range(NB2)] for h in range(8)]

        def load_kv(qb):
            cs = slice(qb * 512, (qb + 1) * 512)
            for h in range(8):
                E.dma(KT[h][0][:, cs], Kd[h, :, cs], [b_KT[h][qb - 1]] if qb > 0 else [], [b_KT[h][qb]], key=("kt", h))
                if h == 0:
                    g0 = 4 * qb
                    E.dma(Vst[:, g0:g0 + 4, :], Vd[g0:g0 + 4].rearrange("t p f -> p t f"),
                          [b_Vst[g0 - 1]] if qb > 0 else [], [b_Vst[t] for t in range(g0, g0 + 4)], key="vst")

        load_q(0, 0)
        load_kv(0)
        for h in range(1, 8):
            load_q(0, h)
        outs = []
        items = [(qb, h, kb) for qb in range(NB2) for h in range(8) for kb in range(4 * qb + 4)]
        NI = len(items)
        LOOK = 2
        FDLY = int(os.environ.get("MK_FDLY", "14"))
        OPB = int(os.environ.get("MK_OPB", "10"))
        OPS = int(os.environ.get("MK_OPS", "6"))
        OPL = int(os.environ.get("MK_OPL", "8"))
        OPF = int(os.environ.get("MK_OPF", "4"))
        CAP = int(os.environ.get("MK_CAP", "3"))
        DIV = False
        st_info = {}
        cur_po = {}
        pend = []
        seqc = [0]
        fin_left = {qb: 8 for qb in range(NB2)}
        b_Rd = [Buf("Rd%d" % i) for i in range(4)]

        def defer(due, fn, g=None):
            pend.append((due, seqc[0], fn, g))
            seqc[0] += 1

        def load_gates(qb):
            qs = slice(qb * 512, (qb + 1) * 512)
            E.dma(gm_t[qb % 2][0][:], Gd[:, :, qs].rearrange("j p t -> p j t"), [], [gm_t[qb % 2][1]])
            E.dma(yh_t[qb % 2][0][:], Yd[:, :, qs].rearrange("j p t -> p j t"), [], [yh_t[qb % 2][1]])

        def emit_st(i):
            qb, h, kb = items[i]
            if h == 0 and kb == 0:
                load_gates(qb)
                if qb + 1 < NB2:
                    load_kv(qb + 1)
            jd = kb - 4 * qb
            qoff = 128 * jd if jd > 0 else 0
            ncol = 512 - qoff
            ps_, b_ps = pS.next()
            E.mm(ps_[:, 0:ncol], KT[h][0][:, kb * 128:(kb + 1) * 128], QT[h][0][:, qoff:512], True, jd < 0,
                 [b_KT[h][kb // 4], QT[h][1]], [b_ps])
            if jd >= 0:
                E.mm(ps_[:, 0:128], identb[:], tri[:], False, True, [b_identb, b_tri], [b_ps])
            st_info[i] = (ps_, b_ps, qoff, ncol, jd)
            if kb == 4 * qb + 3 and qb + 1 < NB2:
                load_q(qb + 1, h)

        def out_proj_chunks(qb, i_now):
            base = i_now + OPB
            for i in range(4):
                gt = qb * 4 + i
                ts_ = slice(i * 128, (i + 1) * 128)
                hold = {}

                def c_load(gt=gt, hold=hold):
                    hold["x"] = xr.next()
                    hold["z"] = z.next()
                    E.dma(hold["x"][0][:], x_d[gt * 128:(gt + 1) * 128, :], [], [hold["x"][1]])
                defer(base + OPS * i - OPL, c_load)
                for n in range(2):
                    def c_mm(n=n, ts_=ts_, hold=hold, qb=qb):
                        ns = slice(n * 512, (n + 1) * 512)
                        pp, b_pp = pOut.next()
                        for ch in range(8):
                            src = ymla[qb % 2][ch] if ch < 4 else yh[qb % 2][ch - 4]
                            E.mm(pp[:, :], src[0][:, ts_], woutb[:, ch, ns], ch == 0, ch == 7, [src[1], b_woutb], [b_pp])
                        E.tt("dve", hold["z"][0][:, ns], pp[:, :], hold["x"][0][:, ns], ALU.add, [b_pp, hold["x"][1]],
                             [hold["z"][1]])
                    defer(base + OPS * i + (OPS // 2) * n, c_mm)

                def c_fin(gt=gt, hold=hold):
                    z_, b_z = hold["z"]
                    r_, b_r = res.next()
                    sq1, b_sq1 = ssq.next()
                    E.memset("pool", sq1[:], 0.0, [b_sq1])
                    E.act(r_[:], z_[:], AF.Square, [b_z], [b_r, b_sq1], accum=sq1[:, 0:1])
                    l1, b_l1 = lnv.next()
                    E.act(l1[:], sq1[:], AF.Ln, [b_sq1], [b_l1], bias=EPS, scale=1.0 / D)
                    r1, b_r1 = rstd.next()
                    E.act(r1[:], l1[:], AF.Exp, [b_l1], [b_r1], scale=-0.5)
                    E.stt("dve", r_[:], z_[:], r1[:, 0:1], fng[:], ALU.mult, ALU.mult, [b_z, b_r1, b_fng], [b_r])
                    outs.append(E.dma(out_d[gt * 128:(gt + 1) * 128, :], r_[:], [b_r], [Buf()], store=True))
                defer(base + OPS * i + (OPS // 2) + OPF, c_fin)

        def emit_exp_pv(i):
            qb, h, kb = items[i]
            nkb = 4 * qb + 4
            pair, odd = h // 2, h % 2
            voff = 32 if odd else 0
            ps_, b_ps, qoff, ncol, jd = st_info.pop(i)
            if kb == 0:
                g = qb * 8 + h
                for p in [p for p in pend if p[3] is not None and p[3] <= g - 3]:
                    pend.remove(p)
                    p[2]()
                cur_po[(qb, h)] = pO.next()
            po, b_po = cur_po[(qb, h)]
            pt, b_pt = PT.next()
            E.act(pt[:, 0:ncol], ps_[:, 0:ncol], AF.Exp, [b_ps], [b_pt])
            E.mm(po[:, qoff:512], Vst[:, kb, pair * 160 + voff:pair * 160 + voff + 128], pt[:, 0:ncol],
                 kb == 0, kb == nkb - 1, [b_Vst[kb], b_pt], [b_po])
            if kb != nkb - 1:
                return
            row = 32 if odd else 64
            rows = slice(64, 128) if odd else slice(0, 64)
            r_, b_r = rr.next()
            if DIV:
                E.cp("dve", r_[row:row + 1, :], po[row:row + 1, :], [b_po], [b_r])
            else:
                E.recip(r_[row:row + 1, :], po[row:row + 1, :], [b_po], [b_r])

            slot = (qb * 8 + h) % 4

            def fin1(row=row, r_=r_, b_r=b_r, slot=slot):
                E.dma(Rd[slot:slot + 1, :], r_[row:row + 1, :], [b_r], [b_Rd[slot]], store=True)
            defer(i + 7, fin1, qb * 8 + h)

            def fin2(qb=qb, h=h, pair=pair, row=row, rows=rows, po=po, b_po=b_po, slot=slot):
                bc_, b_bc = bc.next()
                E.dma(bc_[rows, :], Rd[slot:slot + 1, :].partition_broadcast(64), [b_Rd[slot]], [b_bc])
                y_, b_y = yt.next()
                if DIV:
                    E.cp("dve", y_[rows, :], po[rows, :], [b_po], [b_y])
                    E.tt("pool", y_[rows, :], y_[rows, :], bc_[rows, :], ALU.divide, [b_y, b_bc], [b_y])
                else:
                    E.tt("dve", y_[rows, :], po[rows, :], bc_[rows, :], ALU.mult, [b_po, b_bc], [b_y])
                E.tt("pool", ymla[qb % 2][pair][0][rows, :], y_[rows, :], gm[qb % 2][pair][0][rows, :], ALU.mult,
                     [b_y, gm[qb % 2][pair][1]], [ymla[qb % 2][pair][1]])
                fin_left[qb] -= 1
                if fin_left[qb] == 0:
                    out_proj_chunks(qb, cur_iter[0])
            defer(i + FDLY, fin2, qb * 8 + h)

        def wout_chunk(c, n):
            def f():
                sg_, b_sg = stg.next()
                E.dma(sg_[:], wout_d[:, c, n * 512:(n + 1) * 512], [], [b_sg])
                E.cp("dve", woutb[:, c, n * 512:(n + 1) * 512], sg_[:], [b_sg], [b_woutb])
            return f
        for c in range(8):
            for n in range(2):
                defer(3 + c * 2 + n, wout_chunk(c, n))

        cur_iter = [0]
        for it in range(NI + LOOK):
            cur_iter[0] = it
            if it < NI:
                emit_st(it)
            if it >= LOOK:
                emit_exp_pv(it - LOOK)
            ready = sorted([p for p in pend if p[0] <= it], key=lambda p: (p[0], p[1]))
            for p in ready[:CAP]:
                pend.remove(p)
                p[2]()
        cur_iter[0] = NI + LOOK + 10 ** 6
        while pend:
            p = sorted(pend, key=lambda p: (p[0], p[1]))[0]
            pend.remove(p)
            p[2]()
        P.build(nc, st, "b")
        n2 = len(P.ops)
    nc._mk_stats = (n1, n2)
    return nc


def _consts():
    s = np.arange(128)[:, None]
    t = np.arange(128)[None, :]
    same = (s // 64) == (t // 64)
    L = (same & (s <= t)).astype(np.float32)
    ref = (t // 64) * 64 + 31
    Lr = (same & (s <= ref)).astype(np.float32)
    lm = L - Lr
    us = (same & (s > t)).astype(np.float32)
    lcols = np.zeros((128, 4), np.float32)
    lcols[0:32, 0] = 1
    lcols[0:64, 1] = 1
    lcols[64:96, 2] = 1
    lcols[64:128, 3] = 1
    lrep = np.tile(L, (1, 4)).astype(np.float32)
    ident = np.eye(128, dtype=np.float32).astype(ml_dtypes.bfloat16)
    tri = np.where(t >= s, 0.0, -30000.0).astype(np.float32).astype(ml_dtypes.bfloat16)
    inv = (10000.0 ** (-np.arange(16, dtype=np.float32) / 16.0)).astype(np.float32)
    invf = np.tile(inv / (2 * np.pi), 8).reshape(128, 1).astype(np.float32)
    bf = ml_dtypes.bfloat16
    return dict(c_ident=ident, c_tri=tri, c_lm=lm.astype(bf), c_lcols=lcols.astype(bf), c_us=us.astype(bf), c_lrep=lrep, c_invf=invf)


def _layout_weights(ln_g, w_in, q_a_norm_g, w_q_b, kv_a_norm_g, w_kv_b, hg_lower_bounds, hg_norm_g, w_out, final_norm_g):
    f = lambda a: np.ascontiguousarray(a, dtype=np.float32)
    d = {}
    d["w_in_l"] = f(w_in[0].reshape(8, 128, DIN).transpose(1, 0, 2))
    d["ln_g_l"] = f(ln_g[0].reshape(8, 128).T)
    d["w_q_l"] = f(w_q_b[0].reshape(2, 128, 768).transpose(1, 0, 2))
    d["q_g_l"] = f(q_a_norm_g[0].reshape(2, 128).T)
    d["w_kv_l"] = f(w_kv_b[0])
    d["kv_g_l"] = f(kv_a_norm_g[0].reshape(128, 1))
    d["w_out_l"] = f(w_out[0].reshape(8, 128, 1024).transpose(1, 0, 2))
    d["hgn_l"] = f(hg_norm_g[0].reshape(4, 128).T)
    d["hlb"] = f(hg_lower_bounds)
    d["hlb_fm"] = f(hg_lower_bounds.reshape(2, 4, 128).transpose(2, 0, 1))
    d["fng"] = f(final_norm_g.reshape(1, 1024))
    d.update(_consts())
    return d


_NC_CACHE = {}


def run(x, positions, weights, S, BT=256):
    B = x.shape[0]
    key = (S, BT)
    if key not in _NC_CACHE:
        _NC_CACHE[key] = build(S, BT)
    nc = _NC_CACHE[key]
    shared = _layout_weights(**weights)
    in_maps = []
    for b in range(B):
        m = dict(shared)
        m["x"] = np.ascontiguousarray(x[b], dtype=np.float32)
        m["pos"] = np.ascontiguousarray(positions[b].reshape(1, S), dtype=np.int32)
        in_maps.append(m)
    r = run_bass_kernel_spmd(nc, in_maps, core_ids=list(range(B)))
    return np.stack([np.asarray(r.results[b]["out"]) for b in range(B)], axis=0).astype(np.float32)


def kernel(x, positions, ln_g, w_in, q_a_norm_g, w_q_b, kv_a_norm_g, w_kv_b, hg_lower_bounds, hg_norm_g, w_out,
           final_norm_g):
    x = np.asarray(x)
    weights = dict(ln_g=np.asarray(ln_g), w_in=np.asarray(w_in), q_a_norm_g=np.asarray(q_a_norm_g),
                   w_q_b=np.asarray(w_q_b), kv_a_norm_g=np.asarray(kv_a_norm_g), w_kv_b=np.asarray(w_kv_b),
                   hg_lower_bounds=np.asarray(hg_lower_bounds), hg_norm_g=np.asarray(hg_norm_g),
                   w_out=np.asarray(w_out), final_norm_g=np.asarray(final_norm_g))
    return run(x, np.asarray(positions), weights, x.shape[1])
```
